# Optimizing a Trainium2 kernel written in Bass

```python
import jax, jax.numpy as jnp
from jax import lax
import numpy as np

D_MODEL = 1024
BATCH = 4
SEQ = 4096
DEPTH = 1
DEC_BATCH = 128
DEC_SEQ = 8
PAST_LEN = 16384
PAGE_SIZE = 128

LRU_WIDTH = D_MODEL
LRU_BLOCKS = 16
LRU_BLOCK = LRU_WIDTH // LRU_BLOCKS
CONV_WIDTH = 4
LRU_C = 8.0
N_HEADS = 16
N_KV_HEADS = 4
HEAD_DIM = 64
GROUP = N_HEADS // N_KV_HEADS
Q_WIDTH = N_HEADS * HEAD_DIM
KV_WIDTH = N_KV_HEADS * HEAD_DIM
WINDOW = 128
ATTN_BLOCK = 128
PEER_HEADS = 8
N_KEYS = 128
N_EXPERTS = N_KEYS * N_KEYS
KEY_DIM = 128
PEER_TOPK = 16
N_ACTIVE = PEER_HEADS * PEER_TOPK
PEER_CHUNK = 128
IN_SPLITS = (LRU_WIDTH, LRU_WIDTH, Q_WIDTH, KV_WIDTH, KV_WIDTH, D_MODEL, D_MODEL)
IN_COLS = LRU_WIDTH * 2 + Q_WIDTH + KV_WIDTH * 2 + D_MODEL * 2
EPS = 1e-6
NEG_INF = -1e30

kernel_name = 'griffin_swa_sink_peer_hybrid_step'


def _rmsnorm(x, g):
    xf = x.astype(jnp.float32)
    y = xf * lax.rsqrt(jnp.mean(xf * xf, axis=-1, keepdims=True) + EPS)
    return (y * g.astype(jnp.float32)).astype(x.dtype)


def _split_in(z):
    parts, start = [], 0
    for w in IN_SPLITS:
        parts.append(z[..., start:start + w])
        start += w
    return parts


def _alibi_slopes():
    return 2.0 ** (-8.0 * jnp.arange(1, N_HEADS + 1, dtype=jnp.float32) / N_HEADS)


def _causal_conv(xr, buf, w, b):
    T = xr.shape[1]
    full = jnp.concatenate([buf.astype(xr.dtype), xr], axis=1)
    y = sum(full[:, j:j + T] * w[j] for j in range(CONV_WIDTH)) + b
    return y, full[:, -(CONV_WIDTH - 1):]


def _lin_combine(left, right):
    a1, b1 = left
    a2, b2 = right
    return a1 * a2, a2 * b1 + b2


def _rglru(xc, h0, w_a, b_a, w_x, b_x, lam):
    B, T, C = xc.shape
    xb = xc.reshape(B, T, LRU_BLOCKS, LRU_BLOCK)
    r = jax.nn.sigmoid((jnp.einsum('btnj,njk->btnk', xb, w_a).reshape(B, T, C) + b_a).astype(jnp.float32))
    i = jax.nn.sigmoid((jnp.einsum('btnj,njk->btnk', xb, w_x).reshape(B, T, C) + b_x).astype(jnp.float32))
    log_a = LRU_C * r * jax.nn.log_sigmoid(lam.astype(jnp.float32))
    a = jnp.exp(log_a)
    bterm = jnp.sqrt(-jnp.expm1(2.0 * log_a)) * (i * xc.astype(jnp.float32))
    bterm = bterm.at[:, 0].add(a[:, 0] * h0.astype(jnp.float32))
    _, h = lax.associative_scan(_lin_combine, (a, bterm), axis=1)
    return h, h[:, -1]


def _window_attend(q, k, v, qpos, kpos, sinks):
    B, L, Tq = q.shape[:3]
    s = jnp.einsum('blqkgd,blskd->blkgqs', q, k).astype(jnp.float32) * (HEAD_DIM ** -0.5)
    dist = qpos[:, :, None] - kpos[:, None, :]
    valid = (dist >= 0) & (dist <= WINDOW) & (kpos[:, None, :] >= 0)
    slopes = _alibi_slopes().reshape(N_KV_HEADS, GROUP)
    s = s - slopes[None, None, :, :, None, None] * dist.astype(jnp.float32)[None, :, None, None]
    s = jnp.where(valid[None, :, None, None], s, NEG_INF)
    sink = jnp.broadcast_to(sinks.astype(jnp.float32).reshape(1, 1, N_KV_HEADS, GROUP, 1, 1), s.shape[:-1] + (1,))
    p = jax.nn.softmax(jnp.concatenate([s, sink], axis=-1), axis=-1)[..., :-1]
    o = jnp.einsum('blkgqs,blskd->blqkgd', p.astype(v.dtype), v)
    return o.reshape(B, L * Tq, Q_WIDTH)


def _band(t):
    B, S = t.shape[:2]
    tb = t.reshape(B, S // ATTN_BLOCK, ATTN_BLOCK, N_KV_HEADS, HEAD_DIM)
    prev = jnp.concatenate([jnp.zeros_like(tb[:, :1]), tb[:, :-1]], axis=1)
    return jnp.concatenate([prev, tb], axis=2)


def _peer(xn, w_query, sub_keys, expert_u, expert_v):
    B, T, D = xn.shape
    n = B * T
    xt = xn.reshape(n, D)
    qr = (xt @ w_query).reshape(n, PEER_HEADS, 2, KEY_DIM)
    s = jnp.einsum('nhpd,hpkd->nhpk', qr, sub_keys).astype(jnp.float32)
    s_top, i_top = lax.top_k(s, PEER_TOPK)
    cand = (s_top[:, :, 0, :, None] + s_top[:, :, 1, None, :]).reshape(n, PEER_HEADS, PEER_TOPK * PEER_TOPK)
    cand_idx = (i_top[:, :, 0, :, None] * N_KEYS + i_top[:, :, 1, None, :]).reshape(n, PEER_HEADS, PEER_TOPK * PEER_TOPK)
    best, pos = lax.top_k(cand, PEER_TOPK)
    idx = jnp.take_along_axis(cand_idx, pos, axis=-1).reshape(n, N_ACTIVE)
    g = jax.nn.softmax(best, axis=-1).reshape(n, N_ACTIVE).astype(xt.dtype)
    pad = (-n) % PEER_CHUNK
    xt_p = jnp.pad(xt, ((0, pad), (0, 0))).reshape(-1, PEER_CHUNK, D)
    idx_p = jnp.pad(idx, ((0, pad), (0, 0))).reshape(-1, PEER_CHUNK, N_ACTIVE)
    g_p = jnp.pad(g, ((0, pad), (0, 0))).reshape(-1, PEER_CHUNK, N_ACTIVE)

    def expert_block(args):
        xc, ic, gc = args
        act = jax.nn.gelu(jnp.einsum('cd,ced->ce', xc, expert_u[ic])) * gc
        return jnp.einsum('ce,ced->cd', act, expert_v[ic])

    out = lax.map(expert_block, (xt_p, idx_p, g_p)).reshape(-1, D)[:n]
    return out.reshape(B, T, D)


def _layer(x, conv_buf, h0, k_past, v_past, norm1_g, w_in, conv_w, conv_b, rg_w_a, rg_b_a, rg_w_x, rg_b_x,
           rg_lambda, q_norm_g, k_norm_g, attn_sinks, w_branch_lru, w_branch_attn, w_out, norm2_g,
           peer_w_query, peer_sub_keys, expert_u, expert_v):
    B, T, _ = x.shape
    xn = _rmsnorm(x, norm1_g)
    xr, gr, q, k, v, ga, gb = _split_in(xn @ w_in)
    xc, new_conv = _causal_conv(xr, conv_buf, conv_w, conv_b)
    h, h_last = _rglru(xc, h0, rg_w_a, rg_b_a, rg_w_x, rg_b_x, rg_lambda)
    lru_out = h.astype(x.dtype) * jax.nn.gelu(gr)
    q = _rmsnorm(q.reshape(B, T, N_KV_HEADS, GROUP, HEAD_DIM), q_norm_g)
    k = _rmsnorm(k.reshape(B, T, N_KV_HEADS, HEAD_DIM), k_norm_g)
    v = v.reshape(B, T, N_KV_HEADS, HEAD_DIM)
    if k_past is None:
        nb = T // ATTN_BLOCK
        start = jnp.arange(nb)[:, None] * ATTN_BLOCK
        qpos = start + jnp.arange(ATTN_BLOCK)[None]
        kpos = start - ATTN_BLOCK + jnp.arange(2 * ATTN_BLOCK)[None]
        qb = q.reshape(B, nb, ATTN_BLOCK, N_KV_HEADS, GROUP, HEAD_DIM)
        attn = _window_attend(qb, _band(k), _band(v), qpos, kpos, attn_sinks)
        new_k, new_v = k[:, -WINDOW:], v[:, -WINDOW:]
    else:
        kk = jnp.concatenate([k_past.astype(k.dtype), k], axis=1)
        vv = jnp.concatenate([v_past.astype(v.dtype), v], axis=1)
        qpos = (PAST_LEN + jnp.arange(T))[None]
        kpos = (PAST_LEN - WINDOW + jnp.arange(WINDOW + T))[None]
        attn = _window_attend(q[:, None], kk[:, None], vv[:, None], qpos, kpos, attn_sinks)
        new_k, new_v = kk[:, -WINDOW:], vv[:, -WINDOW:]
    merged = jax.nn.sigmoid(ga) * (lru_out @ w_branch_lru) + jax.nn.sigmoid(gb) * (attn @ w_branch_attn)
    hres = x + merged @ w_out
    y = hres + _peer(_rmsnorm(hres, norm2_g), peer_w_query, peer_sub_keys, expert_u, expert_v)
    return y, new_conv, h_last, new_k, new_v


def setup_inputs(seed: int = 0) -> dict:
    key = jax.random.key(seed)
    ks = jax.random.split(key, 32)
    f32 = jnp.float32
    nrm = lambda k, shape, scale: jax.random.normal(k, shape, f32) * scale
    u = jax.random.uniform(ks[14], (DEPTH, LRU_WIDTH), f32, minval=0.9, maxval=0.999)
    sg = u ** (1.0 / LRU_C)
    lam = jnp.log(sg) - jnp.log1p(-sg)
    return {
        'x_prompt': nrm(ks[0], (BATCH, SEQ, D_MODEL), 1.0),
        'x_sample': nrm(ks[1], (DEC_BATCH, DEC_SEQ, D_MODEL), 1.0),
        'cache_conv': nrm(ks[2], (DEPTH, DEC_BATCH, CONV_WIDTH - 1, LRU_WIDTH), 1.0),
        'state_lru': nrm(ks[3], (DEPTH, DEC_BATCH, LRU_WIDTH), 0.5),
        'cache_k': nrm(ks[4], (DEPTH, DEC_BATCH, WINDOW, N_KV_HEADS, HEAD_DIM), 1.0),
        'cache_v': nrm(ks[5], (DEPTH, DEC_BATCH, WINDOW, N_KV_HEADS, HEAD_DIM), 1.0),
        'norm1_g': 1.0 + nrm(ks[6], (DEPTH, D_MODEL), 0.02),
        'w_in': nrm(ks[7], (DEPTH, D_MODEL, IN_COLS), D_MODEL ** -0.5),
        'conv_w': nrm(ks[8], (DEPTH, CONV_WIDTH, LRU_WIDTH), CONV_WIDTH ** -0.5),
        'conv_b': nrm(ks[9], (DEPTH, LRU_WIDTH), 0.01),
        'rg_w_a': nrm(ks[10], (DEPTH, LRU_BLOCKS, LRU_BLOCK, LRU_BLOCK), LRU_BLOCK ** -0.5),
        'rg_b_a': nrm(ks[11], (DEPTH, LRU_WIDTH), 0.01),
        'rg_w_x': nrm(ks[12], (DEPTH, LRU_BLOCKS, LRU_BLOCK, LRU_BLOCK), LRU_BLOCK ** -0.5),
        'rg_b_x': nrm(ks[13], (DEPTH, LRU_WIDTH), 0.01),
        'rg_lambda': lam,
        'q_norm_g': 1.0 + nrm(ks[15], (DEPTH, HEAD_DIM), 0.02),
        'k_norm_g': 1.0 + nrm(ks[16], (DEPTH, HEAD_DIM), 0.02),
        'attn_sinks': nrm(ks[17], (DEPTH, N_HEADS), 0.5),
        'w_branch_lru': nrm(ks[18], (DEPTH, LRU_WIDTH, D_MODEL), LRU_WIDTH ** -0.5),
        'w_branch_attn': nrm(ks[19], (DEPTH, Q_WIDTH, D_MODEL), Q_WIDTH ** -0.5),
        'w_out': nrm(ks[20], (DEPTH, D_MODEL, D_MODEL), D_MODEL ** -0.5),
        'norm2_g': 1.0 + nrm(ks[21], (DEPTH, D_MODEL), 0.02),
        'peer_w_query': nrm(ks[22], (DEPTH, D_MODEL, PEER_HEADS * 2 * KEY_DIM), D_MODEL ** -0.5),
        'peer_sub_keys': nrm(ks[23], (DEPTH, PEER_HEADS, 2, N_KEYS, KEY_DIM), KEY_DIM ** -0.5),
        'expert_u': nrm(ks[24], (DEPTH, N_EXPERTS, D_MODEL), D_MODEL ** -0.5),
        'expert_v': nrm(ks[25], (DEPTH, N_EXPERTS, D_MODEL), 0.1),
    }


def reference(x_prompt, x_sample, cache_conv, state_lru, cache_k, cache_v, norm1_g, w_in, conv_w, conv_b,
              rg_w_a, rg_b_a, rg_w_x, rg_b_x, rg_lambda, q_norm_g, k_norm_g, attn_sinks, w_branch_lru,
              w_branch_attn, w_out, norm2_g, peer_w_query, peer_sub_keys, expert_u, expert_v):
    yp, ys = x_prompt, x_sample
    conv_p, lru_p, k_p, v_p = [], [], [], []
    conv_s, lru_s, k_s, v_s = [], [], [], []
    for l in range(DEPTH):
        lp = (norm1_g[l], w_in[l], conv_w[l], conv_b[l], rg_w_a[l], rg_b_a[l], rg_w_x[l], rg_b_x[l],
              rg_lambda[l], q_norm_g[l], k_norm_g[l], attn_sinks[l], w_branch_lru[l], w_branch_attn[l],
              w_out[l], norm2_g[l], peer_w_query[l], peer_sub_keys[l], expert_u[l], expert_v[l])
        zero_buf = jnp.zeros((yp.shape[0], CONV_WIDTH - 1, LRU_WIDTH), yp.dtype)
        zero_h = jnp.zeros((yp.shape[0], LRU_WIDTH), jnp.float32)
        yp, c, h, kw, vw = _layer(yp, zero_buf, zero_h, None, None, *lp)
        conv_p.append(c); lru_p.append(h); k_p.append(kw); v_p.append(vw)
        ys, c, h, kw, vw = _layer(ys, cache_conv[l], state_lru[l], cache_k[l], cache_v[l], *lp)
        conv_s.append(c); lru_s.append(h); k_s.append(kw); v_s.append(vw)
    new_conv_p, new_lru_p = jnp.stack(conv_p), jnp.stack(lru_p)
    new_k_p, new_v_p = jnp.stack(k_p), jnp.stack(v_p)
    new_conv_s, new_lru_s = jnp.stack(conv_s), jnp.stack(lru_s)
    new_k_s, new_v_s = jnp.stack(k_s), jnp.stack(v_s)
    return (yp, ys, new_conv_p, new_lru_p, new_k_p, new_v_p, new_conv_s, new_lru_s, new_k_s, new_v_s)
```

```python
import numpy as np
from contextlib import ExitStack
import concourse.bass as bass
import concourse.mybir as mybir
from concourse.bass_utils import run_bass_kernel_spmd

F32 = mybir.dt.float32
BF16 = mybir.dt.bfloat16
I32 = mybir.dt.int32
U32 = mybir.dt.uint32
AF = mybir.ActivationFunctionType
ALU = mybir.AluOpType
AX = mybir.AxisListType

NDS = 12
NCORE = 8
D = 1024
T = 128
NCH = 8
NPRE = 16
NMAIN = 16
NSEQ = 16
DEC = 8
NH = 16
NKV = 4
HD = 64
EPS = 1e-6
PAST = 16384
NEXP = 16384
STAGE = 3
DEBUG_C = False
PE_STRIDED = True
DEBUG_LVL = 9


class Dep:
    __slots__ = ("name", "w", "r")

    def __init__(self, name=""):
        self.name = name
        self.w = None
        self.r = {}


class Sched:
    def __init__(self, nc, stack):
        self.nc = nc
        self.engs = {"pe": nc.tensor, "dve": nc.vector, "act": nc.scalar,
                     "pool": nc.gpsimd, "sp": nc.sync}
        self.sem = {k: stack.enter_context(nc.semaphore("s_" + k)) for k in self.engs}
        self.cnt = {k: 0 for k in self.engs}
        self.waited = {k: {} for k in self.engs}
        self.dsem = {k: [stack.enter_context(nc.semaphore("d_%s%d" % (k, i))) for i in range(NDS)]
                     for k in ("sp", "act", "pool")}
        self.dcnt = {k: 0 for k in self.dsem}
        self.dlast = {}
        self.out_toks = []
        self.ninst = 0

    def _wait(self, e, tok):
        sem, val, key = tok
        if self.waited[e].get(key, 0) >= val:
            return
        self.engs[e].wait_ge(sem, val)
        self.waited[e][key] = val

    def _deps(self, e, reads, writes):
        toks = []
        for d in reads:
            if d.w is not None:
                toks.append(d.w)
        for d in writes:
            if d.w is not None and d.w[2] != e:
                toks.append(d.w)
            toks.extend(t for t in d.r.values() if t[2] != e)
        for t in toks:
            if e == "pe" and t[2] == "pe":
                continue
            if t[2] == e and e in ("dve", "act") and self.cnt[e] - t[1] >= 3:
                continue
            self._wait(e, t)

    def _commit(self, tok, reads, writes):
        for d in reads:
            d.r[tok[2]] = tok
        for d in writes:
            d.w = tok
            d.r = {}

    def op(self, e, fn, reads=(), writes=()):
        self._deps(e, reads, writes)
        inst = fn(self.engs[e])
        self.cnt[e] += 1
        inst.then_inc(self.sem[e], 1)
        self.ninst += 1
        self._commit((self.sem[e], self.cnt[e], e), reads, writes)
        return inst

    def dma(self, e, fn, reads=(), writes=(), is_out=False):
        i = self.dcnt[e]
        self.dcnt[e] += 1
        sem = self.dsem[e][i % NDS]
        key = (e, i % NDS)
        val = 16 * (i // NDS + 1)
        if val > 16:
            self._wait(e, (sem, val - 16, key))
        self._deps(e, reads, writes)
        inst = fn(self.engs[e])
        inst.then_inc(sem, 16)
        self.ninst += 1
        tok = (sem, val, key)
        self.dlast[key] = tok
        self._commit(tok, reads, writes)
        if is_out:
            self.out_toks.append(tok)
        return inst

    def barrier(self):
        toks = [(self.sem[k], self.cnt[k], k) for k in self.engs if self.cnt[k] > 0]
        toks += list(self.dlast.values())
        for e in self.engs:
            for t in toks:
                if t[2] == e:
                    continue
                self._wait(e, t)

    def finish(self):
        for t in self.out_toks:
            self._wait("sp", t)
        for k in self.engs:
            if k != "sp" and self.cnt[k] > 0:
                self._wait("sp", (self.sem[k], self.cnt[k], k))
        for t in self.dlast.values():
            self._wait("sp", t)


def build_program():
    nc = bass.Bass("TRN2", target_bir_lowering=False)

    def din(name, shape, dt=F32):
        return nc.dram_tensor(name, list(shape), dt, kind="ExternalInput").ap()

    def dout(name, shape, dt=F32):
        return nc.dram_tensor(name, list(shape), dt, kind="ExternalOutput").ap()

    xp = din("xp", [(NPRE + NMAIN) * T, D])
    xs = din("xs", [T, D])
    cconv = din("cconv", [NSEQ * 3, D])
    slru = din("slru", [NSEQ, D])
    ck = din("ck", [NSEQ, 128, 256])
    cv = din("cv", [NSEQ, 128, 256])
    flag_d = din("flag", [128, 1])
    identf_d = din("identf", [128, 128])
    E_pr_d = din("E_pr", [128, NH * 2 * 128])
    E_so_d = din("E_so", [128, NH * 128])
    E_sc_d = din("E_sc", [128, NH * DEC])
    prow_d = din("prow", [16, D])
    w_in_d = din("w_in", [D, 5632])
    rgw_d = din("rgw", [2, 16, 64, 64])
    qkg_d = din("qkg", [2, 64])
    sinks_d = din("sinks", [1, NH])
    w_bl_d = din("w_bl", [D, D])
    w_ba_d = din("w_ba", [D, D])
    w_o_d = din("w_o", [D, D])
    w_q_d = din("w_q", [D, 2048])
    subk_d = din("subk", [16, 128, 128])
    if STAGE >= 3 and not DEBUG_C:
        eu_d = din("eu", [NEXP, D])
        ev_d = din("ev", [NEXP, D])

    y_p = dout("y_p", [NMAIN * T, D])
    y_s = dout("y_s", [T, D])
    conv_p = dout("conv_p", [3, D])
    lru_p = dout("lru_p", [1, D])
    k_p = dout("k_p", [128, 256])
    v_p = dout("v_p", [128, 256])
    conv_s = dout("conv_s", [NSEQ * 3, D])
    lru_s = dout("lru_s", [NSEQ, D])
    k_s = dout("k_s", [NSEQ, 128, 256])
    v_s = dout("v_s", [NSEQ, 128, 256])

    NT = NMAIN + 1
    scr = nc.dram_tensor("scr", [NT, 128, 2 * NCH * T], BF16, kind="Internal").ap()
    hres_scr = nc.dram_tensor("hres_scr", [NT, 128, D], F32, kind="Internal").ap()
    NTOK = NT * T
    G_scr = nc.dram_tensor("G_scr", [128, 128, NTOK], BF16, kind="Internal").ap()

    with ExitStack() as st:
        S = Sched(nc, st)
        PS = st.enter_context(nc.psum_tensor("PS", [128, 8, 512], F32))
        dP = [Dep("P%d" % b) for b in range(8)]

        def pbank(b):
            return PS[:, b, :]

        def pbank16(b):
            return PS[:, b, :].bitcast(BF16)

        cst = ExitStack()
        st.enter_context(cst)

        def SB(stack, name, shape, dt):
            return stack.enter_context(nc.sbuf_tensor("sb_" + name, list(shape), dt))

        identf = SB(cst, "identf", [128, 128], F32)
        identb = SB(cst, "identb", [128, 128], BF16)
        flag = SB(cst, "flag", [128, 1], F32)
        cp = SB(cst, "cp", [128, NCH, 16], F32)
        d_const = Dep("const")
        mhalf = SB(cst, "mhalf", [128, 16], F32)
        S.dma("sp", lambda e: e.dma_start(out=identf[:], in_=identf_d), writes=[d_const])
        S.dma("sp", lambda e: e.dma_start(out=flag[:], in_=flag_d), writes=[d_const])
        S.op("dve", lambda e: e.tensor_copy(out=identb[:], in_=identf[:]), reads=[d_const], writes=[d_const])
        S.op("dve", lambda e: e.memset(mhalf[:], -0.5), writes=[d_const])

        with ExitStack() as s0:
            prow = SB(s0, "prow", [16, D], F32)
            d_prow = Dep()
            S.dma("sp", lambda e: e.dma_start(out=prow[:], in_=prow_d), writes=[d_prow])
            for c in range(NCH):
                S.op("pe", lambda e: e.transpose(out=PS[:, 0, c * 16:(c + 1) * 16], in_=prow[:, c * 128:(c + 1) * 128],
                                                 identity=identf[0:16, 0:16]),
                     reads=[d_prow, d_const], writes=[dP[0]])
            S.op("dve", lambda e: e.tensor_copy(out=cp[:].rearrange("p c j -> p (c j)"), in_=PS[:, 0, 0:128]),
                 reads=[dP[0]], writes=[d_const])
            S.op("act", lambda e: e.activation(out=cp[:, :, 10], in_=cp[:, :, 7], func=AF.Exp, scale=-1.0),
                 reads=[d_const], writes=[d_const])
            S.op("act", lambda e: e.activation(out=cp[:, :, 10], in_=cp[:, :, 10], func=AF.Ln, bias=1.0),
                 reads=[d_const], writes=[d_const])
            S.op("dve", lambda e: e.tensor_scalar(out=cp[:, :, 10], in0=cp[:, :, 10], scalar1=-8.0, scalar2=None,
                                                  op0=ALU.mult), reads=[d_const], writes=[d_const])
            S.barrier()

        with ExitStack() as sa:
            WgA = SB(sa, "WgA", [128, NCH, 3584], BF16)
            d_WgA = Dep("WgA")
            with ExitStack() as s1:
                stg = [SB(s1, "stg%d" % i, [128, 3584], F32) for i in range(4)]
                d_stg = [Dep() for _ in range(4)]
                for kc in range(NCH):
                    b = kc % 4
                    S.dma("sp" if kc % 2 == 0 else "pool",
                          lambda e: e.dma_start(out=stg[b][:], in_=w_in_d[kc * 128:(kc + 1) * 128, 0:3584]),
                          writes=[d_stg[b]])
                    if kc % 2 == 0:
                        S.op("dve", lambda e: e.tensor_scalar(out=WgA[:, kc, :], in0=stg[b][:], scalar1=cp[:, kc, 8:9],
                                                              scalar2=None, op0=ALU.mult),
                             reads=[d_stg[b], d_const], writes=[d_WgA])
                    else:
                        S.op("act", lambda e: e.activation(out=WgA[:, kc, :], in_=stg[b][:], func=AF.Copy, scale=cp[:, kc, 8:9]),
                             reads=[d_stg[b], d_const], writes=[d_WgA])
                S.barrier()

            Wbd = SB(sa, "Wbd", [128, 2, NCH, 128], BF16)
            d_Wbd = Dep("Wbd")
            with ExitStack() as s1:
                wbdf = SB(s1, "wbdf", [128, 2, NCH, 128], F32)
                d_wbdf = Dep()
                S.op("pool", lambda e: e.memset(wbdf[:], 0.0), writes=[d_wbdf])
                for g in range(2):
                    for hb in range(2):
                        src = rgw_d[g].rearrange("(c two) k m -> two k c m", two=2)[hb]
                        S.dma("sp", lambda e: e.dma_start(out=wbdf[hb * 64:(hb + 1) * 64, g, :, hb * 64:(hb + 1) * 64],
                                                          in_=src), writes=[d_wbdf])
                S.op("dve", lambda e: e.tensor_copy(out=Wbd[:], in_=wbdf[:]), reads=[d_wbdf], writes=[d_Wbd])
                S.barrier()

            E_pr = SB(sa, "E_pr", [128, NH, 2, 128], F32)
            E_so = SB(sa, "E_so", [128, NH, 128], F32)
            E_sc = SB(sa, "E_sc", [128, NH, DEC], F32)
            gq8 = SB(sa, "gq8", [128, 64], F32)
            gk = SB(sa, "gk", [128, 64], F32)
            esk = SB(sa, "esk", [128, NH], F32)
            d_ac = Dep("attnconst")
            S.dma("sp", lambda e: e.dma_start(out=E_pr[:].rearrange("p h b q -> p (h b q)"), in_=E_pr_d), writes=[d_ac])
            S.dma("act", lambda e: e.dma_start(out=E_so[:].rearrange("p h q -> p (h q)"), in_=E_so_d), writes=[d_ac])
            S.dma("act", lambda e: e.dma_start(out=E_sc[:].rearrange("p h q -> p (h q)"), in_=E_sc_d), writes=[d_ac])
            S.dma("sp", lambda e: e.dma_start(out=gq8[:], in_=qkg_d[0:1, :].partition_broadcast(128)), writes=[d_ac])
            S.dma("sp", lambda e: e.dma_start(out=gk[:], in_=qkg_d[1:2, :].partition_broadcast(128)), writes=[d_ac])
            S.dma("sp", lambda e: e.dma_start(out=esk[:], in_=sinks_d[0:1, :].partition_broadcast(128)), writes=[d_ac])
            S.op("dve", lambda e: e.tensor_scalar(out=gq8[:], in0=gq8[:], scalar1=0.125, scalar2=None, op0=ALU.mult),
                 reads=[d_ac], writes=[d_ac])
            S.op("act", lambda e: e.activation(out=esk[:], in_=esk[:], func=AF.Exp), reads=[d_ac], writes=[d_ac])

            x_t = [SB(sa, "x_t%d" % i, [128, D], F32) for i in range(2)]
            d_x = [Dep(), Dep()]
            st1 = SB(sa, "st1", [128, 8], F32); d_st1 = Dep()
            xn = SB(sa, "xn", [128, D], BF16); d_xn = Dep()
            xnT = SB(sa, "xnT", [128, NCH, T], BF16); d_xnT = Dep()
            xr = [SB(sa, "xr%d" % i, [128, NCH, 3 + T], F32) for i in range(2)]
            d_xr = [Dep(), Dep()]
            xr_s = SB(sa, "xr_s", [128, NCH, NSEQ, 3 + DEC], F32); d_xr_s = Dep()
            gel2 = [SB(sa, "gel%d" % i, [128, NCH, T], BF16) for i in range(2)]; d_gel2 = [Dep(), Dep()]
            stk = SB(sa, "stk", [128, 4], F32); d_stk = Dep()
            rdq = SB(sa, "rdq", [128, 16], F32); d_rdq = Dep()
            xrtm = SB(sa, "xrtm", [128, D], F32); d_xrtm = Dep()
            gtm = SB(sa, "gtm", [128, D], BF16); d_gtm = Dep()
            xc = SB(sa, "xc", [128, NCH, T], F32); d_xc = Dep()
            d_xcc = [Dep() for _ in range(NCH)]
            xcb = SB(sa, "xcb", [128, NCH, T], BF16); d_xcb = Dep()
            rg = SB(sa, "rg", [128, NCH, T], F32); d_rg = Dep()
            ig = SB(sa, "ig", [128, NCH, T], F32); d_ig = Dep()
            sq = SB(sa, "sq", [128, NCH, T], F32); d_sq = Dep()
            hh = [SB(sa, "hh%d" % i, [128, NCH, T], F32) for i in range(2)]
            d_hh = [Dep(), Dep()]
            h0 = SB(sa, "h0", [128, NCH, NSEQ], F32); d_h0 = Dep()
            hcar = SB(sa, "hcar", [128, NCH], F32); d_hcar = Dep()
            fmo = [SB(sa, "fmo%d" % i, [128, 2, NCH, T], BF16) for i in range(2)]
            d_fmo = [Dep(), Dep()]
            qn = SB(sa, "qn", [128, D], BF16); d_qn = Dep()
            qT2 = [SB(sa, "qT%d" % i, [64, NH, T], BF16) for i in range(2)]; d_qT2 = [Dep(), Dep()]
            kn = SB(sa, "kn", [128, 256], F32); d_kn = Dep()
            ks = SB(sa, "ks", [128, 256], BF16); d_ks = Dep()
            kT = [SB(sa, "kT%d" % i, [64, NKV, T], BF16) for i in range(3)]
            d_kT = [Dep(), Dep(), Dep()]
            vf = SB(sa, "vf", [128, 256], F32); d_vf = Dep()
            vau = [SB(sa, "vau%d" % i, [128, NKV, 65], BF16) for i in range(3)]
            d_vau = [Dep(), Dep(), Dep()]
            ex = SB(sa, "ex", [128, 2, 4, T], F32); d_ex = Dep()
            pT = [SB(sa, "pT%d" % i, [128, 2, 4, T], BF16) for i in range(2)]
            d_pT = [Dep(), Dep()]
            rden = SB(sa, "rden", [128, 18], F32); d_rden = Dep()
            attn = SB(sa, "attn", [128, D], BF16); d_attn = Dep()
            tmo = SB(sa, "tmo", [128, D], F32); d_tmo = Dep()
            cmp_ = SB(sa, "cmp", [128, NCH, 48], F32); d_cmp = Dep()
            ckf = [SB(sa, "ckf0", [128, 256], F32)] * 2
            cvf = [SB(sa, "cvf0", [128, 256], F32)] * 2
            d_ckf = [Dep()] * 2
            d_cvf = [Dep()] * 2
            cks = SB(sa, "cks", [128, 256], BF16); d_cks = Dep()
            ckT = SB(sa, "ckT", [64, NKV, 128], BF16); d_ckT = Dep()
            cva = SB(sa, "cva", [128, NKV, 65], BF16); d_cva = Dep()
            exc = SB(sa, "exc", [128, NH, DEC], F32); d_exc = Dep()
            Zp = [SB(sa, "Zp0", [128, NH, 248], BF16)] * 2
            d_Zp = [Dep()] * 2
            oacc = SB(sa, "oacc", [128, 18, 65], F32); d_oacc = Dep()

            for i in range(3):
                S.op("pool", lambda e: e.memset(vau[i][:, :, 64:65], 1.0), writes=[d_vau[i]])
            for i in range(2):
                S.op("pool", lambda e: e.memset(Zp[i][:], 0.0), writes=[d_Zp[i]])
                S.op("pool", lambda e: e.memset(xr[i][:], 0.0), writes=[d_xr[i]])
                S.op("pool", lambda e: e.memset(hh[i][:], 0.0), writes=[d_hh[i]])
            S.op("pool", lambda e: e.memset(cva[:, :, 64:65], 1.0), writes=[d_cva])

            def fm_to_dram(src_fn, n, dst, reads):
                for c in range(NCH):
                    b = 6 + c // 4
                    S.op("pe", lambda e: e.transpose(out=PS[0:n, b, (c % 4) * 128:(c % 4 + 1) * 128], in_=src_fn(c),
                                                     identity=identf[:]),
                         reads=list(reads) + [d_const], writes=[dP[b]])
                S.op("dve", lambda e: e.tensor_copy(out=tmo[0:n, :].rearrange("p (b x) -> p b x", b=2),
                                                    in_=PS[0:n, 6:8, :]),
                     reads=[dP[6], dP[7]], writes=[d_tmo])
                S.dma("sp", lambda e: e.dma_start(out=dst, in_=tmo[0:n, :]), reads=[d_tmo], is_out=True)

            def norm_to_xnT(xtile, d_xtile):
                S.op("act", lambda e: e.activation(out=xn[:], in_=xtile, func=AF.Square, accum_out=st1[:, 0:1]),
                     reads=[d_xtile], writes=[d_xn, d_st1])
                S.op("dve", lambda e: e.tensor_scalar(out=st1[:, 1:2], in0=st1[:, 0:1], scalar1=1.0 / D, scalar2=EPS, op0=ALU.mult, op1=ALU.add),
                     reads=[d_st1], writes=[d_st1])
                S.op("pool", lambda e: e.tensor_tensor(out=st1[:, 2:3], in0=st1[:, 1:2], in1=mhalf[:, 0:1], op=ALU.pow),
                     reads=[d_st1, d_const], writes=[d_st1])
                S.op("act", lambda e: e.activation(out=xn[:], in_=xtile, func=AF.Copy, scale=st1[:, 2:3]),
                     reads=[d_xtile, d_st1], writes=[d_xn])
                for c in range(NCH):
                    S.op("pe", lambda e: e.transpose(out=pbank16(0)[:, c * 128:(c + 1) * 128], in_=xn[:, c * 128:(c + 1) * 128],
                                                     identity=identb[:]), reads=[d_xn, d_const], writes=[dP[0]])
                S.op("dve", lambda e: e.tensor_copy(out=xnT[:].rearrange("p c t -> p (c t)"), in_=pbank16(0)),
                     reads=[dP[0]], writes=[d_xnT])

            def proj_fm(col0, banks):
                for c in range(NCH):
                    b = banks[c // 4]
                    for k in range(NCH):
                        S.op("pe", lambda e: e.matmul(PS[:, b, (c % 4) * 128:(c % 4 + 1) * 128],
                                                      lhsT=WgA[:, k, col0 + c * 128: col0 + (c + 1) * 128],
                                                      rhs=xnT[:, k, :], start=(k == 0), stop=(k == NCH - 1)),
                             reads=[d_WgA, d_xnT], writes=[dP[b]])

            def proj_xr(par, sample=False, conv_first=False):
                for j in range(2):
                    for k in range(NCH):
                        S.op("pe", lambda e: e.matmul(PS[:, 1 + j, :], lhsT=xnT[:, k, :], rhs=WgA[:, k, j * 512:(j + 1) * 512],
                                                      start=(k == 0), stop=(k == NCH - 1)),
                             reads=[d_WgA, d_xnT], writes=[dP[1 + j]])
                S.op("act", lambda e: e.activation(out=xrtm[:].rearrange("p (b x) -> p b x", b=2), in_=PS[:, 1:3, :], func=AF.Copy),
                     reads=[dP[1], dP[2]], writes=[d_xrtm])
                for c in range(NCH):
                    b = 1 + c // 4
                    S.op("pe", lambda e: e.transpose(out=PS[:, b, (c % 4) * 128:(c % 4 + 1) * 128], in_=xrtm[:, c * 128:(c + 1) * 128],
                                                     identity=identf[:]), reads=[d_xrtm, d_const], writes=[dP[b]])
                if sample:
                    S.op("dve", lambda e: e.tensor_copy(
                        out=xr_s[:, :, :, 3:3 + DEC],
                        in_=PS[:, 1:3, :].rearrange("p b (c s t) -> p (b c) s t", c=4, t=DEC)),
                        reads=[dP[1], dP[2]], writes=[d_xr_s])
                else:
                    if not conv_first:
                        S.op("pool", lambda e: e.tensor_copy(out=xr[par][:, :, 0:3], in_=xr[1 - par][:, :, T:T + 3]),
                             reads=[d_xr[1 - par]], writes=[d_xr[par]])
                    S.op("dve", lambda e: e.tensor_copy(
                        out=xr[par][:, :, 3:3 + T], in_=PS[:, 1:3, :].rearrange("p b (c t) -> p (b c) t", c=4)),
                        reads=[dP[1], dP[2]], writes=[d_xr[par]])

            def proj_gr(par):
                gel, d_gel = gel2[par], d_gel2[par]
                for j in range(2):
                    for k in range(NCH):
                        S.op("pe", lambda e: e.matmul(PS[:, 3 + j, :], lhsT=xnT[:, k, :], rhs=WgA[:, k, 1024 + j * 512:1024 + (j + 1) * 512],
                                                      start=(k == 0), stop=(k == NCH - 1)),
                             reads=[d_WgA, d_xnT], writes=[dP[3 + j]])
                S.op("act", lambda e: e.activation(out=gtm[:].rearrange("p (b x) -> p b x", b=2), in_=PS[:, 3:5, :], func=AF.Gelu_apprx_tanh),
                     reads=[dP[3], dP[4]], writes=[d_gtm])
                for c in range(NCH):
                    S.op("pe", lambda e: e.transpose(out=pbank16(3)[:, c * 128:(c + 1) * 128], in_=gtm[:, c * 128:(c + 1) * 128],
                                                     identity=identb[:]), reads=[d_gtm, d_const], writes=[dP[3]])
                S.op("act", lambda e: e.activation(out=gel[:].rearrange("p c t -> p (c t)"), in_=pbank16(3), func=AF.Copy),
                     reads=[dP[3]], writes=[d_gel])

            def lru_tile(par, sample, first, need_out, conv_first=None):
                if conv_first is None:
                    conv_first = first
                XR = xr_s if sample else xr[par]
                dXR = d_xr_s if sample else d_xr[par]
                nseq, L = (NSEQ, DEC) if sample else (1, T)

                def v4(ap3):
                    return ap3.rearrange("p (s t) -> p s t", t=L)

                def win_(c, j):
                    return XR[:, c, :, j:j + L] if sample else XR[:, c, j:j + L]

                def dst_(c):
                    return v4(xc[:, c, :]) if sample else xc[:, c, :]

                for c in range(NCH):
                    S.op("act", lambda e: e.activation(out=dst_(c), in_=win_(c, 0), func=AF.Identity,
                                                       scale=cp[:, c, 0:1], bias=cp[:, c, 4:5]),
                         reads=[dXR, d_const], writes=[d_xcc[c]])
                for c in range(NCH):
                    for j in range(1, 4):
                        S.op("dve", lambda e: e.scalar_tensor_tensor(out=dst_(c), in0=win_(c, j), scalar=cp[:, c, j:j + 1], in1=dst_(c),
                                                                     op0=ALU.mult, op1=ALU.add),
                             reads=[dXR, d_const, d_xcc[c]], writes=[d_xcc[c]])
                    if c % 2 == 1:
                        yield
                yield
                S.op("act", lambda e: e.activation(out=xcb[:], in_=xc[:], func=AF.Copy), reads=d_xcc, writes=[d_xcb])
                for g, (dst, ddst, banks, bcol) in enumerate(((rg, d_rg, (1, 2), 5), (ig, d_ig, (3, 4), 6))):
                    for c in range(NCH):
                        b = banks[c // 4]
                        S.op("pe", lambda e: e.matmul(PS[:, b, (c % 4) * 128:(c % 4 + 1) * 128], lhsT=Wbd[:, g, c, :],
                                                      rhs=xcb[:, c, :], start=True, stop=True),
                             reads=[d_Wbd, d_xcb], writes=[dP[b]])
                    for c in range(NCH):
                        b = banks[c // 4]
                        S.op("act", lambda e: e.activation(out=dst[:, c, :], in_=PS[:, b, (c % 4) * 128:(c % 4 + 1) * 128],
                                                           func=AF.Sigmoid, bias=cp[:, c, bcol:bcol + 1]),
                             reads=[dP[b], d_const], writes=[ddst])
                    yield
                S.op("dve", lambda e: e.tensor_tensor(out=rg[:], in0=rg[:], in1=cp[:, :, 10:11].to_broadcast([128, NCH, T]),
                                                      op=ALU.mult), reads=[d_rg, d_const], writes=[d_rg])
                S.op("act", lambda e: e.activation(out=rg[:], in_=rg[:], func=AF.Exp), reads=[d_rg], writes=[d_rg])
                yield
                S.op("act", lambda e: e.activation(out=sq[:], in_=rg[:], func=AF.Square), reads=[d_rg], writes=[d_sq])
                S.op("act", lambda e: e.activation(out=sq[:], in_=sq[:], func=AF.Sqrt, scale=-1.0, bias=1.0),
                     reads=[d_sq], writes=[d_sq])
                S.op("dve", lambda e: e.tensor_tensor(out=ig[:], in0=ig[:], in1=xc[:], op=ALU.mult),
                     reads=[d_ig] + d_xcc, writes=[d_ig])
                yield
                S.op("dve", lambda e: e.tensor_tensor(out=ig[:], in0=ig[:], in1=sq[:], op=ALU.mult),
                     reads=[d_ig, d_sq], writes=[d_ig])
                H = hh[par]
                if sample:
                    Hv = H[:].rearrange("p c (s t) -> p c s t", t=DEC)
                    Av = rg[:].rearrange("p c (s t) -> p c s t", t=DEC)
                    Bv = ig[:].rearrange("p c (s t) -> p c s t", t=DEC)
                    for t in range(DEC):
                        prev = h0[:] if t == 0 else Hv[:, :, :, t - 1]
                        S.op("dve", lambda e: e.tensor_tensor(out=Hv[:, :, :, t], in0=Av[:, :, :, t], in1=prev, op=ALU.mult),
                             reads=[d_rg, d_h0, d_hh[par]], writes=[d_hh[par]])
                        S.op("dve", lambda e: e.tensor_tensor(out=Hv[:, :, :, t], in0=Hv[:, :, :, t], in1=Bv[:, :, :, t], op=ALU.add),
                             reads=[d_ig, d_hh[par]], writes=[d_hh[par]])
                else:
                    hprev = h0[:, :, 0] if first else hh[1 - par][:, :, T - 1]
                    S.op("dve", lambda e: e.tensor_tensor(out=hcar[:], in0=rg[:, :, 0], in1=hprev, op=ALU.mult),
                         reads=[d_rg, d_h0, d_hh[1 - par]], writes=[d_hcar])
                    S.op("dve", lambda e: e.tensor_tensor(out=ig[:, :, 0], in0=ig[:, :, 0], in1=hcar[:], op=ALU.add),
                         reads=[d_ig, d_hcar], writes=[d_ig])
                    S.op("dve", lambda e: e.memset(rg[:, :, 0], 0.0), reads=[d_hcar], writes=[d_rg])
                    S.op("dve", lambda e: e.tensor_tensor_scan(out=H[:].rearrange("p c t -> p (c t)"),
                                                               data0=rg[:].rearrange("p c t -> p (c t)"),
                                                               data1=ig[:].rearrange("p c t -> p (c t)"),
                                                               initial=0.0, op0=ALU.mult, op1=ALU.add),
                         reads=[d_rg, d_ig], writes=[d_hh[par]])
                yield
                if need_out:
                    S.op("dve", lambda e: e.tensor_tensor(out=fmo[par][:, 0, :, :], in0=H[:], in1=gel2[par][:], op=ALU.mult),
                         reads=[d_hh[par], d_gel2[par]], writes=[d_fmo[par]])
                yield

            def kv_prep(par):
                for k in range(NCH):
                    S.op("pe", lambda e: e.matmul(PS[:, 3, :], lhsT=xnT[:, k, :], rhs=WgA[:, k, 3072:3584],
                                                  start=(k == 0), stop=(k == NCH - 1)),
                         reads=[d_WgA, d_xnT], writes=[dP[3]])
                S.op("act", lambda e: e.activation(out=xrtm[:, 0:256], in_=PS[:, 3, 0:256], func=AF.Square),
                     reads=[dP[3]], writes=[d_xrtm])
                S.op("dve", lambda e: e.tensor_reduce(out=stk[:, 0:4], in_=xrtm[:, 0:256].rearrange("p (h d) -> p h d", d=HD),
                                                      axis=AX.X, op=ALU.add), reads=[d_xrtm], writes=[d_stk])
                S.op("dve", lambda e: e.tensor_scalar(out=stk[:, 0:4], in0=stk[:, 0:4], scalar1=1.0 / HD, scalar2=EPS, op0=ALU.mult, op1=ALU.add),
                     reads=[d_stk], writes=[d_stk])
                S.op("pool", lambda e: e.tensor_tensor(out=stk[:, 0:4], in0=stk[:, 0:4], in1=mhalf[:, 0:4], op=ALU.pow),
                     reads=[d_stk, d_const], writes=[d_stk])
                S.op("dve", lambda e: e.tensor_tensor(out=kn[:].rearrange("p (h d) -> p h d", d=HD),
                                                      in0=PS[:, 3, 0:256].rearrange("p (h d) -> p h d", d=HD),
                                                      in1=stk[:, 0:4].unsqueeze(2).to_broadcast([128, NKV, HD]), op=ALU.mult),
                     reads=[dP[3], d_stk], writes=[d_kn])
                S.op("dve", lambda e: e.tensor_tensor(out=kn[:].rearrange("p (h d) -> p h d", d=HD),
                                                      in0=kn[:].rearrange("p (h d) -> p h d", d=HD),
                                                      in1=gk[:].unsqueeze(1).to_broadcast([128, NKV, HD]), op=ALU.mult),
                     reads=[d_kn, d_ac], writes=[d_kn])
                S.op("pool", lambda e: e.tensor_tensor(out=ks[:].rearrange("p (h d) -> p h d", d=HD),
                                                       in0=kn[:].rearrange("p (h d) -> p h d", d=HD),
                                                       in1=gq8[:].unsqueeze(1).to_broadcast([128, NKV, HD]), op=ALU.mult),
                     reads=[d_kn, d_ac], writes=[d_ks])
                S.op("act", lambda e: e.activation(out=vf[:], in_=PS[:, 3, 256:512], func=AF.Copy),
                     reads=[dP[3]], writes=[d_vf])
                S.op("act", lambda e: e.activation(out=vau[par][:, :, 0:64], in_=vf[:].rearrange("p (h d) -> p h d", d=HD), func=AF.Copy),
                     reads=[d_vf], writes=[d_vau[par]])
                S.op("pool", lambda e: e.memset(vau[par][:, :, 64:65], 1.0), writes=[d_vau[par]])
                for h in range(NKV):
                    S.op("pe", lambda e: e.transpose(out=pbank16(0)[0:64, h * 128:(h + 1) * 128], in_=ks[:, h * 64:(h + 1) * 64],
                                                     identity=identb[:]), reads=[d_ks, d_const], writes=[dP[0]])
                S.op("act", lambda e: e.activation(out=kT[par][:].rearrange("p h t -> p (h t)"), in_=pbank16(0)[0:64, 0:512],
                                                   func=AF.Copy), reads=[dP[0]], writes=[d_kT[par]])

            def q_prep(qb):
                qT, d_qT = qT2[qb], d_qT2[qb]
                for j in range(2):
                    for k in range(NCH):
                        S.op("pe", lambda e: e.matmul(PS[:, 3 + j, :], lhsT=xnT[:, k, :],
                                                      rhs=WgA[:, k, 2048 + j * 512:2048 + (j + 1) * 512],
                                                      start=(k == 0), stop=(k == NCH - 1)),
                             reads=[d_WgA, d_xnT], writes=[dP[3 + j]])
                S.op("act", lambda e: e.activation(out=xrtm[:].rearrange("p (b x) -> p b x", b=2), in_=PS[:, 3:5, :], func=AF.Square),
                     reads=[dP[3], dP[4]], writes=[d_xrtm])
                S.op("dve", lambda e: e.tensor_reduce(out=rdq[:, 0:16], in_=xrtm[:].rearrange("p (h d) -> p h d", d=HD),
                                                      axis=AX.X, op=ALU.add), reads=[d_xrtm], writes=[d_rdq])
                S.op("dve", lambda e: e.tensor_scalar(out=rdq[:, 0:16], in0=rdq[:, 0:16], scalar1=1.0 / HD, scalar2=EPS, op0=ALU.mult, op1=ALU.add),
                     reads=[d_rdq], writes=[d_rdq])
                S.op("pool", lambda e: e.tensor_tensor(out=rdq[:, 0:16], in0=rdq[:, 0:16], in1=mhalf[:, 0:16], op=ALU.pow),
                     reads=[d_rdq, d_const], writes=[d_rdq])
                S.op("dve", lambda e: e.tensor_tensor(out=qn[:].rearrange("p (b h d) -> p b h d", b=2, d=HD),
                                                      in0=PS[:, 3:5, :].rearrange("p b (h d) -> p b h d", d=HD),
                                                      in1=rdq[:, 0:16].rearrange("p (b h) -> p b h", b=2).unsqueeze(3).to_broadcast([128, 2, 8, HD]),
                                                      op=ALU.mult),
                     reads=[dP[3], dP[4], d_rdq], writes=[d_qn])
                for h in range(NH):
                    b = 3 + h // 8
                    S.op("pe", lambda e: e.transpose(out=pbank16(b)[0:64, (h % 8) * 128:(h % 8 + 1) * 128],
                                                     in_=qn[:, h * 64:(h + 1) * 64], identity=identb[:]),
                         reads=[d_qn, d_const], writes=[dP[b]])
                for j in range(2):
                    S.op("act" if j == 0 else "dve",
                         (lambda e: e.activation(out=qT[:, 0:8, :].rearrange("p h t -> p (h t)"), in_=pbank16(3)[0:64, :], func=AF.Copy))
                         if j == 0 else
                         (lambda e: e.tensor_copy(out=qT[:, 8:16, :].rearrange("p h t -> p (h t)"), in_=pbank16(4)[0:64, :])),
                         reads=[dP[3 + j]], writes=[d_qT])

            def oslot(h):
                return PS[:, 5 + h // 6, (h % 6) * 80:(h % 6) * 80 + 65]

            def attn_own_prev(kb, qb, Eown_fn, kbp, Eprev_fn):
                qT, d_qT = qT2[qb], d_qT2[qb]
                nb = 2 if kbp is not None else 1
                for g4 in range(4):
                    pp = g4 % 2
                    for bi in range(nb):
                        kk = kb if bi == 0 else kbp
                        for hh_ in range(4):
                            h = g4 * 4 + hh_
                            S.op("pe", lambda e: e.matmul(PS[:, 1 + bi, hh_ * 128:(hh_ + 1) * 128], lhsT=kT[kk][:, g4, :],
                                                          rhs=qT[:, h, :], start=True, stop=True),
                                 reads=[d_kT[kk], d_qT], writes=[dP[1 + bi]])
                    S.op("act", lambda e: e.activation(out=ex[:, 0:nb, :, :].rearrange("p b h t -> p b (h t)"),
                                                       in_=PS[:, 1:1 + nb, :], func=AF.Exp),
                         reads=[dP[1], dP[2]][:nb], writes=[d_ex])
                    S.op("dve", lambda e: e.tensor_tensor(out=pT[pp][:, 0, :, :], in0=ex[:, 0, :, :], in1=Eown_fn(g4), op=ALU.mult),
                         reads=[d_ex, d_ac], writes=[d_pT[pp]])
                    if nb == 2:
                        S.op("dve", lambda e: e.tensor_tensor(out=pT[pp][:, 1, :, :], in0=ex[:, 1, :, :], in1=Eprev_fn(g4), op=ALU.mult),
                             reads=[d_ex, d_ac], writes=[d_pT[pp]])
                    for hh_ in range(4):
                        h = g4 * 4 + hh_
                        b = 5 + h // 6
                        S.op("pe", lambda e: e.matmul(oslot(h), lhsT=pT[pp][:, 0, hh_, :], rhs=vau[kb][:, g4, :],
                                                      start=True, stop=(nb == 1)),
                             reads=[d_pT[pp], d_vau[kb]], writes=[dP[b]])
                        if nb == 2:
                            S.op("pe", lambda e: e.matmul(oslot(h), lhsT=pT[pp][:, 1, hh_, :], rhs=vau[kbp][:, g4, :],
                                                          start=False, stop=True),
                                 reads=[d_pT[pp], d_vau[kbp]], writes=[dP[b]])
                    yield

            def attn_finish(par, from_oacc):
                if from_oacc:
                    den_src = oacc[:, 0:16, 64]
                    S.op("dve", lambda e: e.tensor_tensor(out=rden[:, 0:16], in0=den_src, in1=esk[:], op=ALU.add),
                         reads=[d_oacc, d_ac], writes=[d_rden])
                    S.op("dve", lambda e: e.reciprocal(out=rden[:, 0:16], in_=rden[:, 0:16]), reads=[d_rden], writes=[d_rden])
                    S.op("dve", lambda e: e.tensor_tensor(out=attn[:].rearrange("p (h d) -> p h d", d=HD), in0=oacc[:, 0:16, 0:64],
                                                          in1=rden[:, 0:16].unsqueeze(2).to_broadcast([128, 16, HD]), op=ALU.mult),
                         reads=[d_oacc, d_rden], writes=[d_attn])
                else:
                    pv = PS[:, 5:8, 0:480].rearrange("p b (s e) -> p b s e", e=80)
                    S.op("dve", lambda e: e.tensor_copy(out=rden[:, 0:12].rearrange("p (b s) -> p b s", b=2), in_=pv[:, 0:2, :, 64]),
                         reads=[dP[5], dP[6]], writes=[d_rden])
                    S.op("dve", lambda e: e.tensor_copy(out=rden[:, 12:16], in_=pv[:, 2, 0:4, 64]),
                         reads=[dP[7]], writes=[d_rden])
                    S.op("dve", lambda e: e.tensor_tensor(out=rden[:, 0:16], in0=rden[:, 0:16], in1=esk[:], op=ALU.add),
                         reads=[d_rden, d_ac], writes=[d_rden])
                    S.op("dve", lambda e: e.reciprocal(out=rden[:, 0:16], in_=rden[:, 0:16]), reads=[d_rden], writes=[d_rden])
                    S.op("dve", lambda e: e.tensor_tensor(out=attn[:, 0:768].rearrange("p (b s d) -> p b s d", b=2, d=HD),
                                                          in0=pv[:, 0:2, :, 0:64],
                                                          in1=rden[:, 0:12].rearrange("p (b s) -> p b s", b=2).unsqueeze(3).to_broadcast([128, 2, 6, HD]),
                                                          op=ALU.mult),
                         reads=[dP[5], dP[6], d_rden], writes=[d_attn])
                    S.op("dve", lambda e: e.tensor_tensor(out=attn[:, 768:1024].rearrange("p (s d) -> p s d", d=HD),
                                                          in0=pv[:, 2, 0:4, 0:64],
                                                          in1=rden[:, 12:16].unsqueeze(2).to_broadcast([128, 4, HD]), op=ALU.mult),
                         reads=[dP[7], d_rden], writes=[d_attn])
                for c in range(NCH):
                    S.op("pe", lambda e: e.transpose(out=pbank16(0)[:, c * 128:(c + 1) * 128], in_=attn[:, c * 128:(c + 1) * 128],
                                                     identity=identb[:]), reads=[d_attn, d_const], writes=[dP[0]])
                S.op("act", lambda e: e.activation(out=fmo[par][:, 1, :, :].rearrange("p c t -> p (c t)"), in_=pbank16(0), func=AF.Copy),
                     reads=[dP[0]], writes=[d_fmo[par]])

            S.op("pool", lambda e: e.memset(h0[:], 0.0), writes=[d_h0])
            ntile = NPRE + NMAIN

            def attn_gen(ti):
                par = ti % 2
                mi = ti - NPRE
                yield from attn_own_prev(ti % 3, ti % 2, lambda g4: E_pr[:, g4 * 4:(g4 + 1) * 4, 1, :],
                                         (ti - 1) % 3, (lambda g4: E_pr[:, g4 * 4:(g4 + 1) * 4, 0, :]))
                attn_finish(par, False)
                S.dma("pool", lambda e: e.dma_start(out=scr[mi], in_=fmo[par][:].rearrange("p a c t -> p (a c t)")),
                      reads=[d_fmo[par]])
                yield

            def run_gens(gens):
                gens = list(gens)
                while gens:
                    for g in list(gens):
                        try:
                            next(g)
                        except StopIteration:
                            gens.remove(g)

            def head_gen(ti):
                par = ti % 2
                main = ti >= NPRE
                S.dma("sp" if par == 0 else "act",
                      lambda e: e.dma_start(out=x_t[par][:], in_=xp[ti * T:(ti + 1) * T, :]), writes=[d_x[par]])
                norm_to_xnT(x_t[par][:], d_x[par])
                yield
                proj_xr(par, sample=False, conv_first=(ti == 0))
                yield
                if main:
                    proj_gr(par)
                    yield
                if ti == NPRE - 1 or main:
                    kv_prep(ti % 3)
                    yield
                if ti == NPRE - 1:
                    kb_ = ti % 3
                    S.op("dve", lambda e: e.tensor_scalar(out=vau[kb_][:], in0=vau[kb_][:], scalar1=flag[:, 0:1], scalar2=None,
                                                          op0=ALU.mult), reads=[d_vau[kb_], d_const], writes=[d_vau[kb_]])
                if main:
                    q_prep(ti % 2)
                    yield
                if ti == ntile - 1:
                    S.dma("sp", lambda e: e.dma_start(out=k_p, in_=kn[:]), reads=[d_kn], is_out=True)
                    S.dma("sp", lambda e: e.dma_start(out=v_p, in_=vf[:]), reads=[d_vf], is_out=True)

            pending = None
            run_gens([head_gen(0)])
            for ti in range(ntile):
                par = ti % 2
                main = ti >= NPRE
                if ti == NPRE:
                    S.op("dve", lambda e: e.tensor_scalar(out=h0[:, :, 0], in0=hh[1 - par][:, :, T - 1], scalar1=flag[:, 0:1],
                                                          scalar2=None, op0=ALU.mult),
                         reads=[d_hh[1 - par], d_const], writes=[d_h0])
                gens = [lru_tile(par, sample=False, first=(ti == 0 or ti == NPRE), need_out=main)]
                if pending is not None:
                    gens.append(attn_gen(pending))
                if ti + 1 < ntile:
                    gens.append(head_gen(ti + 1))
                run_gens(gens)
                pending = ti if main else None
                if ti == ntile - 1:
                    fm_to_dram(lambda c: xr[par][:, c, T:T + 3], 3, conv_p, [d_xr[par]])
                    fm_to_dram(lambda c: hh[par][:, c, T - 1:T], 1, lru_p, [d_hh[par]])
            run_gens([attn_gen(pending)])

            par = ntile % 2
            S.dma("sp", lambda e: e.dma_start(out=x_t[par][:], in_=xs), writes=[d_x[par]])
            S.dma("act", lambda e: e.dma_start(out=tmo[0:48, :], in_=cconv), reads=[], writes=[d_tmo])
            for c in range(NCH):
                S.op("pe", lambda e: e.transpose(out=PS[:, 6, c * 48:(c + 1) * 48], in_=tmo[0:48, c * 128:(c + 1) * 128],
                                                 identity=identf[0:48, 0:48]), reads=[d_tmo, d_const], writes=[dP[6]])
            S.op("dve", lambda e: e.tensor_copy(out=xr_s[:, :, :, 0:3], in_=PS[:, 6, 0:384].rearrange("p (c s j) -> p c s j", c=NCH, j=3)),
                 reads=[dP[6]], writes=[d_xr_s])
            S.dma("act", lambda e: e.dma_start(out=tmo[0:16, :], in_=slru), reads=[], writes=[d_tmo])
            for c in range(NCH):
                S.op("pe", lambda e: e.transpose(out=PS[:, 6, c * 16:(c + 1) * 16], in_=tmo[0:16, c * 128:(c + 1) * 128],
                                                 identity=identf[0:16, 0:16]), reads=[d_tmo, d_const], writes=[dP[6]])
            S.op("dve", lambda e: e.tensor_copy(out=h0[:], in_=PS[:, 6, 0:128].rearrange("p (c s) -> p c s", c=NCH)),
                 reads=[dP[6]], writes=[d_h0])
            norm_to_xnT(x_t[par][:], d_x[par])
            proj_xr(par, sample=True)
            proj_gr(par)
            for _ in lru_tile(par, sample=True, first=True, need_out=True):
                pass
            kv_prep(0)
            q_prep(0)
            qT, d_qT = qT2[0], d_qT2[0]
            for _ in attn_own_prev(0, 0, lambda g4: E_so[:, g4 * 4:(g4 + 1) * 4, :], None, None):
                pass
            pv = PS[:, 5:8, 0:480].rearrange("p b (s e) -> p b s e", e=80)
            S.op("dve", lambda e: e.tensor_copy(out=oacc[:, 0:12, :].rearrange("p (b s) e -> p b s e", b=2), in_=pv[:, 0:2, :, 0:65]),
                 reads=[dP[5], dP[6]], writes=[d_oacc])
            S.op("dve", lambda e: e.tensor_copy(out=oacc[:, 12:16, :], in_=pv[:, 2, 0:4, 0:65]),
                 reads=[dP[7]], writes=[d_oacc])
            for sq_ in range(NSEQ):
                cb = sq_ % 2
                S.dma("sp", lambda e: e.dma_start(out=ckf[cb][:], in_=ck[sq_]), writes=[d_ckf[cb]])
                S.dma("act", lambda e: e.dma_start(out=cvf[cb][:], in_=cv[sq_]), writes=[d_cvf[cb]])
                S.op("pool", lambda e: e.tensor_tensor(out=cks[:].rearrange("p (h d) -> p h d", d=HD),
                                                       in0=ckf[cb][:].rearrange("p (h d) -> p h d", d=HD),
                                                       in1=gq8[:].unsqueeze(1).to_broadcast([128, NKV, HD]), op=ALU.mult),
                     reads=[d_ckf[cb], d_ac], writes=[d_cks])
                S.op("pool", lambda e: e.tensor_copy(out=cva[:, :, 0:64], in_=cvf[cb][:].rearrange("p (h d) -> p h d", d=HD)),
                     reads=[d_cvf[cb]], writes=[d_cva])
                for h in range(NKV):
                    S.op("pe", lambda e: e.transpose(out=pbank16(0)[0:64, h * 128:(h + 1) * 128], in_=cks[:, h * 64:(h + 1) * 64],
                                                     identity=identb[:]), reads=[d_cks, d_const], writes=[dP[0]])
                S.op("act", lambda e: e.activation(out=ckT[:].rearrange("p h t -> p (h t)"), in_=pbank16(0)[0:64, 0:512], func=AF.Copy),
                     reads=[dP[0]], writes=[d_ckT])
                for g4 in range(NKV):
                    S.op("pe", lambda e: e.matmul(PS[:, 1, g4 * 32:(g4 + 1) * 32], lhsT=ckT[:, g4, :],
                                                  rhs=qT[:, g4 * 4:(g4 + 1) * 4, sq_ * DEC:(sq_ + 1) * DEC],
                                                  start=True, stop=True), reads=[d_ckT, d_qT], writes=[dP[1]])
                S.op("act", lambda e: e.activation(out=exc[:].rearrange("p h t -> p (h t)"), in_=PS[:, 1, 0:128], func=AF.Exp),
                     reads=[dP[1]], writes=[d_exc])
                S.op("dve", lambda e: e.tensor_tensor(out=Zp[cb][:, :, 120:128], in0=exc[:], in1=E_sc[:], op=ALU.mult),
                     reads=[d_exc, d_ac], writes=[d_Zp[cb]])
                for h in range(NH):
                    b = 5 + h // 6
                    S.op("pe", lambda e: e.matmul(oslot(h), lhsT=Zp[cb][:, h, 120 - sq_ * DEC:248 - sq_ * DEC], rhs=cva[:, h // 4, :],
                                                  start=True, stop=True), reads=[d_Zp[cb], d_cva], writes=[dP[b]])
                S.op("dve", lambda e: e.tensor_tensor(out=oacc[:, 0:12, :].rearrange("p (b s) e -> p b s e", b=2),
                                                      in0=oacc[:, 0:12, :].rearrange("p (b s) e -> p b s e", b=2), in1=pv[:, 0:2, :, 0:65], op=ALU.add),
                     reads=[dP[5], dP[6], d_oacc], writes=[d_oacc])
                S.op("dve", lambda e: e.tensor_tensor(out=oacc[:, 12:16, :], in0=oacc[:, 12:16, :], in1=pv[:, 2, 0:4, 0:65], op=ALU.add),
                     reads=[dP[7], d_oacc], writes=[d_oacc])
            attn_finish(par, True)
            S.dma("pool", lambda e: e.dma_start(out=scr[NMAIN], in_=fmo[par][:].rearrange("p a c t -> p (a c t)")),
                  reads=[d_fmo[par]])
            S.op("dve", lambda e: e.tensor_copy(out=cmp_[:].rearrange("p c (s j) -> p c s j", j=3), in_=xr_s[:, :, :, DEC:DEC + 3]),
                 reads=[d_xr_s], writes=[d_cmp])
            fm_to_dram(lambda c: cmp_[:, c, :], 48, conv_s, [d_cmp])
            S.op("dve", lambda e: e.tensor_copy(out=cmp_[:, :, 0:16], in_=hh[par][:].rearrange("p c (s t) -> p c s t", t=DEC)[:, :, :, DEC - 1]),
                 reads=[d_hh[par]], writes=[d_cmp])
            fm_to_dram(lambda c: cmp_[:, c, 0:16], 16, lru_s, [d_cmp])
            S.dma("sp", lambda e: e.dma_start(out=k_s[:, 0:120, :], in_=ck[:, 8:128, :]), is_out=True)
            S.dma("act", lambda e: e.dma_start(out=v_s[:, 0:120, :], in_=cv[:, 8:128, :]), is_out=True)
            for sq_ in range(NSEQ):
                S.dma("sp", lambda e: e.dma_start(out=k_s[sq_, 120:128, :], in_=kn[sq_ * DEC:(sq_ + 1) * DEC, :]), reads=[d_kn], is_out=True)
                S.dma("act", lambda e: e.dma_start(out=v_s[sq_, 120:128, :], in_=vf[sq_ * DEC:(sq_ + 1) * DEC, :]), reads=[d_vf], is_out=True)
            S.barrier()

        sbc = ExitStack()
        st.enter_context(sbc)
        d_hres = [Dep() for _ in range(NT)]
        spc = ExitStack()
        st.enter_context(spc)
        xn2T_all = SB(spc, "xn2T_all", [128, NCH, NTOK], BF16)
        d_xn2T_all = [Dep() for _ in range(NT)]
        slotT = SB(spc, "slotT", [128, 3, NTOK], BF16)
        d_slotT = [Dep() for _ in range(NT)]
        swq = ExitStack()
        Wq = SB(swq, "Wq", [128, NCH, 2048], BF16)
        SKT = SB(swq, "SKT", [128, 16, 128], BF16)
        d_WC = Dep("WC")

        if STAGE >= 2:
          with ExitStack() as sb_:
            WgB = SB(sb_, "WgB", [128, NCH, 2048], BF16)
            Wl = SB(sb_, "Wl", [128, NCH, D], BF16)
            Wa = SB(sb_, "Wa", [128, NCH, D], BF16)
            Wo = SB(sb_, "Wo", [128, NCH, D], BF16)
            d_WB = Dep("WB")
            with ExitStack() as s1:
                stg = [SB(s1, "stgB%d" % i, [128, 2048], F32) for i in range(4)]
                d_stg = [Dep() for _ in range(4)]
                jobs = []
                for kc in range(NCH):
                    jobs.append((w_in_d[kc * 128:(kc + 1) * 128, 3584:5632], WgB[:, kc, :], cp[:, kc, 8:9], d_WB))
                for (wd, wsb) in ((w_bl_d, Wl), (w_ba_d, Wa), (w_o_d, Wo)):
                    wv = wd.rearrange("(kc p) n -> p kc n", p=128)
                    for k2 in range(NCH // 2):
                        jobs.append((wv[:, 2 * k2:2 * k2 + 2, :], wsb[:, 2 * k2:2 * k2 + 2, :].rearrange("p a n -> p (a n)"), None, d_WB))
                if STAGE >= 3:
                    for kc in range(NCH):
                        jobs.append((w_q_d[kc * 128:(kc + 1) * 128, :], Wq[:, kc, :], cp[:, kc, 9:10], d_WC))
                for n, (src, dst, sc_ap, ddst) in enumerate(jobs):
                    b = n % 4
                    o_ap = stg[b][:] if len(src.shape) == 2 else stg[b][:].rearrange("p (a n) -> p a n", a=2)
                    S.dma("sp" if n % 2 == 0 else "pool", lambda e: e.dma_start(out=o_ap, in_=src), writes=[d_stg[b]])
                    if n % 2 == 0:
                        if sc_ap is None:
                            S.op("dve", lambda e: e.tensor_copy(out=dst, in_=stg[b][:]), reads=[d_stg[b]], writes=[ddst])
                        else:
                            S.op("dve", lambda e: e.tensor_scalar(out=dst, in0=stg[b][:], scalar1=sc_ap, scalar2=None, op0=ALU.mult),
                                 reads=[d_stg[b], d_const], writes=[ddst])
                    else:
                        if sc_ap is None:
                            S.op("act", lambda e: e.activation(out=dst, in_=stg[b][:], func=AF.Copy), reads=[d_stg[b]], writes=[ddst])
                        else:
                            S.op("act", lambda e: e.activation(out=dst, in_=stg[b][:], func=AF.Copy, scale=sc_ap),
                                 reads=[d_stg[b], d_const], writes=[ddst])
                if STAGE >= 3:
                    S.dma("sp", lambda e: e.dma_start(out=stg[0][:].rearrange("p (h d) -> p h d", d=128),
                                                      in_=subk_d.rearrange("h k d -> k h d")),
                          reads=[d_stg[0]], writes=[d_stg[0]])
                    for hp in range(16):
                        b = 1 + hp // 4
                        S.op("pe", lambda e: e.transpose(out=PS[:, b, (hp % 4) * 128:(hp % 4 + 1) * 128],
                                                         in_=stg[0][:, hp * 128:(hp + 1) * 128], identity=identf[:]),
                             reads=[d_stg[0], d_const], writes=[dP[b]])
                    S.op("dve", lambda e: e.tensor_copy(out=SKT[:].rearrange("p (b h) k -> p b (h k)", b=4), in_=PS[:, 1:5, :]),
                         reads=[dP[1], dP[2], dP[3], dP[4]], writes=[d_WC])
                S.barrier()

            x_t = [SB(sb_, "xB%d" % i, [128, D], F32) for i in range(2)]
            d_x = [Dep(), Dep()]
            st1 = SB(sb_, "st1B", [128, 8], F32); d_st1 = Dep()
            xn2 = [SB(sb_, "xnB%d" % i, [128, D], BF16) for i in range(2)]; d_xn2b = [Dep(), Dep()]
            xnT2 = [SB(sb_, "xnTB%d" % i, [128, NCH, T], BF16) for i in range(2)]; d_xnT2b = [Dep(), Dep()]
            fmoB = [SB(sb_, "fmoB%d" % i, [128, 2, NCH, T], BF16) for i in range(2)]
            d_fmoB = [Dep(), Dep()]
            sga = SB(sb_, "sga", [128, D], F32); d_sga = Dep()
            sgb = SB(sb_, "sgb", [128, D], F32); d_sgb = Dep()
            m1 = SB(sb_, "m1", [128, D], F32); d_m1 = Dep()
            m2 = sga; d_m2 = d_sga
            mtm = SB(sb_, "mtm", [128, D], BF16); d_mtm = Dep()
            mT = SB(sb_, "mT", [128, NCH, T], BF16); d_mT = Dep()

            def b_head(ti):
                par = ti % 2
                xn, d_xn, xnT, d_xnT = xn2[par], d_xn2b[par], xnT2[par], d_xnT2b[par]
                src = xp[(NPRE + ti) * T:(NPRE + ti + 1) * T, :] if ti < NMAIN else xs
                S.dma("sp", lambda e: e.dma_start(out=x_t[par][:], in_=src), writes=[d_x[par]])
                S.dma("act", lambda e: e.dma_start(out=fmoB[par][:].rearrange("p a c t -> p (a c t)"), in_=scr[ti]),
                      writes=[d_fmoB[par]])
                xtile, d_xtile = x_t[par][:], d_x[par]
                S.op("act", lambda e: e.activation(out=xn[:], in_=xtile, func=AF.Square, accum_out=st1[:, 0:1]),
                     reads=[d_xtile], writes=[d_xn, d_st1])
                S.op("dve", lambda e: e.tensor_scalar(out=st1[:, 1:2], in0=st1[:, 0:1], scalar1=1.0 / D, scalar2=EPS, op0=ALU.mult, op1=ALU.add),
                     reads=[d_st1], writes=[d_st1])
                S.op("pool", lambda e: e.tensor_tensor(out=st1[:, 2:3], in0=st1[:, 1:2], in1=mhalf[:, 0:1], op=ALU.pow),
                     reads=[d_st1, d_const], writes=[d_st1])
                S.op("act", lambda e: e.activation(out=xn[:], in_=xtile, func=AF.Copy, scale=st1[:, 2:3]),
                     reads=[d_xtile, d_st1], writes=[d_xn])
                for c in range(NCH):
                    S.op("pe", lambda e: e.transpose(out=pbank16(0)[:, c * 128:(c + 1) * 128], in_=xn[:, c * 128:(c + 1) * 128],
                                                     identity=identb[:]), reads=[d_xn, d_const], writes=[dP[0]])
                S.op("dve", lambda e: e.tensor_copy(out=xnT[:].rearrange("p c t -> p (c t)"), in_=pbank16(0)),
                     reads=[dP[0]], writes=[d_xnT])

            def b_part1(ti):
                par = ti % 2
                xnT, d_xnT = xnT2[par], d_xnT2b[par]
                for gi, (sg, dsg) in enumerate(((sga, d_sga), (sgb, d_sgb))):
                    for j in range(2):
                        b = 1 + gi * 2 + j
                        for k in range(NCH):
                            S.op("pe", lambda e: e.matmul(PS[:, b, :], lhsT=xnT[:, k, :],
                                                          rhs=WgB[:, k, gi * D + j * 512: gi * D + (j + 1) * 512],
                                                          start=(k == 0), stop=(k == NCH - 1)),
                                 reads=[d_WB, d_xnT], writes=[dP[b]])
                    b0 = 1 + gi * 2
                    S.op("act", lambda e: e.activation(out=sg[:].rearrange("p (b x) -> p b x", b=2), in_=PS[:, b0:b0 + 2, :], func=AF.Sigmoid),
                         reads=[dP[b0], dP[b0 + 1]], writes=[dsg])
                for bi, (W_, banks, sg, dsg, mm, dmm) in enumerate(((Wl, (5, 6), sga, d_sga, m1, d_m1), (Wa, (7, 0), sgb, d_sgb, m2, d_m2))):
                    for j in range(2):
                        b = banks[j]
                        for k in range(NCH):
                            S.op("pe", lambda e: e.matmul(PS[:, b, :], lhsT=fmoB[par][:, bi, k, :], rhs=W_[:, k, j * 512:(j + 1) * 512],
                                                          start=(k == 0), stop=(k == NCH - 1)),
                                 reads=[d_WB, d_fmoB[par]], writes=[dP[b]])
                        S.op("dve", lambda e: e.tensor_tensor(out=mm[:, j * 512:(j + 1) * 512], in0=PS[:, b, :], in1=sg[:, j * 512:(j + 1) * 512],
                                                              op=ALU.mult), reads=[dP[b], dsg], writes=[dmm])
                S.op("dve", lambda e: e.tensor_tensor(out=mtm[:], in0=m1[:], in1=m2[:], op=ALU.add),
                     reads=[d_m1, d_m2], writes=[d_mtm])

            def b_part2(ti):
                par = ti % 2
                for c in range(NCH):
                    S.op("pe", lambda e: e.transpose(out=pbank16(1)[:, c * 128:(c + 1) * 128], in_=mtm[:, c * 128:(c + 1) * 128],
                                                     identity=identb[:]), reads=[d_mtm, d_const], writes=[dP[1]])
                S.op("act", lambda e: e.activation(out=mT[:].rearrange("p c t -> p (c t)"), in_=pbank16(1), func=AF.Copy),
                     reads=[dP[1]], writes=[d_mT])
                for j in range(2):
                    for k in range(NCH):
                        S.op("pe", lambda e: e.matmul(PS[:, 2 + j, :], lhsT=mT[:, k, :], rhs=Wo[:, k, j * 512:(j + 1) * 512],
                                                      start=(k == 0), stop=(k == NCH - 1)),
                             reads=[d_WB, d_mT], writes=[dP[2 + j]])
                S.op("dve", lambda e: e.tensor_tensor(out=x_t[par][:].rearrange("p (b x) -> p b x", b=2),
                                                      in0=PS[:, 2:4, :], in1=x_t[par][:].rearrange("p (b x) -> p b x", b=2),
                                                      op=ALU.add),
                     reads=[dP[2], dP[3], d_x[par]], writes=[d_x[par]])
                S.dma("pool", lambda e: e.dma_start(out=hres_scr[ti], in_=x_t[par][:]), reads=[d_x[par]], writes=[d_hres[ti]])
                if STAGE == 2:
                    dst = y_p[ti * T:(ti + 1) * T, :] if ti < NMAIN else y_s
                    S.dma("sp", lambda e: e.dma_start(out=dst, in_=x_t[par][:]), reads=[d_x[par]], is_out=True)

            b_head(0)
            for ti in range(NT):
                b_part1(ti)
                if ti + 1 < NT:
                    b_head(ti + 1)
                b_part2(ti)
            S.barrier()

        PG = 4
        if STAGE >= 3:
          if True:
            with ExitStack() as sc_:
                hr_c = [SB(sc_, "hr_c%d" % i, [128, D], F32) for i in range(2)]
                d_hrc = [Dep(), Dep()]
                junk = SB(sc_, "junkC", [128, D], F32); d_junk = Dep()
                st1 = SB(sc_, "st1C", [128, 8], F32); d_st1 = Dep()
                xn2 = SB(sc_, "xn2", [128, D], BF16); d_xn2 = Dep()
                qrT = SB(sc_, "qrT", [128, 16, T], BF16); d_qrT = Dep()
                scs2 = [SB(sc_, "scs%d" % i, [128, 16, 128], F32) for i in range(2)]; d_scs2 = [Dep(), Dep()]
                d_tvr = [Dep() for _ in range(16)]; d_tiur = [Dep() for _ in range(16)]; d_wrkr = [Dep() for _ in range(16)]
                d_bestr = [Dep() for _ in range(8)]; d_posur = [Dep() for _ in range(8)]; d_cwkr = [Dep() for _ in range(8)]
                wrk = SB(sc_, "wrk", [128, 16, 128], F32); d_wrk = Dep()
                tv = SB(sc_, "tv", [128, 16, 16], F32); d_tv = Dep()
                tiu = SB(sc_, "tiu", [128, 16, 16], U32); d_tiu = Dep()
                tif = SB(sc_, "tif", [128, 16, 16], F32); d_tif = Dep()
                cand = SB(sc_, "cand", [128, 8, 256], F32); d_cand = Dep()
                cwk = SB(sc_, "cwk", [128, 8, 256], F32); d_cwk = Dep()
                best = SB(sc_, "best", [128, 8, 16], F32); d_best = Dep()
                posu = SB(sc_, "posu", [128, 8, 16], U32); d_posu = Dep()
                abu = SB(sc_, "abu", [128, 2, 128], U32); d_abu = Dep()
                abf = SB(sc_, "abf", [128, 2, 128], F32); d_abf = Dep()
                oh = [SB(sc_, "oh%d" % i, [128, 128, 16], F32) for i in range(2)]
                d_oh = [Dep(), Dep()]
                sel = SB(sc_, "sel", [128, 3, 128], F32); d_sel = Dep()
                gsm = SB(sc_, "gsm", [128, 8], F32); d_gsm = Dep()
                iota16 = SB(sc_, "iota16", [128, 16], F32)
                S.op("pool", lambda e: e.iota(iota16[:], pattern=[[1, 16]], base=0, channel_multiplier=0,
                                              allow_small_or_imprecise_dtypes=True), writes=[d_WC])
                gat = sel[:, 2, :].rearrange("p (h k) -> p h k", k=16)

                def c1a_head(ti):
                    par = ti % 2
                    tok0 = ti * T
                    S.dma("sp", lambda e: e.dma_start(out=hr_c[par][:], in_=hres_scr[ti]), reads=[d_hres[ti]], writes=[d_hrc[par]])
                    hres = hr_c[par][:]
                    S.op("act", lambda e: e.activation(out=junk[:], in_=hres, func=AF.Square, accum_out=st1[:, 0:1]),
                         reads=[d_hrc[par]], writes=[d_junk, d_st1])
                    S.op("dve", lambda e: e.tensor_scalar(out=st1[:, 1:2], in0=st1[:, 0:1], scalar1=1.0 / D, scalar2=EPS, op0=ALU.mult, op1=ALU.add),
                         reads=[d_st1], writes=[d_st1])
                    S.op("pool", lambda e: e.tensor_tensor(out=st1[:, 2:3], in0=st1[:, 1:2], in1=mhalf[:, 0:1], op=ALU.pow),
                         reads=[d_st1, d_const], writes=[d_st1])
                    S.op("act", lambda e: e.activation(out=xn2[:], in_=hres, func=AF.Copy, scale=st1[:, 2:3]),
                         reads=[d_hrc[par], d_st1], writes=[d_xn2])
                    for c in range(NCH):
                        S.op("pe", lambda e: e.transpose(out=pbank16(0)[:, c * 128:(c + 1) * 128], in_=xn2[:, c * 128:(c + 1) * 128],
                                                         identity=identb[:]), reads=[d_xn2, d_const], writes=[dP[0]])
                    S.op("act", lambda e: e.activation(out=xn2T_all[:, :, tok0:tok0 + T],
                                                       in_=pbank16(0).rearrange("p (c t) -> p c t", c=NCH), func=AF.Copy),
                         reads=[dP[0]], writes=[d_xn2T_all[ti]])
                    for hp in range(16):
                        b = 1 + hp // 4
                        for k in range(NCH):
                            S.op("pe", lambda e: e.matmul(PS[:, b, (hp % 4) * 128:(hp % 4 + 1) * 128],
                                                          lhsT=Wq[:, k, hp * 128:(hp + 1) * 128], rhs=xn2T_all[:, k, tok0:tok0 + T],
                                                          start=(k == 0), stop=(k == NCH - 1)),
                                 reads=[d_WC, d_xn2T_all[ti]], writes=[dP[b]])
                    S.op("act", lambda e: e.activation(out=qrT[:, 0:8, :].rearrange("p (b h) t -> p b (h t)", b=2), in_=PS[:, 1:3, :],
                                                       func=AF.Copy), reads=[dP[1], dP[2]], writes=[d_qrT])
                    S.op("act", lambda e: e.activation(out=qrT[:, 8:16, :].rearrange("p (b h) t -> p b (h t)", b=2), in_=PS[:, 3:5, :],
                                                       func=AF.Copy), reads=[dP[3], dP[4]], writes=[d_qrT])
                    sbanks = (5, 6, 7, 0)
                    for hp in range(16):
                        b = sbanks[hp // 4]
                        S.op("pe", lambda e: e.matmul(PS[:, b, (hp % 4) * 128:(hp % 4 + 1) * 128], lhsT=qrT[:, hp, :], rhs=SKT[:, hp, :],
                                                      start=True, stop=True), reads=[d_qrT, d_WC], writes=[dP[b]])
                    scs = scs2[par]
                    d_scs = d_scs2[par]
                    for q4 in range(4):
                        b = sbanks[q4]
                        S.op("act", lambda e: e.activation(out=scs[:, q4 * 4:(q4 + 1) * 4, :].rearrange("p h k -> p (h k)"), in_=PS[:, b, :], func=AF.Copy),
                             reads=[dP[b]], writes=[d_scs])

                def c1a_body(ti):
                    par = ti % 2
                    tok0 = ti * T
                    scs = scs2[par]
                    d_scs = d_scs2[par]
                    for hp in range(16):
                        S.op("dve", lambda e: e.max(out=tv[:, hp, 0:8], in_=scs[:, hp, :]), reads=[d_scs], writes=[d_tvr[hp]])
                    for hp in range(16):
                        S.op("dve", lambda e: e.max_index(out=tiu[:, hp, 0:8], in_max=tv[:, hp, 0:8], in_values=scs[:, hp, :]),
                             reads=[d_scs, d_tvr[hp]], writes=[d_tiur[hp]])
                    for hp in range(16):
                        S.op("dve", lambda e: e.match_replace(out=wrk[:, hp, :], in_to_replace=tv[:, hp, 0:8], in_values=scs[:, hp, :],
                                                              imm_value=-1e30), reads=[d_scs, d_tvr[hp]], writes=[d_wrkr[hp]])
                    for hp in range(16):
                        S.op("dve", lambda e: e.max(out=tv[:, hp, 8:16], in_=wrk[:, hp, :]), reads=[d_wrkr[hp]], writes=[d_tvr[hp]])
                    for hp in range(16):
                        S.op("dve", lambda e: e.max_index(out=tiu[:, hp, 8:16], in_max=tv[:, hp, 8:16], in_values=wrk[:, hp, :]),
                             reads=[d_wrkr[hp], d_tvr[hp]], writes=[d_tiur[hp]])
                    S.op("dve", lambda e: e.tensor_copy(out=tif[:], in_=tiu[:]), reads=d_tiur, writes=[d_tif])
                    tvv = tv[:].rearrange("p (h two) k -> p h two k", two=2)
                    S.op("dve", lambda e: e.tensor_tensor(out=cand[:].rearrange("p h (a b) -> p h a b", b=16),
                                                          in0=tvv[:, :, 0, :].unsqueeze(3).to_broadcast([128, 8, 16, 16]),
                                                          in1=tvv[:, :, 1, :].unsqueeze(2).to_broadcast([128, 8, 16, 16]), op=ALU.add),
                         reads=d_tvr, writes=[d_cand])
                    for h in range(8):
                        S.op("dve", lambda e: e.max(out=best[:, h, 0:8], in_=cand[:, h, :]), reads=[d_cand], writes=[d_bestr[h]])
                    for h in range(8):
                        S.op("dve", lambda e: e.max_index(out=posu[:, h, 0:8], in_max=best[:, h, 0:8], in_values=cand[:, h, :]),
                             reads=[d_cand, d_bestr[h]], writes=[d_posur[h]])
                    for h in range(8):
                        S.op("dve", lambda e: e.match_replace(out=cwk[:, h, :], in_to_replace=best[:, h, 0:8], in_values=cand[:, h, :],
                                                              imm_value=-1e30), reads=[d_cand, d_bestr[h]], writes=[d_cwkr[h]])
                    for h in range(8):
                        S.op("dve", lambda e: e.max(out=best[:, h, 8:16], in_=cwk[:, h, :]), reads=[d_cwkr[h]], writes=[d_bestr[h]])
                    for h in range(8):
                        S.op("dve", lambda e: e.max_index(out=posu[:, h, 8:16], in_max=best[:, h, 8:16], in_values=cwk[:, h, :]),
                             reads=[d_cwkr[h], d_bestr[h]], writes=[d_posur[h]])
                    d_posu_l = d_posur
                    d_best_l = d_bestr
                    pflat = posu[:].rearrange("p h k -> p (h k)")
                    S.op("dve", lambda e: e.tensor_single_scalar(out=abu[:, 0, :], in_=pflat, scalar=4, op=ALU.logical_shift_right),
                         reads=d_posu_l, writes=[d_abu])
                    S.op("dve", lambda e: e.tensor_single_scalar(out=abu[:, 1, :], in_=pflat, scalar=15, op=ALU.bitwise_and),
                         reads=d_posu_l, writes=[d_abu])
                    S.op("dve", lambda e: e.tensor_copy(out=abf[:], in_=abu[:]), reads=[d_abu], writes=[d_abf])
                    tfv = tif[:].rearrange("p (h two) k -> p h two k", two=2)
                    for w in range(2):
                        S.op("dve",
                             lambda e: e.tensor_tensor(out=oh[w][:], in0=iota16[:].unsqueeze(1).to_broadcast([128, 128, 16]),
                                                       in1=abf[:, w, :].unsqueeze(2).to_broadcast([128, 128, 16]), op=ALU.is_equal),
                             reads=[d_abf, d_WC], writes=[d_oh[w]])
                    for w in range(2):
                        ohv = oh[w][:].rearrange("p (h j) a -> p h j a", j=16)
                        S.op("dve",
                             lambda e: e.tensor_tensor(out=ohv, in0=ohv,
                                                       in1=tfv[:, :, w, :].unsqueeze(2).to_broadcast([128, 8, 16, 16]), op=ALU.mult),
                             reads=[d_oh[w], d_tif], writes=[d_oh[w]])
                    S.op("dve", lambda e: e.tensor_tensor(out=gat, in0=best[:], in1=best[:, :, 0:1].to_broadcast([128, 8, 16]),
                                                          op=ALU.subtract), reads=d_best_l, writes=[d_sel])
                    S.op("act", lambda e: e.activation(out=gat, in_=gat, func=AF.Exp), reads=[d_sel], writes=[d_sel])
                    S.op("dve", lambda e: e.tensor_reduce(out=gsm[:], in_=gat, axis=AX.X, op=ALU.add), reads=[d_sel], writes=[d_gsm])
                    S.op("dve", lambda e: e.reciprocal(out=gsm[:], in_=gsm[:]), reads=[d_gsm], writes=[d_gsm])
                    S.op("dve", lambda e: e.tensor_tensor(out=gat, in0=gat, in1=gsm[:].unsqueeze(2).to_broadcast([128, 8, 16]),
                                                          op=ALU.mult), reads=[d_sel, d_gsm], writes=[d_sel])
                    for w in range(2):
                        S.op("dve", lambda e: e.tensor_reduce(out=sel[:, w, :], in_=oh[w][:], axis=AX.X, op=ALU.add),
                             reads=[d_oh[w]], writes=[d_sel])
                    for k in range(3):
                        S.op("pe", lambda e: e.transpose(out=PS[:, 7, k * 128:(k + 1) * 128], in_=sel[:, k, :], identity=identf[:]),
                             reads=[d_sel, d_const], writes=[dP[7]])
                    S.op("act", lambda e: e.activation(out=slotT[:, :, tok0:tok0 + T], in_=PS[:, 7, 0:384].rearrange("p (k t) -> p k t", k=3),
                                                       func=AF.Copy), reads=[dP[7]], writes=[d_slotT[ti]])

                c1a_head(0)
                for ti in range(NT):
                    if ti + 1 < NT:
                        c1a_head(ti + 1)
                    c1a_body(ti)
                S.barrier()

            swq.close()
            with ExitStack() as sd_:
                iotaC = SB(sd_, "iotaC", [128, 128], BF16)
                d_io = Dep()
                S.op("pool", lambda e: e.iota(iotaC[:], pattern=[[1, 128]], base=0, channel_multiplier=0,
                                              allow_small_or_imprecise_dtypes=True), writes=[d_io])
                HT = T // 2
                OH1 = [SB(sd_, "OH1_%d" % i, [128, 128, HT], BF16) for i in range(2)]; d_OH1 = [Dep(), Dep()]
                OH2 = [SB(sd_, "OH2_%d" % i, [128, 128, HT], BF16) for i in range(2)]; d_OH2 = [Dep(), Dep()]
                iotaR = SB(sd_, "iotaR", [128, 128, HT], BF16)
                S.op("dve", lambda e: e.tensor_copy(out=iotaR[:], in_=iotaC[:].unsqueeze(2).to_broadcast([128, 128, HT])),
                     reads=[d_io], writes=[d_io])
                Gt = [SB(sd_, "Gt%d" % i, [128, 128, T], BF16) for i in range(2)]
                d_Gt = [Dep(), Dep()]
                ecnt = [0]

                def c1b_build(hi):
                    ti, hf = hi // 2, hi % 2
                    hb_ = hi % 2
                    th0 = ti * T + hf * HT
                    S.op("dve", lambda e: e.tensor_tensor(out=OH1[hb_][:], in0=iotaR[:],
                                                          in1=slotT[:, 0, th0:th0 + HT].unsqueeze(1).to_broadcast([128, 128, HT]),
                                                          op=ALU.is_equal), reads=[d_io, d_slotT[ti]], writes=[d_OH1[hb_]])
                    S.op("dve", lambda e: e.tensor_tensor(out=OH2[hb_][:], in0=iotaR[:],
                                                          in1=slotT[:, 1, th0:th0 + HT].unsqueeze(1).to_broadcast([128, 128, HT]),
                                                          op=ALU.is_equal), reads=[d_io, d_slotT[ti]], writes=[d_OH2[hb_]])
                    S.op("dve", lambda e: e.tensor_tensor(out=OH2[hb_][:], in0=OH2[hb_][:],
                                                          in1=slotT[:, 2, th0:th0 + HT].unsqueeze(1).to_broadcast([128, 128, HT]),
                                                          op=ALU.mult), reads=[d_OH2[hb_], d_slotT[ti]], writes=[d_OH2[hb_]])

                def c1b_mm(hi):
                    ti, hf = hi // 2, hi % 2
                    par = ti % 2
                    hb_ = hi % 2
                    tok0 = ti * T
                    for t4 in range(HT // 4):
                        b = ecnt[0] % 4
                        ecnt[0] += 1
                        for tt in range(4):
                            t = t4 * 4 + tt
                            if PE_STRIDED:
                                o_ap = PS[:, b, :].rearrange("c (p t) -> c t p", t=4)[:, tt, :]
                            else:
                                o_ap = PS[:, b, tt * 128:(tt + 1) * 128]
                            S.op("pe", lambda e: e.matmul(o_ap, lhsT=OH1[hb_][:, :, t], rhs=OH2[hb_][:, :, t],
                                                          start=True, stop=True), reads=[d_OH1[hb_], d_OH2[hb_]], writes=[dP[b]])
                        tg = hf * HT + t4 * 4
                        if PE_STRIDED:
                            dst = Gt[par][:, :, tg:tg + 4]
                            src = PS[:, b, :].rearrange("c (p t) -> c p t", t=4)
                        else:
                            dst = Gt[par][:, :, tg:tg + 4].rearrange("c p t -> c t p")
                            src = PS[:, b, :].rearrange("c (t p) -> c t p", t=4)
                        S.op("act", lambda e: e.activation(out=dst, in_=src, func=AF.Copy), reads=[dP[b]], writes=[d_Gt[par]])
                    if hf == 1:
                        for q in range(4):
                            S.dma("sp" if q % 2 == 0 else "pool",
                                  lambda e: e.dma_start(out=G_scr[q * 32:(q + 1) * 32, :, tok0:tok0 + T].rearrange("p c t -> c p t"),
                                                        in_=Gt[par][:, q * 32:(q + 1) * 32, :]),
                                  reads=[d_Gt[par]])

                c1b_build(0)
                for hi in range(2 * NT):
                    if hi + 1 < 2 * NT:
                        c1b_build(hi + 1)
                    c1b_mm(hi)
                S.barrier()

            with ExitStack() as se_:
                NGRP = 128 // PG
                out_acc = SB(se_, "out_acc", [128, NT, D], F32); d_acc = [Dep() for _ in range(NT)]
                su = [SB(se_, "su%d" % i, [128, D], F32) for i in range(2)] * 2; d_su = [Dep(), Dep()] * 2
                sv = [SB(se_, "sv%d" % i, [128, D], F32) for i in range(2)]; d_sv = [Dep(), Dep()]
                ub = [SB(se_, "ub%d" % i, [128, D], BF16) for i in range(2)] * 2; d_ub = [Dep(), Dep()] * 2
                UT = [SB(se_, "UT%d" % i, [128, PG, NCH, 128], BF16) for i in range(2)]; d_UT = [Dep(), Dep()]
                Vb = [SB(se_, "Vb%d" % i, [128, PG, D], BF16) for i in range(2)]; d_Vb = [Dep(), Dep()]
                Gp = [slotT[:, i, :] for i in range(3)]; d_Gp = [Dep() for _ in range(3)]
                Ab = [SB(se_, "Ab%d" % i, [128, 512], BF16) for i in range(2)]; d_Ab = [Dep(), Dep()]
                A2 = [SB(se_, "A_%d" % i, [128, PG, NTOK], BF16) for i in range(2)]
                d_A2 = [[Dep() for _ in range(PG)] for _ in range(2)]
                eu_v = eu_d.rearrange("(c p) d -> p c d", p=128)
                ev_v = ev_d.rearrange("(c p) d -> p c d", p=128)
                chunks = [(t0, min(512, NTOK - t0)) for t0 in range(0, NTOK, 512)]
                pcount = [0]

                def run_gens2(gens):
                    gens = list(gens)
                    while gens:
                        for g_ in list(gens):
                            try:
                                next(g_)
                            except StopIteration:
                                gens.remove(g_)

                def prepU(g):
                    gb = g % 2
                    for half_ in range(PG // 2):
                        for k in range(2):
                            pi = half_ * 2 + k
                            p = g * PG + pi
                            S.dma("sp", lambda e: e.dma_start(out=su[k][:], in_=eu_v[p]), writes=[d_su[k]])
                            S.op("act", lambda e: e.activation(out=ub[k][:], in_=su[k][:], func=AF.Copy), reads=[d_su[k]], writes=[d_ub[k]])
                        yield
                        yield
                        yield
                        for k in range(2):
                            pi = half_ * 2 + k
                            for dc in range(NCH):
                                S.op("pe", lambda e: e.transpose(out=pbank16(0)[:, dc * 128:(dc + 1) * 128], in_=ub[k][:, dc * 128:(dc + 1) * 128],
                                                                 identity=identb[:]), reads=[d_ub[k], d_const], writes=[dP[0]])
                            S.op("dve", lambda e: e.tensor_copy(out=UT[gb][:, pi, :, :].rearrange("p c x -> p (c x)"), in_=pbank16(0)),
                                 reads=[dP[0]], writes=[d_UT[gb]])
                            yield

                def prepV(g):
                    gb = g % 2
                    for pi in range(PG):
                        p = g * PG + pi
                        k = vcount[0] % 2
                        vcount[0] += 1
                        S.dma("act", lambda e: e.dma_start(out=sv[k][:], in_=ev_v[p]), writes=[d_sv[k]])
                        S.op("act", lambda e: e.activation(out=Vb[gb][:, pi, :], in_=sv[k][:], func=AF.Copy), reads=[d_sv[k]], writes=[d_Vb[gb]])

                gcnt = [0]
                hcnt = [0]

                def H_gen(g):
                    gb = g % 2
                    A_ = A2[gb]
                    for pi in range(PG):
                        p = g * PG + pi
                        gk = gcnt[0] % 3
                        gcnt[0] += 1
                        S.dma("pool", lambda e: e.dma_start(out=Gp[gk], in_=G_scr[p]), writes=[d_Gp[gk]])
                        for (t0, n) in chunks:
                            hb = 1 + hcnt[0] % 2
                            ak = hcnt[0] % 2
                            hcnt[0] += 1
                            for dc in range(NCH):
                                S.op("pe", lambda e: e.matmul(PS[:, hb, 0:n], lhsT=UT[gb][:, pi, dc, :], rhs=xn2T_all[:, dc, t0:t0 + n],
                                                              start=(dc == 0), stop=(dc == NCH - 1)),
                                     reads=[d_UT[gb]], writes=[dP[hb]])
                            S.op("act", lambda e: e.activation(out=Ab[ak][:, 0:n], in_=PS[:, hb, 0:n], func=AF.Gelu_apprx_tanh),
                                 reads=[dP[hb]], writes=[d_Ab[ak]])
                            S.op("dve", lambda e: e.tensor_tensor(out=A_[:, pi, t0:t0 + n], in0=Ab[ak][:, 0:n], in1=Gp[gk][:, t0:t0 + n],
                                                                  op=ALU.mult), reads=[d_Ab[ak], d_Gp[gk]], writes=[d_A2[gb][pi]])
                            yield

                def V_gen(g):
                    gb = g % 2
                    A_ = A2[gb]
                    for s_ in range(NT):
                        ab_ = 3 + 2 * (s_ % 2)
                        for j in range(2):
                            for pi in range(PG):
                                S.op("pe", lambda e: e.matmul(PS[:, ab_ + j, :], lhsT=A_[:, pi, s_ * T:(s_ + 1) * T],
                                                              rhs=Vb[gb][:, pi, j * 512:(j + 1) * 512],
                                                              start=(pi == 0), stop=(pi == PG - 1)),
                                     reads=[d_A2[gb][pi], d_Vb[gb]], writes=[dP[ab_ + j]])
                        S.op("dve", lambda e: e.tensor_tensor(out=out_acc[:, s_, :].rearrange("p (b x) -> p b x", b=2),
                                                              in0=PS[:, ab_:ab_ + 2, :],
                                                              in1=out_acc[:, s_, :].rearrange("p (b x) -> p b x", b=2), op=ALU.add),
                             reads=[dP[ab_], dP[ab_ + 1], d_acc[s_]], writes=[d_acc[s_]])
                        yield

                vcount = [0]
                run_gens2([prepU(0)])
                prepV(0)
                run_gens2([prepU(1)])
                prepV(1)
                for ti in range(NT):
                    S.dma("sp" if ti % 2 == 0 else "act", lambda e: e.dma_start(out=out_acc[:, ti, :], in_=hres_scr[ti]),
                          reads=[d_hres[ti]], writes=[d_acc[ti]])
                run_gens2([H_gen(0)])
                for g in range(NGRP):
                    gens = [V_gen(g)]
                    if g + 1 < NGRP:
                        gens.append(H_gen(g + 1))
                    if g + 2 < NGRP:
                        gens.append(prepU(g + 2))
                    run_gens2(gens)
                    if g + 2 < NGRP:
                        prepV(g + 2)
                for ti in range(NT):
                    dst = y_p[ti * T:(ti + 1) * T, :] if ti < NMAIN else y_s
                    S.dma("sp" if ti % 2 == 0 else "act", lambda e: e.dma_start(out=dst, in_=out_acc[:, ti, :]),
                          reads=[d_acc[ti]], is_out=True)
                S.barrier()

        if STAGE < 2:
            with ExitStack() as sz:
                z = SB(sz, "z", [128, D], F32)
                dz = Dep()
                S.op("pool", lambda e: e.memset(z[:], 0.0), writes=[dz])
                for mi in range(NMAIN):
                    S.dma("sp", lambda e: e.dma_start(out=y_p[mi * T:(mi + 1) * T, :], in_=z[:]), reads=[dz], is_out=True)
                S.dma("sp", lambda e: e.dma_start(out=y_s, in_=z[:]), reads=[dz], is_out=True)
                S.barrier()

        S.finish()
    print("instructions:", S.ninst, {k: S.cnt[k] for k in S.cnt}, S.dcnt)
    return nc


_CACHE = {}


def _tables():
    if "t" in _CACHE:
        return _CACHE["t"]
    slopes = 2.0 ** (-8.0 * np.arange(1, NH + 1, dtype=np.float64) / NH)
    ki = np.arange(128)[:, None]
    qi = np.arange(128)[None, :]
    E_pr = np.zeros((128, NH, 2, 128), np.float64)
    for h in range(NH):
        d_prev = qi - ki + 128
        E_pr[:, h, 0, :] = np.where(ki >= qi, np.exp(-slopes[h] * d_prev), 0.0)
        d_own = qi - ki
        E_pr[:, h, 1, :] = np.where(ki <= qi, np.exp(-slopes[h] * d_own), 0.0)
    ks_, kt_ = ki // DEC, ki % DEC
    qs_, qt_ = qi // DEC, qi % DEC
    E_so = np.zeros((128, NH, 128), np.float64)
    for h in range(NH):
        E_so[:, h, :] = np.where((ks_ == qs_) & (kt_ <= qt_), np.exp(-slopes[h] * (qt_ - kt_)), 0.0)
    E_sc = np.zeros((128, NH, DEC), np.float64)
    j = np.arange(128)[:, None]
    t = np.arange(DEC)[None, :]
    for h in range(NH):
        E_sc[:, h, :] = np.where(j >= t, np.exp(-slopes[h] * (t + 128 - j)), 0.0)
    tb = dict(E_pr=E_pr.reshape(128, -1).astype(np.float32), E_so=E_so.reshape(128, -1).astype(np.float32),
              E_sc=E_sc.reshape(128, -1).astype(np.float32), identf=np.eye(128, dtype=np.float32))
    _CACHE["t"] = tb
    return tb


def kernel(x_prompt, x_sample, cache_conv, state_lru, cache_k, cache_v, norm1_g, w_in, conv_w, conv_b,
           rg_w_a, rg_b_a, rg_w_x, rg_b_x, rg_lambda, q_norm_g, k_norm_g, attn_sinks, w_branch_lru,
           w_branch_attn, w_out, norm2_g, peer_w_query, peer_sub_keys, expert_u, expert_v):
    f = lambda a: np.ascontiguousarray(np.asarray(a, dtype=np.float32))
    x_prompt, x_sample = f(x_prompt), f(x_sample)
    tb = _tables()
    prow = np.zeros((16, D), np.float32)
    prow[0:4] = f(conv_w)[0]
    prow[4] = f(conv_b)[0]
    prow[5] = f(rg_b_a)[0]
    prow[6] = f(rg_b_x)[0]
    prow[7] = f(rg_lambda)[0]
    prow[8] = f(norm1_g)[0]
    prow[9] = f(norm2_g)[0]
    shared = dict(
        identf=tb["identf"], E_pr=tb["E_pr"], E_so=tb["E_so"], E_sc=tb["E_sc"], prow=prow,
        w_in=f(w_in)[0], rgw=np.stack([f(rg_w_a)[0], f(rg_w_x)[0]]),
        qkg=np.stack([f(q_norm_g)[0], f(k_norm_g)[0]]), sinks=f(attn_sinks)[0].reshape(1, NH),
        w_bl=f(w_branch_lru)[0], w_ba=f(w_branch_attn)[0], w_o=f(w_out)[0], w_q=f(peer_w_query)[0],
        subk=f(peer_sub_keys)[0].reshape(16, 128, 128),
    )
    if STAGE >= 3 and not DEBUG_C:
        shared.update(eu=f(expert_u)[0], ev=f(expert_v)[0])
    cc, sl, ckk, cvv = f(cache_conv)[0], f(state_lru)[0], f(cache_k)[0], f(cache_v)[0]
    in_maps = []
    for c in range(NCORE):
        s, half = c // 2, c % 2
        xpc = np.zeros(((NPRE + NMAIN) * T, D), np.float32)
        if half == 1:
            xpc[:NPRE * T] = x_prompt[s, :2048]
        xpc[NPRE * T:] = x_prompt[s, half * 2048:(half + 1) * 2048]
        m = dict(shared)
        m.update(
            xp=xpc, xs=x_sample[c * NSEQ:(c + 1) * NSEQ].reshape(T, D),
            cconv=cc[c * NSEQ:(c + 1) * NSEQ].reshape(NSEQ * 3, D), slru=sl[c * NSEQ:(c + 1) * NSEQ],
            ck=ckk[c * NSEQ:(c + 1) * NSEQ].reshape(NSEQ, 128, 256), cv=cvv[c * NSEQ:(c + 1) * NSEQ].reshape(NSEQ, 128, 256),
            flag=np.full((128, 1), float(half), np.float32),
        )
        in_maps.append(m)
    if "nc" not in _CACHE:
        _CACHE["nc"] = build_program()
    res = run_bass_kernel_spmd(_CACHE["nc"], in_maps, core_ids=list(range(NCORE)))
    R = res.results
    y_prompt = np.zeros((4, 4096, D), np.float32)
    y_sample = np.zeros((128, DEC, D), np.float32)
    conv_pr = np.zeros((1, 4, 3, D), np.float32)
    lru_pr = np.zeros((1, 4, D), np.float32)
    k_pr = np.zeros((1, 4, 128, NKV, HD), np.float32)
    v_pr = np.zeros((1, 4, 128, NKV, HD), np.float32)
    conv_sa = np.zeros((1, 128, 3, D), np.float32)
    lru_sa = np.zeros((1, 128, D), np.float32)
    k_sa = np.zeros((1, 128, 128, NKV, HD), np.float32)
    v_sa = np.zeros((1, 128, 128, NKV, HD), np.float32)
    for c in range(NCORE):
        s, half = c // 2, c % 2
        r = R[c]
        y_prompt[s, half * 2048:(half + 1) * 2048] = r["y_p"]
        y_sample[c * NSEQ:(c + 1) * NSEQ] = r["y_s"].reshape(NSEQ, DEC, D)
        if half == 1:
            conv_pr[0, s] = r["conv_p"]
            lru_pr[0, s] = r["lru_p"][0]
            k_pr[0, s] = r["k_p"].reshape(128, NKV, HD)
            v_pr[0, s] = r["v_p"].reshape(128, NKV, HD)
        conv_sa[0, c * NSEQ:(c + 1) * NSEQ] = r["conv_s"].reshape(NSEQ, 3, D)
        lru_sa[0, c * NSEQ:(c + 1) * NSEQ] = r["lru_s"]
        k_sa[0, c * NSEQ:(c + 1) * NSEQ] = r["k_s"].reshape(NSEQ, 128, NKV, HD)
        v_sa[0, c * NSEQ:(c + 1) * NSEQ] = r["v_s"].reshape(NSEQ, 128, NKV, HD)
    return (y_prompt, y_sample, conv_pr, lru_pr, k_pr, v_pr, conv_sa, lru_sa, k_sa, v_sa)
```

```python
import numpy as np
from contextlib import ExitStack
import concourse.bass as bass
import concourse.mybir as mybir
from concourse.bass_utils import run_bass_kernel_spmd

F32 = mybir.dt.float32
BF16 = mybir.dt.bfloat16
I32 = mybir.dt.int32
U32 = mybir.dt.uint32
AF = mybir.ActivationFunctionType
ALU = mybir.AluOpType
AX = mybir.AxisListType

NDS = 12
NCORE = 8
D = 1024
T = 128
NCH = 8
NPRE = 16
NMAIN = 16
NSEQ = 16
DEC = 8
NH = 16
NKV = 4
HD = 64
EPS = 1e-6
PAST = 16384
NEXP = 16384
STAGE = 3
DEBUG_C = False
PE_STRIDED = True
DEBUG_LVL = 9


class Dep:
    __slots__ = ("name", "w", "r")

    def __init__(self, name=""):
        self.name = name
        self.w = None
        self.r = {}


class Sched:
    def __init__(self, nc, stack):
        self.nc = nc
        self.engs = {"pe": nc.tensor, "dve": nc.vector, "act": nc.scalar,
                     "pool": nc.gpsimd, "sp": nc.sync}
        self.sem = {k: stack.enter_context(nc.semaphore("s_" + k)) for k in self.engs}
        self.cnt = {k: 0 for k in self.engs}
        self.waited = {k: {} for k in self.engs}
        self.dsem = {k: [stack.enter_context(nc.semaphore("d_%s%d" % (k, i))) for i in range(NDS)]
                     for k in ("sp", "act", "pool")}
        self.dcnt = {k: 0 for k in self.dsem}
        self.dlast = {}
        self.out_toks = []
        self.ninst = 0

    def _wait(self, e, tok):
        sem, val, key = tok
        if self.waited[e].get(key, 0) >= val:
            return
        self.engs[e].wait_ge(sem, val)
        self.waited[e][key] = val

    def _deps(self, e, reads, writes):
        toks = []
        for d in reads:
            if d.w is not None:
                toks.append(d.w)
        for d in writes:
            if d.w is not None and d.w[2] != e:
                toks.append(d.w)
            toks.extend(t for t in d.r.values() if t[2] != e)
        for t in toks:
            if e == "pe" and t[2] == "pe":
                continue
            if t[2] == e and e in ("dve", "act") and self.cnt[e] - t[1] >= 3:
                continue
            self._wait(e, t)

    def _commit(self, tok, reads, writes):
        for d in reads:
            d.r[tok[2]] = tok
        for d in writes:
            d.w = tok
            d.r = {}

    def op(self, e, fn, reads=(), writes=()):
        self._deps(e, reads, writes)
        inst = fn(self.engs[e])
        self.cnt[e] += 1
        inst.then_inc(self.sem[e], 1)
        self.ninst += 1
        self._commit((self.sem[e], self.cnt[e], e), reads, writes)
        return inst

    def dma(self, e, fn, reads=(), writes=(), is_out=False):
        i = self.dcnt[e]
        self.dcnt[e] += 1
        sem = self.dsem[e][i % NDS]
        key = (e, i % NDS)
        val = 16 * (i // NDS + 1)
        if val > 16:
            self._wait(e, (sem, val - 16, key))
        self._deps(e, reads, writes)
        inst = fn(self.engs[e])
        inst.then_inc(sem, 16)
        self.ninst += 1
        tok = (sem, val, key)
        self.dlast[key] = tok
        self._commit(tok, reads, writes)
        if is_out:
            self.out_toks.append(tok)
        return inst

    def barrier(self):
        toks = [(self.sem[k], self.cnt[k], k) for k in self.engs if self.cnt[k] > 0]
        toks += list(self.dlast.values())
        for e in self.engs:
            for t in toks:
                if t[2] == e:
                    continue
                self._wait(e, t)

    def finish(self):
        for t in self.out_toks:
            self._wait("sp", t)
        for k in self.engs:
            if k != "sp" and self.cnt[k] > 0:
                self._wait("sp", (self.sem[k], self.cnt[k], k))
        for t in self.dlast.values():
            self._wait("sp", t)


def build_program():
    nc = bass.Bass("TRN2", target_bir_lowering=False)

    def din(name, shape, dt=F32):
        return nc.dram_tensor(name, list(shape), dt, kind="ExternalInput").ap()

    def dout(name, shape, dt=F32):
        return nc.dram_tensor(name, list(shape), dt, kind="ExternalOutput").ap()

    xp = din("xp", [(NPRE + NMAIN) * T, D])
    xs = din("xs", [T, D])
    cconv = din("cconv", [NSEQ * 3, D])
    slru = din("slru", [NSEQ, D])
    ck = din("ck", [NSEQ, 128, 256])
    cv = din("cv", [NSEQ, 128, 256])
    flag_d = din("flag", [128, 1])
    identf_d = din("identf", [128, 128])
    E_pr_d = din("E_pr", [128, NH * 2 * 128])
    E_so_d = din("E_so", [128, NH * 128])
    E_sc_d = din("E_sc", [128, NH * DEC])
    prow_d = din("prow", [16, D])
    w_in_d = din("w_in", [D, 5632])
    rgw_d = din("rgw", [2, 16, 64, 64])
    qkg_d = din("qkg", [2, 64])
    sinks_d = din("sinks", [1, NH])
    w_bl_d = din("w_bl", [D, D])
    w_ba_d = din("w_ba", [D, D])
    w_o_d = din("w_o", [D, D])
    w_q_d = din("w_q", [D, 2048])
    subk_d = din("subk", [16, 128, 128])
    if STAGE >= 3 and not DEBUG_C:
        eu_d = din("eu", [NEXP, D])
        ev_d = din("ev", [NEXP, D])

    y_p = dout("y_p", [NMAIN * T, D])
    y_s = dout("y_s", [T, D])
    conv_p = dout("conv_p", [3, D])
    lru_p = dout("lru_p", [1, D])
    k_p = dout("k_p", [128, 256])
    v_p = dout("v_p", [128, 256])
    conv_s = dout("conv_s", [NSEQ * 3, D])
    lru_s = dout("lru_s", [NSEQ, D])
    k_s = dout("k_s", [NSEQ, 128, 256])
    v_s = dout("v_s", [NSEQ, 128, 256])

    NT = NMAIN + 1
    scr = nc.dram_tensor("scr", [NT, 128, 2 * NCH * T], BF16, kind="Internal").ap()
    hres_scr = nc.dram_tensor("hres_scr", [NT, 128, D], F32, kind="Internal").ap()
    NTOK = NT * T
    G_scr = nc.dram_tensor("G_scr", [128, 128, NTOK], BF16, kind="Internal").ap()

    with ExitStack() as st:
        S = Sched(nc, st)
        PS = st.enter_context(nc.psum_tensor("PS", [128, 8, 512], F32))
        dP = [Dep("P%d" % b) for b in range(8)]

        def pbank(b):
            return PS[:, b, :]

        def pbank16(b):
            return PS[:, b, :].bitcast(BF16)

        cst = ExitStack()
        st.enter_context(cst)

        def SB(stack, name, shape, dt):
            return stack.enter_context(nc.sbuf_tensor("sb_" + name, list(shape), dt))

        identf = SB(cst, "identf", [128, 128], F32)
        identb = SB(cst, "identb", [128, 128], BF16)
        flag = SB(cst, "flag", [128, 1], F32)
        cp = SB(cst, "cp", [128, NCH, 16], F32)
        d_const = Dep("const")
        S.dma("sp", lambda e: e.dma_start(out=identf[:], in_=identf_d), writes=[d_const])
        S.dma("sp", lambda e: e.dma_start(out=flag[:], in_=flag_d), writes=[d_const])
        S.op("dve", lambda e: e.tensor_copy(out=identb[:], in_=identf[:]), reads=[d_const], writes=[d_const])

        with ExitStack() as s0:
            prow = SB(s0, "prow", [16, D], F32)
            d_prow = Dep()
            S.dma("sp", lambda e: e.dma_start(out=prow[:], in_=prow_d), writes=[d_prow])
            for c in range(NCH):
                S.op("pe", lambda e: e.transpose(out=PS[:, 0, c * 16:(c + 1) * 16], in_=prow[:, c * 128:(c + 1) * 128],
                                                 identity=identf[0:16, 0:16]),
                     reads=[d_prow, d_const], writes=[dP[0]])
            S.op("dve", lambda e: e.tensor_copy(out=cp[:].rearrange("p c j -> p (c j)"), in_=PS[:, 0, 0:128]),
                 reads=[dP[0]], writes=[d_const])
            S.op("act", lambda e: e.activation(out=cp[:, :, 10], in_=cp[:, :, 7], func=AF.Exp, scale=-1.0),
                 reads=[d_const], writes=[d_const])
            S.op("act", lambda e: e.activation(out=cp[:, :, 10], in_=cp[:, :, 10], func=AF.Ln, bias=1.0),
                 reads=[d_const], writes=[d_const])
            S.op("dve", lambda e: e.tensor_scalar(out=cp[:, :, 10], in0=cp[:, :, 10], scalar1=-8.0, scalar2=None,
                                                  op0=ALU.mult), reads=[d_const], writes=[d_const])
            S.barrier()

        with ExitStack() as sa:
            WgA = SB(sa, "WgA", [128, NCH, 3584], BF16)
            d_WgA = Dep("WgA")
            with ExitStack() as s1:
                stg = [SB(s1, "stg%d" % i, [128, 3584], F32) for i in range(4)]
                d_stg = [Dep() for _ in range(4)]
                for kc in range(NCH):
                    b = kc % 4
                    S.dma("sp" if kc % 2 == 0 else "pool",
                          lambda e: e.dma_start(out=stg[b][:], in_=w_in_d[kc * 128:(kc + 1) * 128, 0:3584]),
                          writes=[d_stg[b]])
                    if kc % 2 == 0:
                        S.op("dve", lambda e: e.tensor_scalar(out=WgA[:, kc, :], in0=stg[b][:], scalar1=cp[:, kc, 8:9],
                                                              scalar2=None, op0=ALU.mult),
                             reads=[d_stg[b], d_const], writes=[d_WgA])
                    else:
                        S.op("act", lambda e: e.activation(out=WgA[:, kc, :], in_=stg[b][:], func=AF.Copy, scale=cp[:, kc, 8:9]),
                             reads=[d_stg[b], d_const], writes=[d_WgA])
                S.barrier()

            Wbd = SB(sa, "Wbd", [128, 2, NCH, 128], BF16)
            d_Wbd = Dep("Wbd")
            with ExitStack() as s1:
                wbdf = SB(s1, "wbdf", [128, 2, NCH, 128], F32)
                d_wbdf = Dep()
                S.op("pool", lambda e: e.memset(wbdf[:], 0.0), writes=[d_wbdf])
                for g in range(2):
                    for hb in range(2):
                        src = rgw_d[g].rearrange("(c two) k m -> two k c m", two=2)[hb]
                        S.dma("sp", lambda e: e.dma_start(out=wbdf[hb * 64:(hb + 1) * 64, g, :, hb * 64:(hb + 1) * 64],
                                                          in_=src), writes=[d_wbdf])
                S.op("dve", lambda e: e.tensor_copy(out=Wbd[:], in_=wbdf[:]), reads=[d_wbdf], writes=[d_Wbd])
                S.barrier()

            E_pr = SB(sa, "E_pr", [128, NH, 2, 128], F32)
            E_so = SB(sa, "E_so", [128, NH, 128], F32)
            E_sc = SB(sa, "E_sc", [128, NH, DEC], F32)
            gq8 = SB(sa, "gq8", [128, 64], F32)
            gk = SB(sa, "gk", [128, 64], F32)
            esk = SB(sa, "esk", [128, NH], F32)
            d_ac = Dep("attnconst")
            S.dma("sp", lambda e: e.dma_start(out=E_pr[:].rearrange("p h b q -> p (h b q)"), in_=E_pr_d), writes=[d_ac])
            S.dma("act", lambda e: e.dma_start(out=E_so[:].rearrange("p h q -> p (h q)"), in_=E_so_d), writes=[d_ac])
            S.dma("act", lambda e: e.dma_start(out=E_sc[:].rearrange("p h q -> p (h q)"), in_=E_sc_d), writes=[d_ac])
            S.dma("sp", lambda e: e.dma_start(out=gq8[:], in_=qkg_d[0:1, :].partition_broadcast(128)), writes=[d_ac])
            S.dma("sp", lambda e: e.dma_start(out=gk[:], in_=qkg_d[1:2, :].partition_broadcast(128)), writes=[d_ac])
            S.dma("sp", lambda e: e.dma_start(out=esk[:], in_=sinks_d[0:1, :].partition_broadcast(128)), writes=[d_ac])
            S.op("dve", lambda e: e.tensor_scalar(out=gq8[:], in0=gq8[:], scalar1=0.125, scalar2=None, op0=ALU.mult),
                 reads=[d_ac], writes=[d_ac])
            S.op("act", lambda e: e.activation(out=esk[:], in_=esk[:], func=AF.Exp), reads=[d_ac], writes=[d_ac])

            x_t = [SB(sa, "x_t%d" % i, [128, D], F32) for i in range(2)]
            d_x = [Dep(), Dep()]
            st1 = SB(sa, "st1", [128, 8], F32); d_st1 = Dep()
            xn = SB(sa, "xn", [128, D], BF16); d_xn = Dep()
            xnT = SB(sa, "xnT", [128, NCH, T], BF16); d_xnT = Dep()
            xr = [SB(sa, "xr%d" % i, [128, NCH, 3 + T], F32) for i in range(2)]
            d_xr = [Dep(), Dep()]
            xr_s = SB(sa, "xr_s", [128, NCH, NSEQ, 3 + DEC], F32); d_xr_s = Dep()
            gel2 = [SB(sa, "gel%d" % i, [128, NCH, T], BF16) for i in range(2)]; d_gel2 = [Dep(), Dep()]
            stk = SB(sa, "stk", [128, 4], F32); d_stk = Dep()
            rdq = SB(sa, "rdq", [128, 16], F32); d_rdq = Dep()
            xrtm = SB(sa, "xrtm", [128, D], F32); d_xrtm = Dep()
            gtm = SB(sa, "gtm", [128, D], BF16); d_gtm = Dep()
            xc = SB(sa, "xc", [128, NCH, T], F32); d_xc = Dep()
            d_xcc = [Dep() for _ in range(NCH)]
            xcb = SB(sa, "xcb", [128, NCH, T], BF16); d_xcb = Dep()
            rg = SB(sa, "rg", [128, NCH, T], F32); d_rg = Dep()
            ig = SB(sa, "ig", [128, NCH, T], F32); d_ig = Dep()
            sq = SB(sa, "sq", [128, NCH, T], F32); d_sq = Dep()
            hh = [SB(sa, "hh%d" % i, [128, NCH, T], F32) for i in range(2)]
            d_hh = [Dep(), Dep()]
            h0 = SB(sa, "h0", [128, NCH, NSEQ], F32); d_h0 = Dep()
            hcar = SB(sa, "hcar", [128, NCH], F32); d_hcar = Dep()
            fmo = [SB(sa, "fmo%d" % i, [128, 2, NCH, T], BF16) for i in range(2)]
            d_fmo = [Dep(), Dep()]
            qn = SB(sa, "qn", [128, D], BF16); d_qn = Dep()
            qT2 = [SB(sa, "qT%d" % i, [64, NH, T], BF16) for i in range(2)]; d_qT2 = [Dep(), Dep()]
            kn = SB(sa, "kn", [128, 256], F32); d_kn = Dep()
            ks = SB(sa, "ks", [128, 256], BF16); d_ks = Dep()
            kT = [SB(sa, "kT%d" % i, [64, NKV, T], BF16) for i in range(3)]
            d_kT = [Dep(), Dep(), Dep()]
            vf = SB(sa, "vf", [128, 256], F32); d_vf = Dep()
            vau = [SB(sa, "vau%d" % i, [128, NKV, 65], BF16) for i in range(3)]
            d_vau = [Dep(), Dep(), Dep()]
            ex = SB(sa, "ex", [128, 2, 4, T], F32); d_ex = Dep()
            pT = [SB(sa, "pT%d" % i, [128, 2, 4, T], BF16) for i in range(2)]
            d_pT = [Dep(), Dep()]
            rden = SB(sa, "rden", [128, 18], F32); d_rden = Dep()
            attn = SB(sa, "attn", [128, D], BF16); d_attn = Dep()
            tmo = SB(sa, "tmo", [128, D], F32); d_tmo = Dep()
            cmp_ = SB(sa, "cmp", [128, NCH, 48], F32); d_cmp = Dep()
            ckf = [SB(sa, "ckf0", [128, 256], F32)] * 2
            cvf = [SB(sa, "cvf0", [128, 256], F32)] * 2
            d_ckf = [Dep()] * 2
            d_cvf = [Dep()] * 2
            cks = SB(sa, "cks", [128, 256], BF16); d_cks = Dep()
            ckT = SB(sa, "ckT", [64, NKV, 128], BF16); d_ckT = Dep()
            cva = SB(sa, "cva", [128, NKV, 65], BF16); d_cva = Dep()
            exc = SB(sa, "exc", [128, NH, DEC], F32); d_exc = Dep()
            Zp = [SB(sa, "Zp0", [128, NH, 248], BF16)] * 2
            d_Zp = [Dep()] * 2
            oacc = SB(sa, "oacc", [128, 18, 65], F32); d_oacc = Dep()

            for i in range(3):
                S.op("pool", lambda e: e.memset(vau[i][:, :, 64:65], 1.0), writes=[d_vau[i]])
            for i in range(2):
                S.op("pool", lambda e: e.memset(Zp[i][:], 0.0), writes=[d_Zp[i]])
                S.op("pool", lambda e: e.memset(xr[i][:], 0.0), writes=[d_xr[i]])
                S.op("pool", lambda e: e.memset(hh[i][:], 0.0), writes=[d_hh[i]])
            S.op("pool", lambda e: e.memset(cva[:, :, 64:65], 1.0), writes=[d_cva])

            def fm_to_dram(src_fn, n, dst, reads):
                for c in range(NCH):
                    b = 6 + c // 4
                    S.op("pe", lambda e: e.transpose(out=PS[0:n, b, (c % 4) * 128:(c % 4 + 1) * 128], in_=src_fn(c),
                                                     identity=identf[:]),
                         reads=list(reads) + [d_const], writes=[dP[b]])
                S.op("dve", lambda e: e.tensor_copy(out=tmo[0:n, :].rearrange("p (b x) -> p b x", b=2),
                                                    in_=PS[0:n, 6:8, :]),
                     reads=[dP[6], dP[7]], writes=[d_tmo])
                S.dma("sp", lambda e: e.dma_start(out=dst, in_=tmo[0:n, :]), reads=[d_tmo], is_out=True)

            def norm_to_xnT(xtile, d_xtile):
                S.op("act", lambda e: e.activation(out=xn[:], in_=xtile, func=AF.Square, accum_out=st1[:, 0:1]),
                     reads=[d_xtile], writes=[d_xn, d_st1])
                S.op("act", lambda e: e.activation(out=st1[:, 1:2], in_=st1[:, 0:1], func=AF.Sqrt, scale=1.0 / D, bias=EPS),
                     reads=[d_st1], writes=[d_st1])
                S.op("dve", lambda e: e.reciprocal(out=st1[:, 2:3], in_=st1[:, 1:2]), reads=[d_st1], writes=[d_st1])
                S.op("act", lambda e: e.activation(out=xn[:], in_=xtile, func=AF.Copy, scale=st1[:, 2:3]),
                     reads=[d_xtile, d_st1], writes=[d_xn])
                for c in range(NCH):
                    S.op("pe", lambda e: e.transpose(out=pbank16(0)[:, c * 128:(c + 1) * 128], in_=xn[:, c * 128:(c + 1) * 128],
                                                     identity=identb[:]), reads=[d_xn, d_const], writes=[dP[0]])
                S.op("dve", lambda e: e.tensor_copy(out=xnT[:].rearrange("p c t -> p (c t)"), in_=pbank16(0)),
                     reads=[dP[0]], writes=[d_xnT])

            def proj_fm(col0, banks):
                for c in range(NCH):
                    b = banks[c // 4]
                    for k in range(NCH):
                        S.op("pe", lambda e: e.matmul(PS[:, b, (c % 4) * 128:(c % 4 + 1) * 128],
                                                      lhsT=WgA[:, k, col0 + c * 128: col0 + (c + 1) * 128],
                                                      rhs=xnT[:, k, :], start=(k == 0), stop=(k == NCH - 1)),
                             reads=[d_WgA, d_xnT], writes=[dP[b]])

            def proj_xr(par, sample=False, conv_first=False):
                for j in range(2):
                    for k in range(NCH):
                        S.op("pe", lambda e: e.matmul(PS[:, 1 + j, :], lhsT=xnT[:, k, :], rhs=WgA[:, k, j * 512:(j + 1) * 512],
                                                      start=(k == 0), stop=(k == NCH - 1)),
                             reads=[d_WgA, d_xnT], writes=[dP[1 + j]])
                S.op("act", lambda e: e.activation(out=xrtm[:].rearrange("p (b x) -> p b x", b=2), in_=PS[:, 1:3, :], func=AF.Copy),
                     reads=[dP[1], dP[2]], writes=[d_xrtm])
                for c in range(NCH):
                    b = 1 + c // 4
                    S.op("pe", lambda e: e.transpose(out=PS[:, b, (c % 4) * 128:(c % 4 + 1) * 128], in_=xrtm[:, c * 128:(c + 1) * 128],
                                                     identity=identf[:]), reads=[d_xrtm, d_const], writes=[dP[b]])
                if sample:
                    S.op("dve", lambda e: e.tensor_copy(
                        out=xr_s[:, :, :, 3:3 + DEC],
                        in_=PS[:, 1:3, :].rearrange("p b (c s t) -> p (b c) s t", c=4, t=DEC)),
                        reads=[dP[1], dP[2]], writes=[d_xr_s])
                else:
                    if not conv_first:
                        S.op("pool", lambda e: e.tensor_copy(out=xr[par][:, :, 0:3], in_=xr[1 - par][:, :, T:T + 3]),
                             reads=[d_xr[1 - par]], writes=[d_xr[par]])
                    S.op("dve", lambda e: e.tensor_copy(
                        out=xr[par][:, :, 3:3 + T], in_=PS[:, 1:3, :].rearrange("p b (c t) -> p (b c) t", c=4)),
                        reads=[dP[1], dP[2]], writes=[d_xr[par]])

            def proj_gr(par):
                gel, d_gel = gel2[par], d_gel2[par]
                for j in range(2):
                    for k in range(NCH):
                        S.op("pe", lambda e: e.matmul(PS[:, 3 + j, :], lhsT=xnT[:, k, :], rhs=WgA[:, k, 1024 + j * 512:1024 + (j + 1) * 512],
                                                      start=(k == 0), stop=(k == NCH - 1)),
                             reads=[d_WgA, d_xnT], writes=[dP[3 + j]])
                S.op("act", lambda e: e.activation(out=gtm[:].rearrange("p (b x) -> p b x", b=2), in_=PS[:, 3:5, :], func=AF.Gelu_apprx_tanh),
                     reads=[dP[3], dP[4]], writes=[d_gtm])
                for c in range(NCH):
                    S.op("pe", lambda e: e.transpose(out=pbank16(3)[:, c * 128:(c + 1) * 128], in_=gtm[:, c * 128:(c + 1) * 128],
                                                     identity=identb[:]), reads=[d_gtm, d_const], writes=[dP[3]])
                S.op("act", lambda e: e.activation(out=gel[:].rearrange("p c t -> p (c t)"), in_=pbank16(3), func=AF.Copy),
                     reads=[dP[3]], writes=[d_gel])

            def lru_tile(par, sample, first, need_out, conv_first=None):
                if conv_first is None:
                    conv_first = first
                XR = xr_s if sample else xr[par]
                dXR = d_xr_s if sample else d_xr[par]
                nseq, L = (NSEQ, DEC) if sample else (1, T)

                def v4(ap3):
                    return ap3.rearrange("p (s t) -> p s t", t=L)

                def win_(c, j):
                    return XR[:, c, :, j:j + L] if sample else XR[:, c, j:j + L]

                def dst_(c):
                    return v4(xc[:, c, :]) if sample else xc[:, c, :]

                for c in range(NCH):
                    S.op("act", lambda e: e.activation(out=dst_(c), in_=win_(c, 0), func=AF.Identity,
                                                       scale=cp[:, c, 0:1], bias=cp[:, c, 4:5]),
                         reads=[dXR, d_const], writes=[d_xcc[c]])
                for c in range(NCH):
                    for j in range(1, 4):
                        S.op("dve", lambda e: e.scalar_tensor_tensor(out=dst_(c), in0=win_(c, j), scalar=cp[:, c, j:j + 1], in1=dst_(c),
                                                                     op0=ALU.mult, op1=ALU.add),
                             reads=[dXR, d_const, d_xcc[c]], writes=[d_xcc[c]])
                    if c % 2 == 1:
                        yield
                yield
                S.op("dve", lambda e: e.tensor_copy(out=xcb[:], in_=xc[:]), reads=d_xcc, writes=[d_xcb])
                for g, (dst, ddst, banks, bcol) in enumerate(((rg, d_rg, (1, 2), 5), (ig, d_ig, (3, 4), 6))):
                    for c in range(NCH):
                        b = banks[c // 4]
                        S.op("pe", lambda e: e.matmul(PS[:, b, (c % 4) * 128:(c % 4 + 1) * 128], lhsT=Wbd[:, g, c, :],
                                                      rhs=xcb[:, c, :], start=True, stop=True),
                             reads=[d_Wbd, d_xcb], writes=[dP[b]])
                    for c in range(NCH):
                        b = banks[c // 4]
                        S.op("act", lambda e: e.activation(out=dst[:, c, :], in_=PS[:, b, (c % 4) * 128:(c % 4 + 1) * 128],
                                                           func=AF.Sigmoid, bias=cp[:, c, bcol:bcol + 1]),
                             reads=[dP[b], d_const], writes=[ddst])
                    yield
                S.op("dve", lambda e: e.tensor_tensor(out=rg[:], in0=rg[:], in1=cp[:, :, 10:11].to_broadcast([128, NCH, T]),
                                                      op=ALU.mult), reads=[d_rg, d_const], writes=[d_rg])
                S.op("act", lambda e: e.activation(out=rg[:], in_=rg[:], func=AF.Exp), reads=[d_rg], writes=[d_rg])
                yield
                S.op("act", lambda e: e.activation(out=sq[:], in_=rg[:], func=AF.Square), reads=[d_rg], writes=[d_sq])
                S.op("act", lambda e: e.activation(out=sq[:], in_=sq[:], func=AF.Sqrt, scale=-1.0, bias=1.0),
                     reads=[d_sq], writes=[d_sq])
                S.op("dve", lambda e: e.tensor_tensor(out=ig[:], in0=ig[:], in1=xc[:], op=ALU.mult),
                     reads=[d_ig] + d_xcc, writes=[d_ig])
                yield
                S.op("dve", lambda e: e.tensor_tensor(out=ig[:], in0=ig[:], in1=sq[:], op=ALU.mult),
                     reads=[d_ig, d_sq], writes=[d_ig])
                H = hh[par]
                if sample:
                    Hv = H[:].rearrange("p c (s t) -> p c s t", t=DEC)
                    Av = rg[:].rearrange("p c (s t) -> p c s t", t=DEC)
                    Bv = ig[:].rearrange("p c (s t) -> p c s t", t=DEC)
                    for t in range(DEC):
                        prev = h0[:] if t == 0 else Hv[:, :, :, t - 1]
                        S.op("dve", lambda e: e.tensor_tensor(out=Hv[:, :, :, t], in0=Av[:, :, :, t], in1=prev, op=ALU.mult),
                             reads=[d_rg, d_h0, d_hh[par]], writes=[d_hh[par]])
                        S.op("dve", lambda e: e.tensor_tensor(out=Hv[:, :, :, t], in0=Hv[:, :, :, t], in1=Bv[:, :, :, t], op=ALU.add),
                             reads=[d_ig, d_hh[par]], writes=[d_hh[par]])
                else:
                    hprev = h0[:, :, 0] if first else hh[1 - par][:, :, T - 1]
                    S.op("dve", lambda e: e.tensor_tensor(out=hcar[:], in0=rg[:, :, 0], in1=hprev, op=ALU.mult),
                         reads=[d_rg, d_h0, d_hh[1 - par]], writes=[d_hcar])
                    S.op("dve", lambda e: e.tensor_tensor(out=ig[:, :, 0], in0=ig[:, :, 0], in1=hcar[:], op=ALU.add),
                         reads=[d_ig, d_hcar], writes=[d_ig])
                    S.op("dve", lambda e: e.memset(rg[:, :, 0], 0.0), reads=[d_hcar], writes=[d_rg])
                    S.op("dve", lambda e: e.tensor_tensor_scan(out=H[:].rearrange("p c t -> p (c t)"),
                                                               data0=rg[:].rearrange("p c t -> p (c t)"),
                                                               data1=ig[:].rearrange("p c t -> p (c t)"),
                                                               initial=0.0, op0=ALU.mult, op1=ALU.add),
                         reads=[d_rg, d_ig], writes=[d_hh[par]])
                yield
                if need_out:
                    S.op("dve", lambda e: e.tensor_tensor(out=fmo[par][:, 0, :, :], in0=H[:], in1=gel2[par][:], op=ALU.mult),
                         reads=[d_hh[par], d_gel2[par]], writes=[d_fmo[par]])
                yield

            def kv_prep(par):
                for k in range(NCH):
                    S.op("pe", lambda e: e.matmul(PS[:, 3, :], lhsT=xnT[:, k, :], rhs=WgA[:, k, 3072:3584],
                                                  start=(k == 0), stop=(k == NCH - 1)),
                         reads=[d_WgA, d_xnT], writes=[dP[3]])
                S.op("act", lambda e: e.activation(out=xrtm[:, 0:256], in_=PS[:, 3, 0:256], func=AF.Square),
                     reads=[dP[3]], writes=[d_xrtm])
                S.op("dve", lambda e: e.tensor_reduce(out=stk[:, 0:4], in_=xrtm[:, 0:256].rearrange("p (h d) -> p h d", d=HD),
                                                      axis=AX.X, op=ALU.add), reads=[d_xrtm], writes=[d_stk])
                S.op("act", lambda e: e.activation(out=stk[:, 0:4], in_=stk[:, 0:4], func=AF.Sqrt, scale=1.0 / HD, bias=EPS),
                     reads=[d_stk], writes=[d_stk])
                S.op("dve", lambda e: e.reciprocal(out=stk[:, 0:4], in_=stk[:, 0:4]), reads=[d_stk], writes=[d_stk])
                S.op("dve", lambda e: e.tensor_tensor(out=kn[:].rearrange("p (h d) -> p h d", d=HD),
                                                      in0=PS[:, 3, 0:256].rearrange("p (h d) -> p h d", d=HD),
                                                      in1=stk[:, 0:4].unsqueeze(2).to_broadcast([128, NKV, HD]), op=ALU.mult),
                     reads=[dP[3], d_stk], writes=[d_kn])
                S.op("dve", lambda e: e.tensor_tensor(out=kn[:].rearrange("p (h d) -> p h d", d=HD),
                                                      in0=kn[:].rearrange("p (h d) -> p h d", d=HD),
                                                      in1=gk[:].unsqueeze(1).to_broadcast([128, NKV, HD]), op=ALU.mult),
                     reads=[d_kn, d_ac], writes=[d_kn])
                S.op("pool", lambda e: e.tensor_tensor(out=ks[:].rearrange("p (h d) -> p h d", d=HD),
                                                       in0=kn[:].rearrange("p (h d) -> p h d", d=HD),
                                                       in1=gq8[:].unsqueeze(1).to_broadcast([128, NKV, HD]), op=ALU.mult),
                     reads=[d_kn, d_ac], writes=[d_ks])
                S.op("act", lambda e: e.activation(out=vf[:], in_=PS[:, 3, 256:512], func=AF.Copy),
                     reads=[dP[3]], writes=[d_vf])
                S.op("act", lambda e: e.activation(out=vau[par][:, :, 0:64], in_=vf[:].rearrange("p (h d) -> p h d", d=HD), func=AF.Copy),
                     reads=[d_vf], writes=[d_vau[par]])
                S.op("pool", lambda e: e.memset(vau[par][:, :, 64:65], 1.0), writes=[d_vau[par]])
                for h in range(NKV):
                    S.op("pe", lambda e: e.transpose(out=pbank16(0)[0:64, h * 128:(h + 1) * 128], in_=ks[:, h * 64:(h + 1) * 64],
                                                     identity=identb[:]), reads=[d_ks, d_const], writes=[dP[0]])
                S.op("act", lambda e: e.activation(out=kT[par][:].rearrange("p h t -> p (h t)"), in_=pbank16(0)[0:64, 0:512],
                                                   func=AF.Copy), reads=[dP[0]], writes=[d_kT[par]])

            def q_prep(qb):
                qT, d_qT = qT2[qb], d_qT2[qb]
                for j in range(2):
                    for k in range(NCH):
                        S.op("pe", lambda e: e.matmul(PS[:, 3 + j, :], lhsT=xnT[:, k, :],
                                                      rhs=WgA[:, k, 2048 + j * 512:2048 + (j + 1) * 512],
                                                      start=(k == 0), stop=(k == NCH - 1)),
                             reads=[d_WgA, d_xnT], writes=[dP[3 + j]])
                S.op("act", lambda e: e.activation(out=xrtm[:].rearrange("p (b x) -> p b x", b=2), in_=PS[:, 3:5, :], func=AF.Square),
                     reads=[dP[3], dP[4]], writes=[d_xrtm])
                S.op("dve", lambda e: e.tensor_reduce(out=rdq[:, 0:16], in_=xrtm[:].rearrange("p (h d) -> p h d", d=HD),
                                                      axis=AX.X, op=ALU.add), reads=[d_xrtm], writes=[d_rdq])
                S.op("act", lambda e: e.activation(out=rdq[:, 0:16], in_=rdq[:, 0:16], func=AF.Sqrt, scale=1.0 / HD, bias=EPS),
                     reads=[d_rdq], writes=[d_rdq])
                S.op("dve", lambda e: e.reciprocal(out=rdq[:, 0:16], in_=rdq[:, 0:16]), reads=[d_rdq], writes=[d_rdq])
                S.op("dve", lambda e: e.tensor_tensor(out=qn[:].rearrange("p (b h d) -> p b h d", b=2, d=HD),
                                                      in0=PS[:, 3:5, :].rearrange("p b (h d) -> p b h d", d=HD),
                                                      in1=rdq[:, 0:16].rearrange("p (b h) -> p b h", b=2).unsqueeze(3).to_broadcast([128, 2, 8, HD]),
                                                      op=ALU.mult),
                     reads=[dP[3], dP[4], d_rdq], writes=[d_qn])
                for h in range(NH):
                    b = 3 + h // 8
                    S.op("pe", lambda e: e.transpose(out=pbank16(b)[0:64, (h % 8) * 128:(h % 8 + 1) * 128],
                                                     in_=qn[:, h * 64:(h + 1) * 64], identity=identb[:]),
                         reads=[d_qn, d_const], writes=[dP[b]])
                for j in range(2):
                    S.op("act" if j == 0 else "dve",
                         (lambda e: e.activation(out=qT[:, 0:8, :].rearrange("p h t -> p (h t)"), in_=pbank16(3)[0:64, :], func=AF.Copy))
                         if j == 0 else
                         (lambda e: e.tensor_copy(out=qT[:, 8:16, :].rearrange("p h t -> p (h t)"), in_=pbank16(4)[0:64, :])),
                         reads=[dP[3 + j]], writes=[d_qT])

            def oslot(h):
                return PS[:, 5 + h // 6, (h % 6) * 80:(h % 6) * 80 + 65]

            def attn_own_prev(kb, qb, Eown_fn, kbp, Eprev_fn):
                qT, d_qT = qT2[qb], d_qT2[qb]
                nb = 2 if kbp is not None else 1
                for g4 in range(4):
                    pp = g4 % 2
                    for bi in range(nb):
                        kk = kb if bi == 0 else kbp
                        for hh_ in range(4):
                            h = g4 * 4 + hh_
                            S.op("pe", lambda e: e.matmul(PS[:, 1 + bi, hh_ * 128:(hh_ + 1) * 128], lhsT=kT[kk][:, g4, :],
                                                          rhs=qT[:, h, :], start=True, stop=True),
                                 reads=[d_kT[kk], d_qT], writes=[dP[1 + bi]])
                    S.op("act", lambda e: e.activation(out=ex[:, 0:nb, :, :].rearrange("p b h t -> p b (h t)"),
                                                       in_=PS[:, 1:1 + nb, :], func=AF.Exp),
                         reads=[dP[1], dP[2]][:nb], writes=[d_ex])
                    S.op("dve", lambda e: e.tensor_tensor(out=pT[pp][:, 0, :, :], in0=ex[:, 0, :, :], in1=Eown_fn(g4), op=ALU.mult),
                         reads=[d_ex, d_ac], writes=[d_pT[pp]])
                    if nb == 2:
                        S.op("dve", lambda e: e.tensor_tensor(out=pT[pp][:, 1, :, :], in0=ex[:, 1, :, :], in1=Eprev_fn(g4), op=ALU.mult),
                             reads=[d_ex, d_ac], writes=[d_pT[pp]])
                    for hh_ in range(4):
                        h = g4 * 4 + hh_
                        b = 5 + h // 6
                        S.op("pe", lambda e: e.matmul(oslot(h), lhsT=pT[pp][:, 0, hh_, :], rhs=vau[kb][:, g4, :],
                                                      start=True, stop=(nb == 1)),
                             reads=[d_pT[pp], d_vau[kb]], writes=[dP[b]])
                        if nb == 2:
                            S.op("pe", lambda e: e.matmul(oslot(h), lhsT=pT[pp][:, 1, hh_, :], rhs=vau[kbp][:, g4, :],
                                                          start=False, stop=True),
                                 reads=[d_pT[pp], d_vau[kbp]], writes=[dP[b]])
                    yield

            def attn_finish(par, from_oacc):
                if from_oacc:
                    den_src = oacc[:, 0:16, 64]
                    S.op("dve", lambda e: e.tensor_tensor(out=rden[:, 0:16], in0=den_src, in1=esk[:], op=ALU.add),
                         reads=[d_oacc, d_ac], writes=[d_rden])
                    S.op("dve", lambda e: e.reciprocal(out=rden[:, 0:16], in_=rden[:, 0:16]), reads=[d_rden], writes=[d_rden])
                    S.op("dve", lambda e: e.tensor_tensor(out=attn[:].rearrange("p (h d) -> p h d", d=HD), in0=oacc[:, 0:16, 0:64],
                                                          in1=rden[:, 0:16].unsqueeze(2).to_broadcast([128, 16, HD]), op=ALU.mult),
                         reads=[d_oacc, d_rden], writes=[d_attn])
                else:
                    pv = PS[:, 5:8, 0:480].rearrange("p b (s e) -> p b s e", e=80)
                    S.op("dve", lambda e: e.tensor_copy(out=rden[:, 0:12].rearrange("p (b s) -> p b s", b=2), in_=pv[:, 0:2, :, 64]),
                         reads=[dP[5], dP[6]], writes=[d_rden])
                    S.op("dve", lambda e: e.tensor_copy(out=rden[:, 12:16], in_=pv[:, 2, 0:4, 64]),
                         reads=[dP[7]], writes=[d_rden])
                    S.op("dve", lambda e: e.tensor_tensor(out=rden[:, 0:16], in0=rden[:, 0:16], in1=esk[:], op=ALU.add),
                         reads=[d_rden, d_ac], writes=[d_rden])
                    S.op("dve", lambda e: e.reciprocal(out=rden[:, 0:16], in_=rden[:, 0:16]), reads=[d_rden], writes=[d_rden])
                    S.op("dve", lambda e: e.tensor_tensor(out=attn[:, 0:768].rearrange("p (b s d) -> p b s d", b=2, d=HD),
                                                          in0=pv[:, 0:2, :, 0:64],
                                                          in1=rden[:, 0:12].rearrange("p (b s) -> p b s", b=2).unsqueeze(3).to_broadcast([128, 2, 6, HD]),
                                                          op=ALU.mult),
                         reads=[dP[5], dP[6], d_rden], writes=[d_attn])
                    S.op("dve", lambda e: e.tensor_tensor(out=attn[:, 768:1024].rearrange("p (s d) -> p s d", d=HD),
                                                          in0=pv[:, 2, 0:4, 0:64],
                                                          in1=rden[:, 12:16].unsqueeze(2).to_broadcast([128, 4, HD]), op=ALU.mult),
                         reads=[dP[7], d_rden], writes=[d_attn])
                for c in range(NCH):
                    S.op("pe", lambda e: e.transpose(out=pbank16(0)[:, c * 128:(c + 1) * 128], in_=attn[:, c * 128:(c + 1) * 128],
                                                     identity=identb[:]), reads=[d_attn, d_const], writes=[dP[0]])
                S.op("act", lambda e: e.activation(out=fmo[par][:, 1, :, :].rearrange("p c t -> p (c t)"), in_=pbank16(0), func=AF.Copy),
                     reads=[dP[0]], writes=[d_fmo[par]])

            S.op("pool", lambda e: e.memset(h0[:], 0.0), writes=[d_h0])
            ntile = NPRE + NMAIN

            def attn_gen(ti):
                par = ti % 2
                mi = ti - NPRE
                yield from attn_own_prev(ti % 3, ti % 2, lambda g4: E_pr[:, g4 * 4:(g4 + 1) * 4, 1, :],
                                         (ti - 1) % 3, (lambda g4: E_pr[:, g4 * 4:(g4 + 1) * 4, 0, :]))
                attn_finish(par, False)
                S.dma("pool", lambda e: e.dma_start(out=scr[mi], in_=fmo[par][:].rearrange("p a c t -> p (a c t)")),
                      reads=[d_fmo[par]])
                yield

            def run_gens(gens):
                gens = list(gens)
                while gens:
                    for g in list(gens):
                        try:
                            next(g)
                        except StopIteration:
                            gens.remove(g)

            def head_gen(ti):
                par = ti % 2
                main = ti >= NPRE
                S.dma("sp" if par == 0 else "act",
                      lambda e: e.dma_start(out=x_t[par][:], in_=xp[ti * T:(ti + 1) * T, :]), writes=[d_x[par]])
                norm_to_xnT(x_t[par][:], d_x[par])
                yield
                proj_xr(par, sample=False, conv_first=(ti == 0))
                yield
                if main:
                    proj_gr(par)
                    yield
                if ti == NPRE - 1 or main:
                    kv_prep(ti % 3)
                    yield
                if ti == NPRE - 1:
                    kb_ = ti % 3
                    S.op("dve", lambda e: e.tensor_scalar(out=vau[kb_][:], in0=vau[kb_][:], scalar1=flag[:, 0:1], scalar2=None,
                                                          op0=ALU.mult), reads=[d_vau[kb_], d_const], writes=[d_vau[kb_]])
                if main:
                    q_prep(ti % 2)
                    yield
                if ti == ntile - 1:
                    S.dma("sp", lambda e: e.dma_start(out=k_p, in_=kn[:]), reads=[d_kn], is_out=True)
                    S.dma("sp", lambda e: e.dma_start(out=v_p, in_=vf[:]), reads=[d_vf], is_out=True)

            pending = None
            run_gens([head_gen(0)])
            for ti in range(ntile):
                par = ti % 2
                main = ti >= NPRE
                if ti == NPRE:
                    S.op("dve", lambda e: e.tensor_scalar(out=h0[:, :, 0], in0=hh[1 - par][:, :, T - 1], scalar1=flag[:, 0:1],
                                                          scalar2=None, op0=ALU.mult),
                         reads=[d_hh[1 - par], d_const], writes=[d_h0])
                gens = [lru_tile(par, sample=False, first=(ti == 0 or ti == NPRE), need_out=main)]
                if pending is not None:
                    gens.append(attn_gen(pending))
                if ti + 1 < ntile:
                    gens.append(head_gen(ti + 1))
                run_gens(gens)
                pending = ti if main else None
                if ti == ntile - 1:
                    fm_to_dram(lambda c: xr[par][:, c, T:T + 3], 3, conv_p, [d_xr[par]])
                    fm_to_dram(lambda c: hh[par][:, c, T - 1:T], 1, lru_p, [d_hh[par]])
            run_gens([attn_gen(pending)])

            par = ntile % 2
            S.dma("sp", lambda e: e.dma_start(out=x_t[par][:], in_=xs), writes=[d_x[par]])
            S.dma("act", lambda e: e.dma_start(out=tmo[0:48, :], in_=cconv), reads=[], writes=[d_tmo])
            for c in range(NCH):
                S.op("pe", lambda e: e.transpose(out=PS[:, 6, c * 48:(c + 1) * 48], in_=tmo[0:48, c * 128:(c + 1) * 128],
                                                 identity=identf[0:48, 0:48]), reads=[d_tmo, d_const], writes=[dP[6]])
            S.op("dve", lambda e: e.tensor_copy(out=xr_s[:, :, :, 0:3], in_=PS[:, 6, 0:384].rearrange("p (c s j) -> p c s j", c=NCH, j=3)),
                 reads=[dP[6]], writes=[d_xr_s])
            S.dma("act", lambda e: e.dma_start(out=tmo[0:16, :], in_=slru), reads=[], writes=[d_tmo])
            for c in range(NCH):
                S.op("pe", lambda e: e.transpose(out=PS[:, 6, c * 16:(c + 1) * 16], in_=tmo[0:16, c * 128:(c + 1) * 128],
                                                 identity=identf[0:16, 0:16]), reads=[d_tmo, d_const], writes=[dP[6]])
            S.op("dve", lambda e: e.tensor_copy(out=h0[:], in_=PS[:, 6, 0:128].rearrange("p (c s) -> p c s", c=NCH)),
                 reads=[dP[6]], writes=[d_h0])
            norm_to_xnT(x_t[par][:], d_x[par])
            proj_xr(par, sample=True)
            proj_gr(par)
            for _ in lru_tile(par, sample=True, first=True, need_out=True):
                pass
            kv_prep(0)
            q_prep(0)
            qT, d_qT = qT2[0], d_qT2[0]
            for _ in attn_own_prev(0, 0, lambda g4: E_so[:, g4 * 4:(g4 + 1) * 4, :], None, None):
                pass
            pv = PS[:, 5:8, 0:480].rearrange("p b (s e) -> p b s e", e=80)
            S.op("dve", lambda e: e.tensor_copy(out=oacc[:, 0:12, :].rearrange("p (b s) e -> p b s e", b=2), in_=pv[:, 0:2, :, 0:65]),
                 reads=[dP[5], dP[6]], writes=[d_oacc])
            S.op("dve", lambda e: e.tensor_copy(out=oacc[:, 12:16, :], in_=pv[:, 2, 0:4, 0:65]),
                 reads=[dP[7]], writes=[d_oacc])
            for sq_ in range(NSEQ):
                cb = sq_ % 2
                S.dma("sp", lambda e: e.dma_start(out=ckf[cb][:], in_=ck[sq_]), writes=[d_ckf[cb]])
                S.dma("act", lambda e: e.dma_start(out=cvf[cb][:], in_=cv[sq_]), writes=[d_cvf[cb]])
                S.op("pool", lambda e: e.tensor_tensor(out=cks[:].rearrange("p (h d) -> p h d", d=HD),
                                                       in0=ckf[cb][:].rearrange("p (h d) -> p h d", d=HD),
                                                       in1=gq8[:].unsqueeze(1).to_broadcast([128, NKV, HD]), op=ALU.mult),
                     reads=[d_ckf[cb], d_ac], writes=[d_cks])
                S.op("pool", lambda e: e.tensor_copy(out=cva[:, :, 0:64], in_=cvf[cb][:].rearrange("p (h d) -> p h d", d=HD)),
                     reads=[d_cvf[cb]], writes=[d_cva])
                for h in range(NKV):
                    S.op("pe", lambda e: e.transpose(out=pbank16(0)[0:64, h * 128:(h + 1) * 128], in_=cks[:, h * 64:(h + 1) * 64],
                                                     identity=identb[:]), reads=[d_cks, d_const], writes=[dP[0]])
                S.op("act", lambda e: e.activation(out=ckT[:].rearrange("p h t -> p (h t)"), in_=pbank16(0)[0:64, 0:512], func=AF.Copy),
                     reads=[dP[0]], writes=[d_ckT])
                for g4 in range(NKV):
                    S.op("pe", lambda e: e.matmul(PS[:, 1, g4 * 32:(g4 + 1) * 32], lhsT=ckT[:, g4, :],
                                                  rhs=qT[:, g4 * 4:(g4 + 1) * 4, sq_ * DEC:(sq_ + 1) * DEC],
                                                  start=True, stop=True), reads=[d_ckT, d_qT], writes=[dP[1]])
                S.op("act", lambda e: e.activation(out=exc[:].rearrange("p h t -> p (h t)"), in_=PS[:, 1, 0:128], func=AF.Exp),
                     reads=[dP[1]], writes=[d_exc])
                S.op("dve", lambda e: e.tensor_tensor(out=Zp[cb][:, :, 120:128], in0=exc[:], in1=E_sc[:], op=ALU.mult),
                     reads=[d_exc, d_ac], writes=[d_Zp[cb]])
                for h in range(NH):
                    b = 5 + h // 6
                    S.op("pe", lambda e: e.matmul(oslot(h), lhsT=Zp[cb][:, h, 120 - sq_ * DEC:248 - sq_ * DEC], rhs=cva[:, h // 4, :],
                                                  start=True, stop=True), reads=[d_Zp[cb], d_cva], writes=[dP[b]])
                S.op("dve", lambda e: e.tensor_tensor(out=oacc[:, 0:12, :].rearrange("p (b s) e -> p b s e", b=2),
                                                      in0=oacc[:, 0:12, :].rearrange("p (b s) e -> p b s e", b=2), in1=pv[:, 0:2, :, 0:65], op=ALU.add),
                     reads=[dP[5], dP[6], d_oacc], writes=[d_oacc])
                S.op("dve", lambda e: e.tensor_tensor(out=oacc[:, 12:16, :], in0=oacc[:, 12:16, :], in1=pv[:, 2, 0:4, 0:65], op=ALU.add),
                     reads=[dP[7], d_oacc], writes=[d_oacc])
            attn_finish(par, True)
            S.dma("pool", lambda e: e.dma_start(out=scr[NMAIN], in_=fmo[par][:].rearrange("p a c t -> p (a c t)")),
                  reads=[d_fmo[par]])
            S.op("dve", lambda e: e.tensor_copy(out=cmp_[:].rearrange("p c (s j) -> p c s j", j=3), in_=xr_s[:, :, :, DEC:DEC + 3]),
                 reads=[d_xr_s], writes=[d_cmp])
            fm_to_dram(lambda c: cmp_[:, c, :], 48, conv_s, [d_cmp])
            S.op("dve", lambda e: e.tensor_copy(out=cmp_[:, :, 0:16], in_=hh[par][:].rearrange("p c (s t) -> p c s t", t=DEC)[:, :, :, DEC - 1]),
                 reads=[d_hh[par]], writes=[d_cmp])
            fm_to_dram(lambda c: cmp_[:, c, 0:16], 16, lru_s, [d_cmp])
            S.dma("sp", lambda e: e.dma_start(out=k_s[:, 0:120, :], in_=ck[:, 8:128, :]), is_out=True)
            S.dma("act", lambda e: e.dma_start(out=v_s[:, 0:120, :], in_=cv[:, 8:128, :]), is_out=True)
            for sq_ in range(NSEQ):
                S.dma("sp", lambda e: e.dma_start(out=k_s[sq_, 120:128, :], in_=kn[sq_ * DEC:(sq_ + 1) * DEC, :]), reads=[d_kn], is_out=True)
                S.dma("act", lambda e: e.dma_start(out=v_s[sq_, 120:128, :], in_=vf[sq_ * DEC:(sq_ + 1) * DEC, :]), reads=[d_vf], is_out=True)
            S.barrier()

        sbc = ExitStack()
        st.enter_context(sbc)
        d_hres = [Dep() for _ in range(NT)]
        spc = ExitStack()
        st.enter_context(spc)
        xn2T_all = SB(spc, "xn2T_all", [128, NCH, NTOK], BF16)
        d_xn2T_all = [Dep() for _ in range(NT)]
        slotT = SB(spc, "slotT", [128, 3, NTOK], BF16)
        d_slotT = [Dep() for _ in range(NT)]
        swq = ExitStack()
        Wq = SB(swq, "Wq", [128, NCH, 2048], BF16)
        SKT = SB(swq, "SKT", [128, 16, 128], BF16)
        d_WC = Dep("WC")

        if STAGE >= 2:
          with ExitStack() as sb_:
            WgB = SB(sb_, "WgB", [128, NCH, 2048], BF16)
            Wl = SB(sb_, "Wl", [128, NCH, D], BF16)
            Wa = SB(sb_, "Wa", [128, NCH, D], BF16)
            Wo = SB(sb_, "Wo", [128, NCH, D], BF16)
            d_WB = Dep("WB")
            with ExitStack() as s1:
                stg = [SB(s1, "stgB%d" % i, [128, 2048], F32) for i in range(4)]
                d_stg = [Dep() for _ in range(4)]
                jobs = []
                for kc in range(NCH):
                    jobs.append((w_in_d[kc * 128:(kc + 1) * 128, 3584:5632], WgB[:, kc, :], cp[:, kc, 8:9], d_WB))
                for (wd, wsb) in ((w_bl_d, Wl), (w_ba_d, Wa), (w_o_d, Wo)):
                    wv = wd.rearrange("(kc p) n -> p kc n", p=128)
                    for k2 in range(NCH // 2):
                        jobs.append((wv[:, 2 * k2:2 * k2 + 2, :], wsb[:, 2 * k2:2 * k2 + 2, :].rearrange("p a n -> p (a n)"), None, d_WB))
                if STAGE >= 3:
                    for kc in range(NCH):
                        jobs.append((w_q_d[kc * 128:(kc + 1) * 128, :], Wq[:, kc, :], cp[:, kc, 9:10], d_WC))
                for n, (src, dst, sc_ap, ddst) in enumerate(jobs):
                    b = n % 4
                    o_ap = stg[b][:] if len(src.shape) == 2 else stg[b][:].rearrange("p (a n) -> p a n", a=2)
                    S.dma("sp" if n % 2 == 0 else "pool", lambda e: e.dma_start(out=o_ap, in_=src), writes=[d_stg[b]])
                    if n % 2 == 0:
                        if sc_ap is None:
                            S.op("dve", lambda e: e.tensor_copy(out=dst, in_=stg[b][:]), reads=[d_stg[b]], writes=[ddst])
                        else:
                            S.op("dve", lambda e: e.tensor_scalar(out=dst, in0=stg[b][:], scalar1=sc_ap, scalar2=None, op0=ALU.mult),
                                 reads=[d_stg[b], d_const], writes=[ddst])
                    else:
                        if sc_ap is None:
                            S.op("act", lambda e: e.activation(out=dst, in_=stg[b][:], func=AF.Copy), reads=[d_stg[b]], writes=[ddst])
                        else:
                            S.op("act", lambda e: e.activation(out=dst, in_=stg[b][:], func=AF.Copy, scale=sc_ap),
                                 reads=[d_stg[b], d_const], writes=[ddst])
                if STAGE >= 3:
                    S.dma("sp", lambda e: e.dma_start(out=stg[0][:].rearrange("p (h d) -> p h d", d=128),
                                                      in_=subk_d.rearrange("h k d -> k h d")),
                          reads=[d_stg[0]], writes=[d_stg[0]])
                    for hp in range(16):
                        b = 1 + hp // 4
                        S.op("pe", lambda e: e.transpose(out=PS[:, b, (hp % 4) * 128:(hp % 4 + 1) * 128],
                                                         in_=stg[0][:, hp * 128:(hp + 1) * 128], identity=identf[:]),
                             reads=[d_stg[0], d_const], writes=[dP[b]])
                    S.op("dve", lambda e: e.tensor_copy(out=SKT[:].rearrange("p (b h) k -> p b (h k)", b=4), in_=PS[:, 1:5, :]),
                         reads=[dP[1], dP[2], dP[3], dP[4]], writes=[d_WC])
                S.barrier()

            x_t = [SB(sb_, "xB%d" % i, [128, D], F32) for i in range(2)]
            d_x = [Dep(), Dep()]
            st1 = SB(sb_, "st1B", [128, 8], F32); d_st1 = Dep()
            xn2 = [SB(sb_, "xnB%d" % i, [128, D], BF16) for i in range(2)]; d_xn2b = [Dep(), Dep()]
            xnT2 = [SB(sb_, "xnTB%d" % i, [128, NCH, T], BF16) for i in range(2)]; d_xnT2b = [Dep(), Dep()]
            fmoB = [SB(sb_, "fmoB%d" % i, [128, 2, NCH, T], BF16) for i in range(2)]
            d_fmoB = [Dep(), Dep()]
            sga = SB(sb_, "sga", [128, D], F32); d_sga = Dep()
            sgb = SB(sb_, "sgb", [128, D], F32); d_sgb = Dep()
            m1 = SB(sb_, "m1", [128, D], F32); d_m1 = Dep()
            m2 = sga; d_m2 = d_sga
            mtm = SB(sb_, "mtm", [128, D], BF16); d_mtm = Dep()
            mT = SB(sb_, "mT", [128, NCH, T], BF16); d_mT = Dep()

            def b_head(ti):
                par = ti % 2
                xn, d_xn, xnT, d_xnT = xn2[par], d_xn2b[par], xnT2[par], d_xnT2b[par]
                src = xp[(NPRE + ti) * T:(NPRE + ti + 1) * T, :] if ti < NMAIN else xs
                S.dma("sp", lambda e: e.dma_start(out=x_t[par][:], in_=src), writes=[d_x[par]])
                S.dma("act", lambda e: e.dma_start(out=fmoB[par][:].rearrange("p a c t -> p (a c t)"), in_=scr[ti]),
                      writes=[d_fmoB[par]])
                xtile, d_xtile = x_t[par][:], d_x[par]
                S.op("act", lambda e: e.activation(out=xn[:], in_=xtile, func=AF.Square, accum_out=st1[:, 0:1]),
                     reads=[d_xtile], writes=[d_xn, d_st1])
                S.op("act", lambda e: e.activation(out=st1[:, 1:2], in_=st1[:, 0:1], func=AF.Sqrt, scale=1.0 / D, bias=EPS),
                     reads=[d_st1], writes=[d_st1])
                S.op("dve", lambda e: e.reciprocal(out=st1[:, 2:3], in_=st1[:, 1:2]), reads=[d_st1], writes=[d_st1])
                S.op("act", lambda e: e.activation(out=xn[:], in_=xtile, func=AF.Copy, scale=st1[:, 2:3]),
                     reads=[d_xtile, d_st1], writes=[d_xn])
                for c in range(NCH):
                    S.op("pe", lambda e: e.transpose(out=pbank16(0)[:, c * 128:(c + 1) * 128], in_=xn[:, c * 128:(c + 1) * 128],
                                                     identity=identb[:]), reads=[d_xn, d_const], writes=[dP[0]])
                S.op("dve", lambda e: e.tensor_copy(out=xnT[:].rearrange("p c t -> p (c t)"), in_=pbank16(0)),
                     reads=[dP[0]], writes=[d_xnT])

            def b_part1(ti):
                par = ti % 2
                xnT, d_xnT = xnT2[par], d_xnT2b[par]
                for gi, (sg, dsg) in enumerate(((sga, d_sga), (sgb, d_sgb))):
                    for j in range(2):
                        b = 1 + gi * 2 + j
                        for k in range(NCH):
                            S.op("pe", lambda e: e.matmul(PS[:, b, :], lhsT=xnT[:, k, :],
                                                          rhs=WgB[:, k, gi * D + j * 512: gi * D + (j + 1) * 512],
                                                          start=(k == 0), stop=(k == NCH - 1)),
                                 reads=[d_WB, d_xnT], writes=[dP[b]])
                    b0 = 1 + gi * 2
                    S.op("act", lambda e: e.activation(out=sg[:].rearrange("p (b x) -> p b x", b=2), in_=PS[:, b0:b0 + 2, :], func=AF.Sigmoid),
                         reads=[dP[b0], dP[b0 + 1]], writes=[dsg])
                for bi, (W_, banks, sg, dsg, mm, dmm) in enumerate(((Wl, (5, 6), sga, d_sga, m1, d_m1), (Wa, (7, 0), sgb, d_sgb, m2, d_m2))):
                    for j in range(2):
                        b = banks[j]
                        for k in range(NCH):
                            S.op("pe", lambda e: e.matmul(PS[:, b, :], lhsT=fmoB[par][:, bi, k, :], rhs=W_[:, k, j * 512:(j + 1) * 512],
                                                          start=(k == 0), stop=(k == NCH - 1)),
                                 reads=[d_WB, d_fmoB[par]], writes=[dP[b]])
                        S.op("dve", lambda e: e.tensor_tensor(out=mm[:, j * 512:(j + 1) * 512], in0=PS[:, b, :], in1=sg[:, j * 512:(j + 1) * 512],
                                                              op=ALU.mult), reads=[dP[b], dsg], writes=[dmm])
                S.op("dve", lambda e: e.tensor_tensor(out=mtm[:], in0=m1[:], in1=m2[:], op=ALU.add),
                     reads=[d_m1, d_m2], writes=[d_mtm])

            def b_part2(ti):
                par = ti % 2
                for c in range(NCH):
                    S.op("pe", lambda e: e.transpose(out=pbank16(1)[:, c * 128:(c + 1) * 128], in_=mtm[:, c * 128:(c + 1) * 128],
                                                     identity=identb[:]), reads=[d_mtm, d_const], writes=[dP[1]])
                S.op("act", lambda e: e.activation(out=mT[:].rearrange("p c t -> p (c t)"), in_=pbank16(1), func=AF.Copy),
                     reads=[dP[1]], writes=[d_mT])
                for j in range(2):
                    for k in range(NCH):
                        S.op("pe", lambda e: e.matmul(PS[:, 2 + j, :], lhsT=mT[:, k, :], rhs=Wo[:, k, j * 512:(j + 1) * 512],
                                                      start=(k == 0), stop=(k == NCH - 1)),
                             reads=[d_WB, d_mT], writes=[dP[2 + j]])
                S.op("dve", lambda e: e.tensor_tensor(out=x_t[par][:].rearrange("p (b x) -> p b x", b=2),
                                                      in0=PS[:, 2:4, :], in1=x_t[par][:].rearrange("p (b x) -> p b x", b=2),
                                                      op=ALU.add),
                     reads=[dP[2], dP[3], d_x[par]], writes=[d_x[par]])
                S.dma("pool", lambda e: e.dma_start(out=hres_scr[ti], in_=x_t[par][:]), reads=[d_x[par]], writes=[d_hres[ti]])
                if STAGE == 2:
                    dst = y_p[ti * T:(ti + 1) * T, :] if ti < NMAIN else y_s
                    S.dma("sp", lambda e: e.dma_start(out=dst, in_=x_t[par][:]), reads=[d_x[par]], is_out=True)

            b_head(0)
            for ti in range(NT):
                b_part1(ti)
                if ti + 1 < NT:
                    b_head(ti + 1)
                b_part2(ti)
            S.barrier()

        PG = 4
        if STAGE >= 3:
          if True:
            with ExitStack() as sc_:
                hr_c = [SB(sc_, "hr_c%d" % i, [128, D], F32) for i in range(2)]
                d_hrc = [Dep(), Dep()]
                junk = SB(sc_, "junkC", [128, D], F32); d_junk = Dep()
                st1 = SB(sc_, "st1C", [128, 8], F32); d_st1 = Dep()
                xn2 = SB(sc_, "xn2", [128, D], BF16); d_xn2 = Dep()
                qrT = SB(sc_, "qrT", [128, 16, T], BF16); d_qrT = Dep()
                scs2 = [SB(sc_, "scs%d" % i, [128, 16, 128], F32) for i in range(2)]; d_scs2 = [Dep(), Dep()]
                d_tvr = [Dep() for _ in range(16)]; d_tiur = [Dep() for _ in range(16)]; d_wrkr = [Dep() for _ in range(16)]
                d_bestr = [Dep() for _ in range(8)]; d_posur = [Dep() for _ in range(8)]; d_cwkr = [Dep() for _ in range(8)]
                wrk = SB(sc_, "wrk", [128, 16, 128], F32); d_wrk = Dep()
                tv = SB(sc_, "tv", [128, 16, 16], F32); d_tv = Dep()
                tiu = SB(sc_, "tiu", [128, 16, 16], U32); d_tiu = Dep()
                tif = SB(sc_, "tif", [128, 16, 16], F32); d_tif = Dep()
                cand = SB(sc_, "cand", [128, 8, 256], F32); d_cand = Dep()
                cwk = SB(sc_, "cwk", [128, 8, 256], F32); d_cwk = Dep()
                best = SB(sc_, "best", [128, 8, 16], F32); d_best = Dep()
                posu = SB(sc_, "posu", [128, 8, 16], U32); d_posu = Dep()
                abu = SB(sc_, "abu", [128, 2, 128], U32); d_abu = Dep()
                abf = SB(sc_, "abf", [128, 2, 128], F32); d_abf = Dep()
                oh = [SB(sc_, "oh%d" % i, [128, 128, 16], F32) for i in range(2)]
                d_oh = [Dep(), Dep()]
                sel = SB(sc_, "sel", [128, 3, 128], F32); d_sel = Dep()
                gsm = SB(sc_, "gsm", [128, 8], F32); d_gsm = Dep()
                iota16 = SB(sc_, "iota16", [128, 16], F32)
                S.op("pool", lambda e: e.iota(iota16[:], pattern=[[1, 16]], base=0, channel_multiplier=0,
                                              allow_small_or_imprecise_dtypes=True), writes=[d_WC])
                gat = sel[:, 2, :].rearrange("p (h k) -> p h k", k=16)

                def c1a_head(ti):
                    par = ti % 2
                    tok0 = ti * T
                    S.dma("sp", lambda e: e.dma_start(out=hr_c[par][:], in_=hres_scr[ti]), reads=[d_hres[ti]], writes=[d_hrc[par]])
                    hres = hr_c[par][:]
                    S.op("act", lambda e: e.activation(out=junk[:], in_=hres, func=AF.Square, accum_out=st1[:, 0:1]),
                         reads=[d_hrc[par]], writes=[d_junk, d_st1])
                    S.op("act", lambda e: e.activation(out=st1[:, 1:2], in_=st1[:, 0:1], func=AF.Sqrt, scale=1.0 / D, bias=EPS),
                         reads=[d_st1], writes=[d_st1])
                    S.op("dve", lambda e: e.reciprocal(out=st1[:, 2:3], in_=st1[:, 1:2]), reads=[d_st1], writes=[d_st1])
                    S.op("act", lambda e: e.activation(out=xn2[:], in_=hres, func=AF.Copy, scale=st1[:, 2:3]),
                         reads=[d_hrc[par], d_st1], writes=[d_xn2])
                    for c in range(NCH):
                        S.op("pe", lambda e: e.transpose(out=pbank16(0)[:, c * 128:(c + 1) * 128], in_=xn2[:, c * 128:(c + 1) * 128],
                                                         identity=identb[:]), reads=[d_xn2, d_const], writes=[dP[0]])
                    S.op("act", lambda e: e.activation(out=xn2T_all[:, :, tok0:tok0 + T],
                                                       in_=pbank16(0).rearrange("p (c t) -> p c t", c=NCH), func=AF.Copy),
                         reads=[dP[0]], writes=[d_xn2T_all[ti]])
                    for hp in range(16):
                        b = 1 + hp // 4
                        for k in range(NCH):
                            S.op("pe", lambda e: e.matmul(PS[:, b, (hp % 4) * 128:(hp % 4 + 1) * 128],
                                                          lhsT=Wq[:, k, hp * 128:(hp + 1) * 128], rhs=xn2T_all[:, k, tok0:tok0 + T],
                                                          start=(k == 0), stop=(k == NCH - 1)),
                                 reads=[d_WC, d_xn2T_all[ti]], writes=[dP[b]])
                    S.op("act", lambda e: e.activation(out=qrT[:, 0:8, :].rearrange("p (b h) t -> p b (h t)", b=2), in_=PS[:, 1:3, :],
                                                       func=AF.Copy), reads=[dP[1], dP[2]], writes=[d_qrT])
                    S.op("act", lambda e: e.activation(out=qrT[:, 8:16, :].rearrange("p (b h) t -> p b (h t)", b=2), in_=PS[:, 3:5, :],
                                                       func=AF.Copy), reads=[dP[3], dP[4]], writes=[d_qrT])
                    sbanks = (5, 6, 7, 0)
                    for hp in range(16):
                        b = sbanks[hp // 4]
                        S.op("pe", lambda e: e.matmul(PS[:, b, (hp % 4) * 128:(hp % 4 + 1) * 128], lhsT=qrT[:, hp, :], rhs=SKT[:, hp, :],
                                                      start=True, stop=True), reads=[d_qrT, d_WC], writes=[dP[b]])
                    scs = scs2[par]
                    d_scs = d_scs2[par]
                    for q4 in range(4):
                        b = sbanks[q4]
                        S.op("act", lambda e: e.activation(out=scs[:, q4 * 4:(q4 + 1) * 4, :].rearrange("p h k -> p (h k)"), in_=PS[:, b, :], func=AF.Copy),
                             reads=[dP[b]], writes=[d_scs])

                def c1a_body(ti):
                    par = ti % 2
                    tok0 = ti * T
                    scs = scs2[par]
                    d_scs = d_scs2[par]
                    for hp in range(16):
                        S.op("dve", lambda e: e.max(out=tv[:, hp, 0:8], in_=scs[:, hp, :]), reads=[d_scs], writes=[d_tvr[hp]])
                    for hp in range(16):
                        S.op("dve", lambda e: e.max_index(out=tiu[:, hp, 0:8], in_max=tv[:, hp, 0:8], in_values=scs[:, hp, :]),
                             reads=[d_scs, d_tvr[hp]], writes=[d_tiur[hp]])
                    for hp in range(16):
                        S.op("dve", lambda e: e.match_replace(out=wrk[:, hp, :], in_to_replace=tv[:, hp, 0:8], in_values=scs[:, hp, :],
                                                              imm_value=-1e30), reads=[d_scs, d_tvr[hp]], writes=[d_wrkr[hp]])
                    for hp in range(16):
                        S.op("dve", lambda e: e.max(out=tv[:, hp, 8:16], in_=wrk[:, hp, :]), reads=[d_wrkr[hp]], writes=[d_tvr[hp]])
                    for hp in range(16):
                        S.op("dve", lambda e: e.max_index(out=tiu[:, hp, 8:16], in_max=tv[:, hp, 8:16], in_values=wrk[:, hp, :]),
                             reads=[d_wrkr[hp], d_tvr[hp]], writes=[d_tiur[hp]])
                    S.op("dve", lambda e: e.tensor_copy(out=tif[:], in_=tiu[:]), reads=d_tiur, writes=[d_tif])
                    tvv = tv[:].rearrange("p (h two) k -> p h two k", two=2)
                    S.op("dve", lambda e: e.tensor_tensor(out=cand[:].rearrange("p h (a b) -> p h a b", b=16),
                                                          in0=tvv[:, :, 0, :].unsqueeze(3).to_broadcast([128, 8, 16, 16]),
                                                          in1=tvv[:, :, 1, :].unsqueeze(2).to_broadcast([128, 8, 16, 16]), op=ALU.add),
                         reads=d_tvr, writes=[d_cand])
                    for h in range(8):
                        S.op("dve", lambda e: e.max(out=best[:, h, 0:8], in_=cand[:, h, :]), reads=[d_cand], writes=[d_bestr[h]])
                    for h in range(8):
                        S.op("dve", lambda e: e.max_index(out=posu[:, h, 0:8], in_max=best[:, h, 0:8], in_values=cand[:, h, :]),
                             reads=[d_cand, d_bestr[h]], writes=[d_posur[h]])
                    for h in range(8):
                        S.op("dve", lambda e: e.match_replace(out=cwk[:, h, :], in_to_replace=best[:, h, 0:8], in_values=cand[:, h, :],
                                                              imm_value=-1e30), reads=[d_cand, d_bestr[h]], writes=[d_cwkr[h]])
                    for h in range(8):
                        S.op("dve", lambda e: e.max(out=best[:, h, 8:16], in_=cwk[:, h, :]), reads=[d_cwkr[h]], writes=[d_bestr[h]])
                    for h in range(8):
                        S.op("dve", lambda e: e.max_index(out=posu[:, h, 8:16], in_max=best[:, h, 8:16], in_values=cwk[:, h, :]),
                             reads=[d_cwkr[h], d_bestr[h]], writes=[d_posur[h]])
                    d_posu_l = d_posur
                    d_best_l = d_bestr
                    pflat = posu[:].rearrange("p h k -> p (h k)")
                    S.op("dve", lambda e: e.tensor_single_scalar(out=abu[:, 0, :], in_=pflat, scalar=4, op=ALU.logical_shift_right),
                         reads=d_posu_l, writes=[d_abu])
                    S.op("dve", lambda e: e.tensor_single_scalar(out=abu[:, 1, :], in_=pflat, scalar=15, op=ALU.bitwise_and),
                         reads=d_posu_l, writes=[d_abu])
                    S.op("dve", lambda e: e.tensor_copy(out=abf[:], in_=abu[:]), reads=[d_abu], writes=[d_abf])
                    tfv = tif[:].rearrange("p (h two) k -> p h two k", two=2)
                    for w in range(2):
                        S.op("dve",
                             lambda e: e.tensor_tensor(out=oh[w][:], in0=iota16[:].unsqueeze(1).to_broadcast([128, 128, 16]),
                                                       in1=abf[:, w, :].unsqueeze(2).to_broadcast([128, 128, 16]), op=ALU.is_equal),
                             reads=[d_abf, d_WC], writes=[d_oh[w]])
                    for w in range(2):
                        ohv = oh[w][:].rearrange("p (h j) a -> p h j a", j=16)
                        S.op("dve",
                             lambda e: e.tensor_tensor(out=ohv, in0=ohv,
                                                       in1=tfv[:, :, w, :].unsqueeze(2).to_broadcast([128, 8, 16, 16]), op=ALU.mult),
                             reads=[d_oh[w], d_tif], writes=[d_oh[w]])
                    S.op("dve", lambda e: e.tensor_tensor(out=gat, in0=best[:], in1=best[:, :, 0:1].to_broadcast([128, 8, 16]),
                                                          op=ALU.subtract), reads=d_best_l, writes=[d_sel])
                    S.op("act", lambda e: e.activation(out=gat, in_=gat, func=AF.Exp), reads=[d_sel], writes=[d_sel])
                    S.op("dve", lambda e: e.tensor_reduce(out=gsm[:], in_=gat, axis=AX.X, op=ALU.add), reads=[d_sel], writes=[d_gsm])
                    S.op("dve", lambda e: e.reciprocal(out=gsm[:], in_=gsm[:]), reads=[d_gsm], writes=[d_gsm])
                    S.op("dve", lambda e: e.tensor_tensor(out=gat, in0=gat, in1=gsm[:].unsqueeze(2).to_broadcast([128, 8, 16]),
                                                          op=ALU.mult), reads=[d_sel, d_gsm], writes=[d_sel])
                    for w in range(2):
                        S.op("dve", lambda e: e.tensor_reduce(out=sel[:, w, :], in_=oh[w][:], axis=AX.X, op=ALU.add),
                             reads=[d_oh[w]], writes=[d_sel])
                    for k in range(3):
                        S.op("pe", lambda e: e.transpose(out=PS[:, 7, k * 128:(k + 1) * 128], in_=sel[:, k, :], identity=identf[:]),
                             reads=[d_sel, d_const], writes=[dP[7]])
                    S.op("act", lambda e: e.activation(out=slotT[:, :, tok0:tok0 + T], in_=PS[:, 7, 0:384].rearrange("p (k t) -> p k t", k=3),
                                                       func=AF.Copy), reads=[dP[7]], writes=[d_slotT[ti]])

                c1a_head(0)
                for ti in range(NT):
                    if ti + 1 < NT:
                        c1a_head(ti + 1)
                    c1a_body(ti)
                S.barrier()

            swq.close()
            with ExitStack() as sd_:
                iotaC = SB(sd_, "iotaC", [128, 128], BF16)
                d_io = Dep()
                S.op("pool", lambda e: e.iota(iotaC[:], pattern=[[1, 128]], base=0, channel_multiplier=0,
                                              allow_small_or_imprecise_dtypes=True), writes=[d_io])
                HT = T // 2
                OH1 = [SB(sd_, "OH1_%d" % i, [128, 128, HT], BF16) for i in range(2)]; d_OH1 = [Dep(), Dep()]
                OH2 = [SB(sd_, "OH2_%d" % i, [128, 128, HT], BF16) for i in range(2)]; d_OH2 = [Dep(), Dep()]
                iotaR = SB(sd_, "iotaR", [128, 128, HT], BF16)
                S.op("dve", lambda e: e.tensor_copy(out=iotaR[:], in_=iotaC[:].unsqueeze(2).to_broadcast([128, 128, HT])),
                     reads=[d_io], writes=[d_io])
                Gt = [SB(sd_, "Gt%d" % i, [128, 128, T], BF16) for i in range(2)]
                d_Gt = [Dep(), Dep()]
                ecnt = [0]

                def c1b_build(hi):
                    ti, hf = hi // 2, hi % 2
                    hb_ = hi % 2
                    th0 = ti * T + hf * HT
                    S.op("dve", lambda e: e.tensor_tensor(out=OH1[hb_][:], in0=iotaR[:],
                                                          in1=slotT[:, 0, th0:th0 + HT].unsqueeze(1).to_broadcast([128, 128, HT]),
                                                          op=ALU.is_equal), reads=[d_io, d_slotT[ti]], writes=[d_OH1[hb_]])
                    S.op("dve", lambda e: e.tensor_tensor(out=OH2[hb_][:], in0=iotaR[:],
                                                          in1=slotT[:, 1, th0:th0 + HT].unsqueeze(1).to_broadcast([128, 128, HT]),
                                                          op=ALU.is_equal), reads=[d_io, d_slotT[ti]], writes=[d_OH2[hb_]])
                    S.op("dve", lambda e: e.tensor_tensor(out=OH2[hb_][:], in0=OH2[hb_][:],
                                                          in1=slotT[:, 2, th0:th0 + HT].unsqueeze(1).to_broadcast([128, 128, HT]),
                                                          op=ALU.mult), reads=[d_OH2[hb_], d_slotT[ti]], writes=[d_OH2[hb_]])

                def c1b_mm(hi):
                    ti, hf = hi // 2, hi % 2
                    par = ti % 2
                    hb_ = hi % 2
                    tok0 = ti * T
                    for t4 in range(HT // 4):
                        b = ecnt[0] % 4
                        ecnt[0] += 1
                        for tt in range(4):
                            t = t4 * 4 + tt
                            if PE_STRIDED:
                                o_ap = PS[:, b, :].rearrange("c (p t) -> c t p", t=4)[:, tt, :]
                            else:
                                o_ap = PS[:, b, tt * 128:(tt + 1) * 128]
                            S.op("pe", lambda e: e.matmul(o_ap, lhsT=OH1[hb_][:, :, t], rhs=OH2[hb_][:, :, t],
                                                          start=True, stop=True), reads=[d_OH1[hb_], d_OH2[hb_]], writes=[dP[b]])
                        tg = hf * HT + t4 * 4
                        if PE_STRIDED:
                            dst = Gt[par][:, :, tg:tg + 4]
                            src = PS[:, b, :].rearrange("c (p t) -> c p t", t=4)
                        else:
                            dst = Gt[par][:, :, tg:tg + 4].rearrange("c p t -> c t p")
                            src = PS[:, b, :].rearrange("c (t p) -> c t p", t=4)
                        S.op("act", lambda e: e.activation(out=dst, in_=src, func=AF.Copy), reads=[dP[b]], writes=[d_Gt[par]])
                    if hf == 1:
                        for q in range(4):
                            S.dma("sp" if q % 2 == 0 else "pool",
                                  lambda e: e.dma_start(out=G_scr[q * 32:(q + 1) * 32, :, tok0:tok0 + T].rearrange("p c t -> c p t"),
                                                        in_=Gt[par][:, q * 32:(q + 1) * 32, :]),
                                  reads=[d_Gt[par]])

                c1b_build(0)
                for hi in range(2 * NT):
                    if hi + 1 < 2 * NT:
                        c1b_build(hi + 1)
                    c1b_mm(hi)
                S.barrier()

            with ExitStack() as se_:
                NGRP = 128 // PG
                out_acc = SB(se_, "out_acc", [128, NT, D], F32); d_acc = [Dep() for _ in range(NT)]
                su = [SB(se_, "su%d" % i, [128, D], F32) for i in range(2)] * 2; d_su = [Dep(), Dep()] * 2
                sv = [SB(se_, "sv%d" % i, [128, D], F32) for i in range(2)]; d_sv = [Dep(), Dep()]
                ub = [SB(se_, "ub%d" % i, [128, D], BF16) for i in range(2)] * 2; d_ub = [Dep(), Dep()] * 2
                UT = [SB(se_, "UT%d" % i, [128, PG, NCH, 128], BF16) for i in range(2)]; d_UT = [Dep(), Dep()]
                Vb = [SB(se_, "Vb%d" % i, [128, PG, D], BF16) for i in range(2)]; d_Vb = [Dep(), Dep()]
                Gp = [slotT[:, i, :] for i in range(3)]; d_Gp = [Dep() for _ in range(3)]
                Ab = [SB(se_, "Ab%d" % i, [128, 512], BF16) for i in range(2)]; d_Ab = [Dep(), Dep()]
                A2 = [SB(se_, "A_%d" % i, [128, PG, NTOK], BF16) for i in range(2)]
                d_A2 = [[Dep() for _ in range(PG)] for _ in range(2)]
                eu_v = eu_d.rearrange("(c p) d -> p c d", p=128)
                ev_v = ev_d.rearrange("(c p) d -> p c d", p=128)
                chunks = [(t0, min(512, NTOK - t0)) for t0 in range(0, NTOK, 512)]
                pcount = [0]

                def run_gens2(gens):
                    gens = list(gens)
                    while gens:
                        for g_ in list(gens):
                            try:
                                next(g_)
                            except StopIteration:
                                gens.remove(g_)

                def prepU(g):
                    gb = g % 2
                    for half_ in range(PG // 2):
                        for k in range(2):
                            pi = half_ * 2 + k
                            p = g * PG + pi
                            S.dma("sp", lambda e: e.dma_start(out=su[k][:], in_=eu_v[p]), writes=[d_su[k]])
                            S.op("act", lambda e: e.activation(out=ub[k][:], in_=su[k][:], func=AF.Copy), reads=[d_su[k]], writes=[d_ub[k]])
                        yield
                        yield
                        yield
                        for k in range(2):
                            pi = half_ * 2 + k
                            for dc in range(NCH):
                                S.op("pe", lambda e: e.transpose(out=pbank16(0)[:, dc * 128:(dc + 1) * 128], in_=ub[k][:, dc * 128:(dc + 1) * 128],
                                                                 identity=identb[:]), reads=[d_ub[k], d_const], writes=[dP[0]])
                            S.op("dve", lambda e: e.tensor_copy(out=UT[gb][:, pi, :, :].rearrange("p c x -> p (c x)"), in_=pbank16(0)),
                                 reads=[dP[0]], writes=[d_UT[gb]])
                            yield

                def prepV(g):
                    gb = g % 2
                    for pi in range(PG):
                        p = g * PG + pi
                        k = vcount[0] % 2
                        vcount[0] += 1
                        S.dma("act", lambda e: e.dma_start(out=sv[k][:], in_=ev_v[p]), writes=[d_sv[k]])
                        S.op("act", lambda e: e.activation(out=Vb[gb][:, pi, :], in_=sv[k][:], func=AF.Copy), reads=[d_sv[k]], writes=[d_Vb[gb]])

                gcnt = [0]
                hcnt = [0]

                def H_gen(g):
                    gb = g % 2
                    A_ = A2[gb]
                    for pi in range(PG):
                        p = g * PG + pi
                        gk = gcnt[0] % 3
                        gcnt[0] += 1
                        S.dma("pool", lambda e: e.dma_start(out=Gp[gk], in_=G_scr[p]), writes=[d_Gp[gk]])
                        for (t0, n) in chunks:
                            hb = 1 + hcnt[0] % 2
                            ak = hcnt[0] % 2
                            hcnt[0] += 1
                            for dc in range(NCH):
                                S.op("pe", lambda e: e.matmul(PS[:, hb, 0:n], lhsT=UT[gb][:, pi, dc, :], rhs=xn2T_all[:, dc, t0:t0 + n],
                                                              start=(dc == 0), stop=(dc == NCH - 1)),
                                     reads=[d_UT[gb]], writes=[dP[hb]])
                            S.op("act", lambda e: e.activation(out=Ab[ak][:, 0:n], in_=PS[:, hb, 0:n], func=AF.Gelu_apprx_tanh),
                                 reads=[dP[hb]], writes=[d_Ab[ak]])
                            S.op("dve", lambda e: e.tensor_tensor(out=A_[:, pi, t0:t0 + n], in0=Ab[ak][:, 0:n], in1=Gp[gk][:, t0:t0 + n],
                                                                  op=ALU.mult), reads=[d_Ab[ak], d_Gp[gk]], writes=[d_A2[gb][pi]])
                            yield

                def V_gen(g):
                    gb = g % 2
                    A_ = A2[gb]
                    for s_ in range(NT):
                        ab_ = 3 + 2 * (s_ % 2)
                        for j in range(2):
                            for pi in range(PG):
                                S.op("pe", lambda e: e.matmul(PS[:, ab_ + j, :], lhsT=A_[:, pi, s_ * T:(s_ + 1) * T],
                                                              rhs=Vb[gb][:, pi, j * 512:(j + 1) * 512],
                                                              start=(pi == 0), stop=(pi == PG - 1)),
                                     reads=[d_A2[gb][pi], d_Vb[gb]], writes=[dP[ab_ + j]])
                        S.op("dve", lambda e: e.tensor_tensor(out=out_acc[:, s_, :].rearrange("p (b x) -> p b x", b=2),
                                                              in0=PS[:, ab_:ab_ + 2, :],
                                                              in1=out_acc[:, s_, :].rearrange("p (b x) -> p b x", b=2), op=ALU.add),
                             reads=[dP[ab_], dP[ab_ + 1], d_acc[s_]], writes=[d_acc[s_]])
                        yield

                vcount = [0]
                run_gens2([prepU(0)])
                prepV(0)
                run_gens2([prepU(1)])
                prepV(1)
                for ti in range(NT):
                    S.dma("sp" if ti % 2 == 0 else "act", lambda e: e.dma_start(out=out_acc[:, ti, :], in_=hres_scr[ti]),
                          reads=[d_hres[ti]], writes=[d_acc[ti]])
                run_gens2([H_gen(0)])
                for g in range(NGRP):
                    gens = [V_gen(g)]
                    if g + 1 < NGRP:
                        gens.append(H_gen(g + 1))
                    if g + 2 < NGRP:
                        gens.append(prepU(g + 2))
                    run_gens2(gens)
                    if g + 2 < NGRP:
                        prepV(g + 2)
                for ti in range(NT):
                    dst = y_p[ti * T:(ti + 1) * T, :] if ti < NMAIN else y_s
                    S.dma("sp" if ti % 2 == 0 else "act", lambda e: e.dma_start(out=dst, in_=out_acc[:, ti, :]),
                          reads=[d_acc[ti]], is_out=True)
                S.barrier()

        if STAGE < 2:
            with ExitStack() as sz:
                z = SB(sz, "z", [128, D], F32)
                dz = Dep()
                S.op("pool", lambda e: e.memset(z[:], 0.0), writes=[dz])
                for mi in range(NMAIN):
                    S.dma("sp", lambda e: e.dma_start(out=y_p[mi * T:(mi + 1) * T, :], in_=z[:]), reads=[dz], is_out=True)
                S.dma("sp", lambda e: e.dma_start(out=y_s, in_=z[:]), reads=[dz], is_out=True)
                S.barrier()

        S.finish()
    print("instructions:", S.ninst, {k: S.cnt[k] for k in S.cnt}, S.dcnt)
    return nc


_CACHE = {}


def _tables():
    if "t" in _CACHE:
        return _CACHE["t"]
    slopes = 2.0 ** (-8.0 * np.arange(1, NH + 1, dtype=np.float64) / NH)
    ki = np.arange(128)[:, None]
    qi = np.arange(128)[None, :]
    E_pr = np.zeros((128, NH, 2, 128), np.float64)
    for h in range(NH):
        d_prev = qi - ki + 128
        E_pr[:, h, 0, :] = np.where(ki >= qi, np.exp(-slopes[h] * d_prev), 0.0)
        d_own = qi - ki
        E_pr[:, h, 1, :] = np.where(ki <= qi, np.exp(-slopes[h] * d_own), 0.0)
    ks_, kt_ = ki // DEC, ki % DEC
    qs_, qt_ = qi // DEC, qi % DEC
    E_so = np.zeros((128, NH, 128), np.float64)
    for h in range(NH):
        E_so[:, h, :] = np.where((ks_ == qs_) & (kt_ <= qt_), np.exp(-slopes[h] * (qt_ - kt_)), 0.0)
    E_sc = np.zeros((128, NH, DEC), np.float64)
    j = np.arange(128)[:, None]
    t = np.arange(DEC)[None, :]
    for h in range(NH):
        E_sc[:, h, :] = np.where(j >= t, np.exp(-slopes[h] * (t + 128 - j)), 0.0)
    tb = dict(E_pr=E_pr.reshape(128, -1).astype(np.float32), E_so=E_so.reshape(128, -1).astype(np.float32),
              E_sc=E_sc.reshape(128, -1).astype(np.float32), identf=np.eye(128, dtype=np.float32))
    _CACHE["t"] = tb
    return tb


def kernel(x_prompt, x_sample, cache_conv, state_lru, cache_k, cache_v, norm1_g, w_in, conv_w, conv_b,
           rg_w_a, rg_b_a, rg_w_x, rg_b_x, rg_lambda, q_norm_g, k_norm_g, attn_sinks, w_branch_lru,
           w_branch_attn, w_out, norm2_g, peer_w_query, peer_sub_keys, expert_u, expert_v):
    f = lambda a: np.ascontiguousarray(np.asarray(a, dtype=np.float32))
    x_prompt, x_sample = f(x_prompt), f(x_sample)
    tb = _tables()
    prow = np.zeros((16, D), np.float32)
    prow[0:4] = f(conv_w)[0]
    prow[4] = f(conv_b)[0]
    prow[5] = f(rg_b_a)[0]
    prow[6] = f(rg_b_x)[0]
    prow[7] = f(rg_lambda)[0]
    prow[8] = f(norm1_g)[0]
    prow[9] = f(norm2_g)[0]
    shared = dict(
        identf=tb["identf"], E_pr=tb["E_pr"], E_so=tb["E_so"], E_sc=tb["E_sc"], prow=prow,
        w_in=f(w_in)[0], rgw=np.stack([f(rg_w_a)[0], f(rg_w_x)[0]]),
        qkg=np.stack([f(q_norm_g)[0], f(k_norm_g)[0]]), sinks=f(attn_sinks)[0].reshape(1, NH),
        w_bl=f(w_branch_lru)[0], w_ba=f(w_branch_attn)[0], w_o=f(w_out)[0], w_q=f(peer_w_query)[0],
        subk=f(peer_sub_keys)[0].reshape(16, 128, 128),
    )
    if STAGE >= 3 and not DEBUG_C:
        shared.update(eu=f(expert_u)[0], ev=f(expert_v)[0])
    cc, sl, ckk, cvv = f(cache_conv)[0], f(state_lru)[0], f(cache_k)[0], f(cache_v)[0]
    in_maps = []
    for c in range(NCORE):
        s, half = c // 2, c % 2
        xpc = np.zeros(((NPRE + NMAIN) * T, D), np.float32)
        if half == 1:
            xpc[:NPRE * T] = x_prompt[s, :2048]
        xpc[NPRE * T:] = x_prompt[s, half * 2048:(half + 1) * 2048]
        m = dict(shared)
        m.update(
            xp=xpc, xs=x_sample[c * NSEQ:(c + 1) * NSEQ].reshape(T, D),
            cconv=cc[c * NSEQ:(c + 1) * NSEQ].reshape(NSEQ * 3, D), slru=sl[c * NSEQ:(c + 1) * NSEQ],
            ck=ckk[c * NSEQ:(c + 1) * NSEQ].reshape(NSEQ, 128, 256), cv=cvv[c * NSEQ:(c + 1) * NSEQ].reshape(NSEQ, 128, 256),
            flag=np.full((128, 1), float(half), np.float32),
        )
        in_maps.append(m)
    if "nc" not in _CACHE:
        _CACHE["nc"] = build_program()
    res = run_bass_kernel_spmd(_CACHE["nc"], in_maps, core_ids=list(range(NCORE)))
    R = res.results
    y_prompt = np.zeros((4, 4096, D), np.float32)
    y_sample = np.zeros((128, DEC, D), np.float32)
    conv_pr = np.zeros((1, 4, 3, D), np.float32)
    lru_pr = np.zeros((1, 4, D), np.float32)
    k_pr = np.zeros((1, 4, 128, NKV, HD), np.float32)
    v_pr = np.zeros((1, 4, 128, NKV, HD), np.float32)
    conv_sa = np.zeros((1, 128, 3, D), np.float32)
    lru_sa = np.zeros((1, 128, D), np.float32)
    k_sa = np.zeros((1, 128, 128, NKV, HD), np.float32)
    v_sa = np.zeros((1, 128, 128, NKV, HD), np.float32)
    for c in range(NCORE):
        s, half = c // 2, c % 2
        r = R[c]
        y_prompt[s, half * 2048:(half + 1) * 2048] = r["y_p"]
        y_sample[c * NSEQ:(c + 1) * NSEQ] = r["y_s"].reshape(NSEQ, DEC, D)
        if half == 1:
            conv_pr[0, s] = r["conv_p"]
            lru_pr[0, s] = r["lru_p"][0]
            k_pr[0, s] = r["k_p"].reshape(128, NKV, HD)
            v_pr[0, s] = r["v_p"].reshape(128, NKV, HD)
        conv_sa[0, c * NSEQ:(c + 1) * NSEQ] = r["conv_s"].reshape(NSEQ, 3, D)
        lru_sa[0, c * NSEQ:(c + 1) * NSEQ] = r["lru_s"]
        k_sa[0, c * NSEQ:(c + 1) * NSEQ] = r["k_s"].reshape(NSEQ, 128, NKV, HD)
        v_sa[0, c * NSEQ:(c + 1) * NSEQ] = r["v_s"].reshape(NSEQ, 128, NKV, HD)
    return (y_prompt, y_sample, conv_pr, lru_pr, k_pr, v_pr, conv_sa, lru_sa, k_sa, v_sa)
```

```python
import numpy as np
from contextlib import ExitStack
import concourse.bass as bass
import concourse.mybir as mybir
from concourse.bass_utils import run_bass_kernel_spmd

F32 = mybir.dt.float32
BF16 = mybir.dt.bfloat16
I32 = mybir.dt.int32
U32 = mybir.dt.uint32
AF = mybir.ActivationFunctionType
ALU = mybir.AluOpType
AX = mybir.AxisListType

NDS = 12
NCORE = 8
D = 1024
T = 128
NCH = 8
NPRE = 16
NMAIN = 16
NSEQ = 16
DEC = 8
NH = 16
NKV = 4
HD = 64
EPS = 1e-6
PAST = 16384
NEXP = 16384
STAGE = 3
DEBUG_C = False
PE_STRIDED = True
DEBUG_LVL = 9


class Dep:
    __slots__ = ("name", "w", "r")

    def __init__(self, name=""):
        self.name = name
        self.w = None
        self.r = {}


class Sched:
    def __init__(self, nc, stack):
        self.nc = nc
        self.engs = {"pe": nc.tensor, "dve": nc.vector, "act": nc.scalar,
                     "pool": nc.gpsimd, "sp": nc.sync}
        self.sem = {k: stack.enter_context(nc.semaphore("s_" + k)) for k in self.engs}
        self.cnt = {k: 0 for k in self.engs}
        self.waited = {k: {} for k in self.engs}
        self.dsem = {k: [stack.enter_context(nc.semaphore("d_%s%d" % (k, i))) for i in range(NDS)]
                     for k in ("sp", "act", "pool")}
        self.dcnt = {k: 0 for k in self.dsem}
        self.dlast = {}
        self.out_toks = []
        self.ninst = 0

    def _wait(self, e, tok):
        sem, val, key = tok
        if self.waited[e].get(key, 0) >= val:
            return
        self.engs[e].wait_ge(sem, val)
        self.waited[e][key] = val

    def _deps(self, e, reads, writes):
        toks = []
        for d in reads:
            if d.w is not None:
                toks.append(d.w)
        for d in writes:
            if d.w is not None and d.w[2] != e:
                toks.append(d.w)
            toks.extend(t for t in d.r.values() if t[2] != e)
        for t in toks:
            if e == "pe" and t[2] == "pe":
                continue
            if t[2] == e and e in ("dve", "act") and self.cnt[e] - t[1] >= 3:
                continue
            self._wait(e, t)

    def _commit(self, tok, reads, writes):
        for d in reads:
            d.r[tok[2]] = tok
        for d in writes:
            d.w = tok
            d.r = {}

    def op(self, e, fn, reads=(), writes=()):
        self._deps(e, reads, writes)
        inst = fn(self.engs[e])
        self.cnt[e] += 1
        inst.then_inc(self.sem[e], 1)
        self.ninst += 1
        self._commit((self.sem[e], self.cnt[e], e), reads, writes)
        return inst

    def dma(self, e, fn, reads=(), writes=(), is_out=False):
        i = self.dcnt[e]
        self.dcnt[e] += 1
        sem = self.dsem[e][i % NDS]
        key = (e, i % NDS)
        val = 16 * (i // NDS + 1)
        if val > 16:
            self._wait(e, (sem, val - 16, key))
        self._deps(e, reads, writes)
        inst = fn(self.engs[e])
        inst.then_inc(sem, 16)
        self.ninst += 1
        tok = (sem, val, key)
        self.dlast[key] = tok
        self._commit(tok, reads, writes)
        if is_out:
            self.out_toks.append(tok)
        return inst

    def barrier(self):
        toks = [(self.sem[k], self.cnt[k], k) for k in self.engs if self.cnt[k] > 0]
        toks += list(self.dlast.values())
        for e in self.engs:
            for t in toks:
                if t[2] == e:
                    continue
                self._wait(e, t)

    def finish(self):
        for t in self.out_toks:
            self._wait("sp", t)
        for k in self.engs:
            if k != "sp" and self.cnt[k] > 0:
                self._wait("sp", (self.sem[k], self.cnt[k], k))
        for t in self.dlast.values():
            self._wait("sp", t)


def build_program():
    nc = bass.Bass("TRN2", target_bir_lowering=False)

    def din(name, shape, dt=F32):
        return nc.dram_tensor(name, list(shape), dt, kind="ExternalInput").ap()

    def dout(name, shape, dt=F32):
        return nc.dram_tensor(name, list(shape), dt, kind="ExternalOutput").ap()

    xp = din("xp", [(NPRE + NMAIN) * T, D])
    xs = din("xs", [T, D])
    cconv = din("cconv", [NSEQ * 3, D])
    slru = din("slru", [NSEQ, D])
    ck = din("ck", [NSEQ, 128, 256])
    cv = din("cv", [NSEQ, 128, 256])
    flag_d = din("flag", [128, 1])
    identf_d = din("identf", [128, 128])
    E_pr_d = din("E_pr", [128, NH * 2 * 128])
    E_so_d = din("E_so", [128, NH * 128])
    E_sc_d = din("E_sc", [128, NH * DEC])
    prow_d = din("prow", [16, D])
    w_in_d = din("w_in", [D, 5632])
    rgw_d = din("rgw", [2, 16, 64, 64])
    qkg_d = din("qkg", [2, 64])
    sinks_d = din("sinks", [1, NH])
    w_bl_d = din("w_bl", [D, D])
    w_ba_d = din("w_ba", [D, D])
    w_o_d = din("w_o", [D, D])
    w_q_d = din("w_q", [D, 2048])
    subk_d = din("subk", [16, 128, 128])
    if STAGE >= 3 and not DEBUG_C:
        eu_d = din("eu", [NEXP, D])
        ev_d = din("ev", [NEXP, D])

    y_p = dout("y_p", [NMAIN * T, D])
    y_s = dout("y_s", [T, D])
    conv_p = dout("conv_p", [3, D])
    lru_p = dout("lru_p", [1, D])
    k_p = dout("k_p", [128, 256])
    v_p = dout("v_p", [128, 256])
    conv_s = dout("conv_s", [NSEQ * 3, D])
    lru_s = dout("lru_s", [NSEQ, D])
    k_s = dout("k_s", [NSEQ, 128, 256])
    v_s = dout("v_s", [NSEQ, 128, 256])

    NT = NMAIN + 1
    scr = nc.dram_tensor("scr", [NT, 128, 2 * NCH * T], BF16, kind="Internal").ap()
    hres_scr = nc.dram_tensor("hres_scr", [NT, 128, D], F32, kind="Internal").ap()
    NTOK = NT * T
    G_scr = nc.dram_tensor("G_scr", [128, 128, NTOK], BF16, kind="Internal").ap()

    with ExitStack() as st:
        S = Sched(nc, st)
        PS = st.enter_context(nc.psum_tensor("PS", [128, 8, 512], F32))
        dP = [Dep("P%d" % b) for b in range(8)]

        def pbank(b):
            return PS[:, b, :]

        def pbank16(b):
            return PS[:, b, :].bitcast(BF16)

        cst = ExitStack()
        st.enter_context(cst)

        def SB(stack, name, shape, dt):
            return stack.enter_context(nc.sbuf_tensor("sb_" + name, list(shape), dt))

        identf = SB(cst, "identf", [128, 128], F32)
        identb = SB(cst, "identb", [128, 128], BF16)
        flag = SB(cst, "flag", [128, 1], F32)
        cp = SB(cst, "cp", [128, NCH, 16], F32)
        d_const = Dep("const")
        S.dma("sp", lambda e: e.dma_start(out=identf[:], in_=identf_d), writes=[d_const])
        S.dma("sp", lambda e: e.dma_start(out=flag[:], in_=flag_d), writes=[d_const])
        S.op("dve", lambda e: e.tensor_copy(out=identb[:], in_=identf[:]), reads=[d_const], writes=[d_const])

        with ExitStack() as s0:
            prow = SB(s0, "prow", [16, D], F32)
            d_prow = Dep()
            S.dma("sp", lambda e: e.dma_start(out=prow[:], in_=prow_d), writes=[d_prow])
            for c in range(NCH):
                S.op("pe", lambda e: e.transpose(out=PS[:, 0, c * 16:(c + 1) * 16], in_=prow[:, c * 128:(c + 1) * 128],
                                                 identity=identf[0:16, 0:16]),
                     reads=[d_prow, d_const], writes=[dP[0]])
            S.op("dve", lambda e: e.tensor_copy(out=cp[:].rearrange("p c j -> p (c j)"), in_=PS[:, 0, 0:128]),
                 reads=[dP[0]], writes=[d_const])
            S.op("act", lambda e: e.activation(out=cp[:, :, 10], in_=cp[:, :, 7], func=AF.Exp, scale=-1.0),
                 reads=[d_const], writes=[d_const])
            S.op("act", lambda e: e.activation(out=cp[:, :, 10], in_=cp[:, :, 10], func=AF.Ln, bias=1.0),
                 reads=[d_const], writes=[d_const])
            S.op("dve", lambda e: e.tensor_scalar(out=cp[:, :, 10], in0=cp[:, :, 10], scalar1=-8.0, scalar2=None,
                                                  op0=ALU.mult), reads=[d_const], writes=[d_const])
            S.barrier()

        with ExitStack() as sa:
            WgA = SB(sa, "WgA", [128, NCH, 3584], BF16)
            d_WgA = Dep("WgA")
            with ExitStack() as s1:
                stg = [SB(s1, "stg%d" % i, [128, 3584], F32) for i in range(4)]
                d_stg = [Dep() for _ in range(4)]
                for kc in range(NCH):
                    b = kc % 4
                    S.dma("sp" if kc % 2 == 0 else "pool",
                          lambda e: e.dma_start(out=stg[b][:], in_=w_in_d[kc * 128:(kc + 1) * 128, 0:3584]),
                          writes=[d_stg[b]])
                    if kc % 2 == 0:
                        S.op("dve", lambda e: e.tensor_scalar(out=WgA[:, kc, :], in0=stg[b][:], scalar1=cp[:, kc, 8:9],
                                                              scalar2=None, op0=ALU.mult),
                             reads=[d_stg[b], d_const], writes=[d_WgA])
                    else:
                        S.op("act", lambda e: e.activation(out=WgA[:, kc, :], in_=stg[b][:], func=AF.Copy, scale=cp[:, kc, 8:9]),
                             reads=[d_stg[b], d_const], writes=[d_WgA])
                S.barrier()

            Wbd = SB(sa, "Wbd", [128, 2, NCH, 128], BF16)
            d_Wbd = Dep("Wbd")
            with ExitStack() as s1:
                wbdf = SB(s1, "wbdf", [128, 2, NCH, 128], F32)
                d_wbdf = Dep()
                S.op("pool", lambda e: e.memset(wbdf[:], 0.0), writes=[d_wbdf])
                for g in range(2):
                    for hb in range(2):
                        src = rgw_d[g].rearrange("(c two) k m -> two k c m", two=2)[hb]
                        S.dma("sp", lambda e: e.dma_start(out=wbdf[hb * 64:(hb + 1) * 64, g, :, hb * 64:(hb + 1) * 64],
                                                          in_=src), writes=[d_wbdf])
                S.op("dve", lambda e: e.tensor_copy(out=Wbd[:], in_=wbdf[:]), reads=[d_wbdf], writes=[d_Wbd])
                S.barrier()

            E_pr = SB(sa, "E_pr", [128, NH, 2, 128], F32)
            E_so = SB(sa, "E_so", [128, NH, 128], F32)
            E_sc = SB(sa, "E_sc", [128, NH, DEC], F32)
            gq8 = SB(sa, "gq8", [128, 64], F32)
            gk = SB(sa, "gk", [128, 64], F32)
            esk = SB(sa, "esk", [128, NH], F32)
            d_ac = Dep("attnconst")
            S.dma("sp", lambda e: e.dma_start(out=E_pr[:].rearrange("p h b q -> p (h b q)"), in_=E_pr_d), writes=[d_ac])
            S.dma("act", lambda e: e.dma_start(out=E_so[:].rearrange("p h q -> p (h q)"), in_=E_so_d), writes=[d_ac])
            S.dma("act", lambda e: e.dma_start(out=E_sc[:].rearrange("p h q -> p (h q)"), in_=E_sc_d), writes=[d_ac])
            S.dma("sp", lambda e: e.dma_start(out=gq8[:], in_=qkg_d[0:1, :].partition_broadcast(128)), writes=[d_ac])
            S.dma("sp", lambda e: e.dma_start(out=gk[:], in_=qkg_d[1:2, :].partition_broadcast(128)), writes=[d_ac])
            S.dma("sp", lambda e: e.dma_start(out=esk[:], in_=sinks_d[0:1, :].partition_broadcast(128)), writes=[d_ac])
            S.op("dve", lambda e: e.tensor_scalar(out=gq8[:], in0=gq8[:], scalar1=0.125, scalar2=None, op0=ALU.mult),
                 reads=[d_ac], writes=[d_ac])
            S.op("act", lambda e: e.activation(out=esk[:], in_=esk[:], func=AF.Exp), reads=[d_ac], writes=[d_ac])

            x_t = [SB(sa, "x_t%d" % i, [128, D], F32) for i in range(2)]
            d_x = [Dep(), Dep()]
            st1 = SB(sa, "st1", [128, 8], F32); d_st1 = Dep()
            xn = SB(sa, "xn", [128, D], BF16); d_xn = Dep()
            xnT = SB(sa, "xnT", [128, NCH, T], BF16); d_xnT = Dep()
            xr = [SB(sa, "xr%d" % i, [128, NCH, 3 + T], F32) for i in range(2)]
            d_xr = [Dep(), Dep()]
            xr_s = SB(sa, "xr_s", [128, NCH, NSEQ, 3 + DEC], F32); d_xr_s = Dep()
            gel2 = [SB(sa, "gel%d" % i, [128, NCH, T], BF16) for i in range(2)]; d_gel2 = [Dep(), Dep()]
            stk = SB(sa, "stk", [128, 4], F32); d_stk = Dep()
            rdq = SB(sa, "rdq", [128, 16], F32); d_rdq = Dep()
            xrtm = SB(sa, "xrtm", [128, D], F32); d_xrtm = Dep()
            gtm = SB(sa, "gtm", [128, D], BF16); d_gtm = Dep()
            xc = SB(sa, "xc", [128, NCH, T], F32); d_xc = Dep()
            d_xcc = [Dep() for _ in range(NCH)]
            xcb = SB(sa, "xcb", [128, NCH, T], BF16); d_xcb = Dep()
            rg = SB(sa, "rg", [128, NCH, T], F32); d_rg = Dep()
            ig = SB(sa, "ig", [128, NCH, T], F32); d_ig = Dep()
            sq = SB(sa, "sq", [128, NCH, T], F32); d_sq = Dep()
            hh = [SB(sa, "hh%d" % i, [128, NCH, T], F32) for i in range(2)]
            d_hh = [Dep(), Dep()]
            h0 = SB(sa, "h0", [128, NCH, NSEQ], F32); d_h0 = Dep()
            hcar = SB(sa, "hcar", [128, NCH], F32); d_hcar = Dep()
            fmo = [SB(sa, "fmo%d" % i, [128, 2, NCH, T], BF16) for i in range(2)]
            d_fmo = [Dep(), Dep()]
            qn = SB(sa, "qn", [128, D], BF16); d_qn = Dep()
            qT2 = [SB(sa, "qT%d" % i, [64, NH, T], BF16) for i in range(2)]; d_qT2 = [Dep(), Dep()]
            kn = SB(sa, "kn", [128, 256], F32); d_kn = Dep()
            ks = SB(sa, "ks", [128, 256], BF16); d_ks = Dep()
            kT = [SB(sa, "kT%d" % i, [64, NKV, T], BF16) for i in range(3)]
            d_kT = [Dep(), Dep(), Dep()]
            vf = SB(sa, "vf", [128, 256], F32); d_vf = Dep()
            vau = [SB(sa, "vau%d" % i, [128, NKV, 65], BF16) for i in range(3)]
            d_vau = [Dep(), Dep(), Dep()]
            ex = SB(sa, "ex", [128, 2, 4, T], F32); d_ex = Dep()
            pT = [SB(sa, "pT%d" % i, [128, 2, 4, T], BF16) for i in range(2)]
            d_pT = [Dep(), Dep()]
            rden = SB(sa, "rden", [128, 18], F32); d_rden = Dep()
            attn = SB(sa, "attn", [128, D], BF16); d_attn = Dep()
            tmo = SB(sa, "tmo", [128, D], F32); d_tmo = Dep()
            cmp_ = SB(sa, "cmp", [128, NCH, 48], F32); d_cmp = Dep()
            ckf = [SB(sa, "ckf0", [128, 256], F32)] * 2
            cvf = [SB(sa, "cvf0", [128, 256], F32)] * 2
            d_ckf = [Dep()] * 2
            d_cvf = [Dep()] * 2
            cks = SB(sa, "cks", [128, 256], BF16); d_cks = Dep()
            ckT = SB(sa, "ckT", [64, NKV, 128], BF16); d_ckT = Dep()
            cva = SB(sa, "cva", [128, NKV, 65], BF16); d_cva = Dep()
            exc = SB(sa, "exc", [128, NH, DEC], F32); d_exc = Dep()
            Zp = [SB(sa, "Zp0", [128, NH, 248], BF16)] * 2
            d_Zp = [Dep()] * 2
            oacc = SB(sa, "oacc", [128, 18, 65], F32); d_oacc = Dep()

            for i in range(3):
                S.op("pool", lambda e: e.memset(vau[i][:, :, 64:65], 1.0), writes=[d_vau[i]])
            for i in range(2):
                S.op("pool", lambda e: e.memset(Zp[i][:], 0.0), writes=[d_Zp[i]])
                S.op("pool", lambda e: e.memset(xr[i][:], 0.0), writes=[d_xr[i]])
                S.op("pool", lambda e: e.memset(hh[i][:], 0.0), writes=[d_hh[i]])
            S.op("pool", lambda e: e.memset(cva[:, :, 64:65], 1.0), writes=[d_cva])

            def fm_to_dram(src_fn, n, dst, reads):
                for c in range(NCH):
                    b = 6 + c // 4
                    S.op("pe", lambda e: e.transpose(out=PS[0:n, b, (c % 4) * 128:(c % 4 + 1) * 128], in_=src_fn(c),
                                                     identity=identf[:]),
                         reads=list(reads) + [d_const], writes=[dP[b]])
                S.op("dve", lambda e: e.tensor_copy(out=tmo[0:n, :].rearrange("p (b x) -> p b x", b=2),
                                                    in_=PS[0:n, 6:8, :]),
                     reads=[dP[6], dP[7]], writes=[d_tmo])
                S.dma("sp", lambda e: e.dma_start(out=dst, in_=tmo[0:n, :]), reads=[d_tmo], is_out=True)

            def norm_to_xnT(xtile, d_xtile):
                S.op("act", lambda e: e.activation(out=xn[:], in_=xtile, func=AF.Square, accum_out=st1[:, 0:1]),
                     reads=[d_xtile], writes=[d_xn, d_st1])
                S.op("act", lambda e: e.activation(out=st1[:, 1:2], in_=st1[:, 0:1], func=AF.Sqrt, scale=1.0 / D, bias=EPS),
                     reads=[d_st1], writes=[d_st1])
                S.op("dve", lambda e: e.reciprocal(out=st1[:, 2:3], in_=st1[:, 1:2]), reads=[d_st1], writes=[d_st1])
                S.op("act", lambda e: e.activation(out=xn[:], in_=xtile, func=AF.Copy, scale=st1[:, 2:3]),
                     reads=[d_xtile, d_st1], writes=[d_xn])
                for c in range(NCH):
                    S.op("pe", lambda e: e.transpose(out=pbank16(0)[:, c * 128:(c + 1) * 128], in_=xn[:, c * 128:(c + 1) * 128],
                                                     identity=identb[:]), reads=[d_xn, d_const], writes=[dP[0]])
                S.op("dve", lambda e: e.tensor_copy(out=xnT[:].rearrange("p c t -> p (c t)"), in_=pbank16(0)),
                     reads=[dP[0]], writes=[d_xnT])

            def proj_fm(col0, banks):
                for c in range(NCH):
                    b = banks[c // 4]
                    for k in range(NCH):
                        S.op("pe", lambda e: e.matmul(PS[:, b, (c % 4) * 128:(c % 4 + 1) * 128],
                                                      lhsT=WgA[:, k, col0 + c * 128: col0 + (c + 1) * 128],
                                                      rhs=xnT[:, k, :], start=(k == 0), stop=(k == NCH - 1)),
                             reads=[d_WgA, d_xnT], writes=[dP[b]])

            def proj_xr(par, sample=False, conv_first=False):
                for j in range(2):
                    for k in range(NCH):
                        S.op("pe", lambda e: e.matmul(PS[:, 1 + j, :], lhsT=xnT[:, k, :], rhs=WgA[:, k, j * 512:(j + 1) * 512],
                                                      start=(k == 0), stop=(k == NCH - 1)),
                             reads=[d_WgA, d_xnT], writes=[dP[1 + j]])
                S.op("act", lambda e: e.activation(out=xrtm[:].rearrange("p (b x) -> p b x", b=2), in_=PS[:, 1:3, :], func=AF.Copy),
                     reads=[dP[1], dP[2]], writes=[d_xrtm])
                for c in range(NCH):
                    b = 1 + c // 4
                    S.op("pe", lambda e: e.transpose(out=PS[:, b, (c % 4) * 128:(c % 4 + 1) * 128], in_=xrtm[:, c * 128:(c + 1) * 128],
                                                     identity=identf[:]), reads=[d_xrtm, d_const], writes=[dP[b]])
                if sample:
                    S.op("dve", lambda e: e.tensor_copy(
                        out=xr_s[:, :, :, 3:3 + DEC],
                        in_=PS[:, 1:3, :].rearrange("p b (c s t) -> p (b c) s t", c=4, t=DEC)),
                        reads=[dP[1], dP[2]], writes=[d_xr_s])
                else:
                    if not conv_first:
                        S.op("pool", lambda e: e.tensor_copy(out=xr[par][:, :, 0:3], in_=xr[1 - par][:, :, T:T + 3]),
                             reads=[d_xr[1 - par]], writes=[d_xr[par]])
                    S.op("act", lambda e: e.activation(
                        out=xr[par][:, :, 3:3 + T], in_=PS[:, 1:3, :].rearrange("p b (c t) -> p (b c) t", c=4), func=AF.Copy),
                        reads=[dP[1], dP[2]], writes=[d_xr[par]])

            def proj_gr(par):
                gel, d_gel = gel2[par], d_gel2[par]
                for j in range(2):
                    for k in range(NCH):
                        S.op("pe", lambda e: e.matmul(PS[:, 3 + j, :], lhsT=xnT[:, k, :], rhs=WgA[:, k, 1024 + j * 512:1024 + (j + 1) * 512],
                                                      start=(k == 0), stop=(k == NCH - 1)),
                             reads=[d_WgA, d_xnT], writes=[dP[3 + j]])
                S.op("act", lambda e: e.activation(out=gtm[:].rearrange("p (b x) -> p b x", b=2), in_=PS[:, 3:5, :], func=AF.Gelu_apprx_tanh),
                     reads=[dP[3], dP[4]], writes=[d_gtm])
                for c in range(NCH):
                    S.op("pe", lambda e: e.transpose(out=pbank16(3)[:, c * 128:(c + 1) * 128], in_=gtm[:, c * 128:(c + 1) * 128],
                                                     identity=identb[:]), reads=[d_gtm, d_const], writes=[dP[3]])
                S.op("act", lambda e: e.activation(out=gel[:].rearrange("p c t -> p (c t)"), in_=pbank16(3), func=AF.Copy),
                     reads=[dP[3]], writes=[d_gel])

            def lru_tile(par, sample, first, need_out, conv_first=None):
                if conv_first is None:
                    conv_first = first
                XR = xr_s if sample else xr[par]
                dXR = d_xr_s if sample else d_xr[par]
                nseq, L = (NSEQ, DEC) if sample else (1, T)

                def v4(ap3):
                    return ap3.rearrange("p (s t) -> p s t", t=L)

                def win_(c, j):
                    return XR[:, c, :, j:j + L] if sample else XR[:, c, j:j + L]

                def dst_(c):
                    return v4(xc[:, c, :]) if sample else xc[:, c, :]

                for c in range(NCH):
                    S.op("act", lambda e: e.activation(out=dst_(c), in_=win_(c, 0), func=AF.Identity,
                                                       scale=cp[:, c, 0:1], bias=cp[:, c, 4:5]),
                         reads=[dXR, d_const], writes=[d_xcc[c]])
                for c in range(NCH):
                    for j in range(1, 4):
                        S.op("dve", lambda e: e.scalar_tensor_tensor(out=dst_(c), in0=win_(c, j), scalar=cp[:, c, j:j + 1], in1=dst_(c),
                                                                     op0=ALU.mult, op1=ALU.add),
                             reads=[dXR, d_const, d_xcc[c]], writes=[d_xcc[c]])
                    if c % 2 == 1:
                        yield
                yield
                S.op("act", lambda e: e.activation(out=xcb[:], in_=xc[:], func=AF.Copy), reads=d_xcc, writes=[d_xcb])
                for g, (dst, ddst, banks, bcol) in enumerate(((rg, d_rg, (1, 2), 5), (ig, d_ig, (3, 4), 6))):
                    for c in range(NCH):
                        b = banks[c // 4]
                        S.op("pe", lambda e: e.matmul(PS[:, b, (c % 4) * 128:(c % 4 + 1) * 128], lhsT=Wbd[:, g, c, :],
                                                      rhs=xcb[:, c, :], start=True, stop=True),
                             reads=[d_Wbd, d_xcb], writes=[dP[b]])
                    for c in range(NCH):
                        b = banks[c // 4]
                        S.op("act", lambda e: e.activation(out=dst[:, c, :], in_=PS[:, b, (c % 4) * 128:(c % 4 + 1) * 128],
                                                           func=AF.Sigmoid, bias=cp[:, c, bcol:bcol + 1]),
                             reads=[dP[b], d_const], writes=[ddst])
                    yield
                S.op("dve", lambda e: e.tensor_tensor(out=rg[:], in0=rg[:], in1=cp[:, :, 10:11].to_broadcast([128, NCH, T]),
                                                      op=ALU.mult), reads=[d_rg, d_const], writes=[d_rg])
                S.op("act", lambda e: e.activation(out=rg[:], in_=rg[:], func=AF.Exp), reads=[d_rg], writes=[d_rg])
                yield
                S.op("act", lambda e: e.activation(out=sq[:], in_=rg[:], func=AF.Square), reads=[d_rg], writes=[d_sq])
                S.op("act", lambda e: e.activation(out=sq[:], in_=sq[:], func=AF.Sqrt, scale=-1.0, bias=1.0),
                     reads=[d_sq], writes=[d_sq])
                S.op("dve", lambda e: e.tensor_tensor(out=ig[:], in0=ig[:], in1=xc[:], op=ALU.mult),
                     reads=[d_ig] + d_xcc, writes=[d_ig])
                yield
                S.op("dve", lambda e: e.tensor_tensor(out=ig[:], in0=ig[:], in1=sq[:], op=ALU.mult),
                     reads=[d_ig, d_sq], writes=[d_ig])
                H = hh[par]
                if sample:
                    Hv = H[:].rearrange("p c (s t) -> p c s t", t=DEC)
                    Av = rg[:].rearrange("p c (s t) -> p c s t", t=DEC)
                    Bv = ig[:].rearrange("p c (s t) -> p c s t", t=DEC)
                    for t in range(DEC):
                        prev = h0[:] if t == 0 else Hv[:, :, :, t - 1]
                        S.op("dve", lambda e: e.tensor_tensor(out=Hv[:, :, :, t], in0=Av[:, :, :, t], in1=prev, op=ALU.mult),
                             reads=[d_rg, d_h0, d_hh[par]], writes=[d_hh[par]])
                        S.op("dve", lambda e: e.tensor_tensor(out=Hv[:, :, :, t], in0=Hv[:, :, :, t], in1=Bv[:, :, :, t], op=ALU.add),
                             reads=[d_ig, d_hh[par]], writes=[d_hh[par]])
                else:
                    hprev = h0[:, :, 0] if first else hh[1 - par][:, :, T - 1]
                    S.op("dve", lambda e: e.tensor_tensor(out=hcar[:], in0=rg[:, :, 0], in1=hprev, op=ALU.mult),
                         reads=[d_rg, d_h0, d_hh[1 - par]], writes=[d_hcar])
                    S.op("dve", lambda e: e.tensor_tensor(out=ig[:, :, 0], in0=ig[:, :, 0], in1=hcar[:], op=ALU.add),
                         reads=[d_ig, d_hcar], writes=[d_ig])
                    S.op("dve", lambda e: e.memset(rg[:, :, 0], 0.0), reads=[d_hcar], writes=[d_rg])
                    S.op("dve", lambda e: e.tensor_tensor_scan(out=H[:].rearrange("p c t -> p (c t)"),
                                                               data0=rg[:].rearrange("p c t -> p (c t)"),
                                                               data1=ig[:].rearrange("p c t -> p (c t)"),
                                                               initial=0.0, op0=ALU.mult, op1=ALU.add),
                         reads=[d_rg, d_ig], writes=[d_hh[par]])
                yield
                if need_out:
                    S.op("dve", lambda e: e.tensor_tensor(out=fmo[par][:, 0, :, :], in0=H[:], in1=gel2[par][:], op=ALU.mult),
                         reads=[d_hh[par], d_gel2[par]], writes=[d_fmo[par]])
                yield

            def kv_prep(par):
                for k in range(NCH):
                    S.op("pe", lambda e: e.matmul(PS[:, 3, :], lhsT=xnT[:, k, :], rhs=WgA[:, k, 3072:3584],
                                                  start=(k == 0), stop=(k == NCH - 1)),
                         reads=[d_WgA, d_xnT], writes=[dP[3]])
                S.op("act", lambda e: e.activation(out=xrtm[:, 0:256], in_=PS[:, 3, 0:256], func=AF.Square),
                     reads=[dP[3]], writes=[d_xrtm])
                S.op("dve", lambda e: e.tensor_reduce(out=stk[:, 0:4], in_=xrtm[:, 0:256].rearrange("p (h d) -> p h d", d=HD),
                                                      axis=AX.X, op=ALU.add), reads=[d_xrtm], writes=[d_stk])
                S.op("act", lambda e: e.activation(out=stk[:, 0:4], in_=stk[:, 0:4], func=AF.Sqrt, scale=1.0 / HD, bias=EPS),
                     reads=[d_stk], writes=[d_stk])
                S.op("dve", lambda e: e.reciprocal(out=stk[:, 0:4], in_=stk[:, 0:4]), reads=[d_stk], writes=[d_stk])
                S.op("dve", lambda e: e.tensor_tensor(out=kn[:].rearrange("p (h d) -> p h d", d=HD),
                                                      in0=PS[:, 3, 0:256].rearrange("p (h d) -> p h d", d=HD),
                                                      in1=stk[:, 0:4].unsqueeze(2).to_broadcast([128, NKV, HD]), op=ALU.mult),
                     reads=[dP[3], d_stk], writes=[d_kn])
                S.op("dve", lambda e: e.tensor_tensor(out=kn[:].rearrange("p (h d) -> p h d", d=HD),
                                                      in0=kn[:].rearrange("p (h d) -> p h d", d=HD),
                                                      in1=gk[:].unsqueeze(1).to_broadcast([128, NKV, HD]), op=ALU.mult),
                     reads=[d_kn, d_ac], writes=[d_kn])
                S.op("pool", lambda e: e.tensor_tensor(out=ks[:].rearrange("p (h d) -> p h d", d=HD),
                                                       in0=kn[:].rearrange("p (h d) -> p h d", d=HD),
                                                       in1=gq8[:].unsqueeze(1).to_broadcast([128, NKV, HD]), op=ALU.mult),
                     reads=[d_kn, d_ac], writes=[d_ks])
                S.op("act", lambda e: e.activation(out=vf[:], in_=PS[:, 3, 256:512], func=AF.Copy),
                     reads=[dP[3]], writes=[d_vf])
                S.op("act", lambda e: e.activation(out=vau[par][:, :, 0:64], in_=vf[:].rearrange("p (h d) -> p h d", d=HD), func=AF.Copy),
                     reads=[d_vf], writes=[d_vau[par]])
                S.op("pool", lambda e: e.memset(vau[par][:, :, 64:65], 1.0), writes=[d_vau[par]])
                for h in range(NKV):
                    S.op("pe", lambda e: e.transpose(out=pbank16(0)[0:64, h * 128:(h + 1) * 128], in_=ks[:, h * 64:(h + 1) * 64],
                                                     identity=identb[:]), reads=[d_ks, d_const], writes=[dP[0]])
                S.op("act", lambda e: e.activation(out=kT[par][:].rearrange("p h t -> p (h t)"), in_=pbank16(0)[0:64, 0:512],
                                                   func=AF.Copy), reads=[dP[0]], writes=[d_kT[par]])

            def q_prep(qb):
                qT, d_qT = qT2[qb], d_qT2[qb]
                for j in range(2):
                    for k in range(NCH):
                        S.op("pe", lambda e: e.matmul(PS[:, 3 + j, :], lhsT=xnT[:, k, :],
                                                      rhs=WgA[:, k, 2048 + j * 512:2048 + (j + 1) * 512],
                                                      start=(k == 0), stop=(k == NCH - 1)),
                             reads=[d_WgA, d_xnT], writes=[dP[3 + j]])
                S.op("act", lambda e: e.activation(out=xrtm[:].rearrange("p (b x) -> p b x", b=2), in_=PS[:, 3:5, :], func=AF.Square),
                     reads=[dP[3], dP[4]], writes=[d_xrtm])
                S.op("dve", lambda e: e.tensor_reduce(out=rdq[:, 0:16], in_=xrtm[:].rearrange("p (h d) -> p h d", d=HD),
                                                      axis=AX.X, op=ALU.add), reads=[d_xrtm], writes=[d_rdq])
                S.op("act", lambda e: e.activation(out=rdq[:, 0:16], in_=rdq[:, 0:16], func=AF.Sqrt, scale=1.0 / HD, bias=EPS),
                     reads=[d_rdq], writes=[d_rdq])
                S.op("dve", lambda e: e.reciprocal(out=rdq[:, 0:16], in_=rdq[:, 0:16]), reads=[d_rdq], writes=[d_rdq])
                S.op("dve", lambda e: e.tensor_tensor(out=qn[:].rearrange("p (b h d) -> p b h d", b=2, d=HD),
                                                      in0=PS[:, 3:5, :].rearrange("p b (h d) -> p b h d", d=HD),
                                                      in1=rdq[:, 0:16].rearrange("p (b h) -> p b h", b=2).unsqueeze(3).to_broadcast([128, 2, 8, HD]),
                                                      op=ALU.mult),
                     reads=[dP[3], dP[4], d_rdq], writes=[d_qn])
                for h in range(NH):
                    b = 3 + h // 8
                    S.op("pe", lambda e: e.transpose(out=pbank16(b)[0:64, (h % 8) * 128:(h % 8 + 1) * 128],
                                                     in_=qn[:, h * 64:(h + 1) * 64], identity=identb[:]),
                         reads=[d_qn, d_const], writes=[dP[b]])
                for j in range(2):
                    S.op("act" if j == 0 else "dve",
                         (lambda e: e.activation(out=qT[:, 0:8, :].rearrange("p h t -> p (h t)"), in_=pbank16(3)[0:64, :], func=AF.Copy))
                         if j == 0 else
                         (lambda e: e.tensor_copy(out=qT[:, 8:16, :].rearrange("p h t -> p (h t)"), in_=pbank16(4)[0:64, :])),
                         reads=[dP[3 + j]], writes=[d_qT])

            def oslot(h):
                return PS[:, 5 + h // 6, (h % 6) * 80:(h % 6) * 80 + 65]

            def attn_own_prev(kb, qb, Eown_fn, kbp, Eprev_fn):
                qT, d_qT = qT2[qb], d_qT2[qb]
                nb = 2 if kbp is not None else 1
                for g4 in range(4):
                    pp = g4 % 2
                    for bi in range(nb):
                        kk = kb if bi == 0 else kbp
                        for hh_ in range(4):
                            h = g4 * 4 + hh_
                            S.op("pe", lambda e: e.matmul(PS[:, 1 + bi, hh_ * 128:(hh_ + 1) * 128], lhsT=kT[kk][:, g4, :],
                                                          rhs=qT[:, h, :], start=True, stop=True),
                                 reads=[d_kT[kk], d_qT], writes=[dP[1 + bi]])
                    S.op("act", lambda e: e.activation(out=ex[:, 0:nb, :, :].rearrange("p b h t -> p b (h t)"),
                                                       in_=PS[:, 1:1 + nb, :], func=AF.Exp),
                         reads=[dP[1], dP[2]][:nb], writes=[d_ex])
                    S.op("dve", lambda e: e.tensor_tensor(out=pT[pp][:, 0, :, :], in0=ex[:, 0, :, :], in1=Eown_fn(g4), op=ALU.mult),
                         reads=[d_ex, d_ac], writes=[d_pT[pp]])
                    if nb == 2:
                        S.op("dve", lambda e: e.tensor_tensor(out=pT[pp][:, 1, :, :], in0=ex[:, 1, :, :], in1=Eprev_fn(g4), op=ALU.mult),
                             reads=[d_ex, d_ac], writes=[d_pT[pp]])
                    for hh_ in range(4):
                        h = g4 * 4 + hh_
                        b = 5 + h // 6
                        S.op("pe", lambda e: e.matmul(oslot(h), lhsT=pT[pp][:, 0, hh_, :], rhs=vau[kb][:, g4, :],
                                                      start=True, stop=(nb == 1)),
                             reads=[d_pT[pp], d_vau[kb]], writes=[dP[b]])
                        if nb == 2:
                            S.op("pe", lambda e: e.matmul(oslot(h), lhsT=pT[pp][:, 1, hh_, :], rhs=vau[kbp][:, g4, :],
                                                          start=False, stop=True),
                                 reads=[d_pT[pp], d_vau[kbp]], writes=[dP[b]])
                    yield

            def attn_finish(par, from_oacc):
                if from_oacc:
                    den_src = oacc[:, 0:16, 64]
                    S.op("dve", lambda e: e.tensor_tensor(out=rden[:, 0:16], in0=den_src, in1=esk[:], op=ALU.add),
                         reads=[d_oacc, d_ac], writes=[d_rden])
                    S.op("dve", lambda e: e.reciprocal(out=rden[:, 0:16], in_=rden[:, 0:16]), reads=[d_rden], writes=[d_rden])
                    S.op("dve", lambda e: e.tensor_tensor(out=attn[:].rearrange("p (h d) -> p h d", d=HD), in0=oacc[:, 0:16, 0:64],
                                                          in1=rden[:, 0:16].unsqueeze(2).to_broadcast([128, 16, HD]), op=ALU.mult),
                         reads=[d_oacc, d_rden], writes=[d_attn])
                else:
                    pv = PS[:, 5:8, 0:480].rearrange("p b (s e) -> p b s e", e=80)
                    S.op("dve", lambda e: e.tensor_copy(out=rden[:, 0:12].rearrange("p (b s) -> p b s", b=2), in_=pv[:, 0:2, :, 64]),
                         reads=[dP[5], dP[6]], writes=[d_rden])
                    S.op("dve", lambda e: e.tensor_copy(out=rden[:, 12:16], in_=pv[:, 2, 0:4, 64]),
                         reads=[dP[7]], writes=[d_rden])
                    S.op("dve", lambda e: e.tensor_tensor(out=rden[:, 0:16], in0=rden[:, 0:16], in1=esk[:], op=ALU.add),
                         reads=[d_rden, d_ac], writes=[d_rden])
                    S.op("dve", lambda e: e.reciprocal(out=rden[:, 0:16], in_=rden[:, 0:16]), reads=[d_rden], writes=[d_rden])
                    S.op("dve", lambda e: e.tensor_tensor(out=attn[:, 0:768].rearrange("p (b s d) -> p b s d", b=2, d=HD),
                                                          in0=pv[:, 0:2, :, 0:64],
                                                          in1=rden[:, 0:12].rearrange("p (b s) -> p b s", b=2).unsqueeze(3).to_broadcast([128, 2, 6, HD]),
                                                          op=ALU.mult),
                         reads=[dP[5], dP[6], d_rden], writes=[d_attn])
                    S.op("dve", lambda e: e.tensor_tensor(out=attn[:, 768:1024].rearrange("p (s d) -> p s d", d=HD),
                                                          in0=pv[:, 2, 0:4, 0:64],
                                                          in1=rden[:, 12:16].unsqueeze(2).to_broadcast([128, 4, HD]), op=ALU.mult),
                         reads=[dP[7], d_rden], writes=[d_attn])
                for c in range(NCH):
                    S.op("pe", lambda e: e.transpose(out=pbank16(0)[:, c * 128:(c + 1) * 128], in_=attn[:, c * 128:(c + 1) * 128],
                                                     identity=identb[:]), reads=[d_attn, d_const], writes=[dP[0]])
                S.op("act", lambda e: e.activation(out=fmo[par][:, 1, :, :].rearrange("p c t -> p (c t)"), in_=pbank16(0), func=AF.Copy),
                     reads=[dP[0]], writes=[d_fmo[par]])

            S.op("pool", lambda e: e.memset(h0[:], 0.0), writes=[d_h0])
            ntile = NPRE + NMAIN

            def attn_gen(ti):
                par = ti % 2
                mi = ti - NPRE
                yield from attn_own_prev(ti % 3, ti % 2, lambda g4: E_pr[:, g4 * 4:(g4 + 1) * 4, 1, :],
                                         (ti - 1) % 3, (lambda g4: E_pr[:, g4 * 4:(g4 + 1) * 4, 0, :]))
                attn_finish(par, False)
                S.dma("pool", lambda e: e.dma_start(out=scr[mi], in_=fmo[par][:].rearrange("p a c t -> p (a c t)")),
                      reads=[d_fmo[par]])
                yield

            def run_gens(gens):
                gens = list(gens)
                while gens:
                    for g in list(gens):
                        try:
                            next(g)
                        except StopIteration:
                            gens.remove(g)

            def head_gen(ti):
                par = ti % 2
                main = ti >= NPRE
                S.dma("sp" if par == 0 else "act",
                      lambda e: e.dma_start(out=x_t[par][:], in_=xp[ti * T:(ti + 1) * T, :]), writes=[d_x[par]])
                norm_to_xnT(x_t[par][:], d_x[par])
                yield
                proj_xr(par, sample=False, conv_first=(ti == 0))
                yield
                if main:
                    proj_gr(par)
                    yield
                if ti == NPRE - 1 or main:
                    kv_prep(ti % 3)
                    yield
                if ti == NPRE - 1:
                    kb_ = ti % 3
                    S.op("dve", lambda e: e.tensor_scalar(out=vau[kb_][:], in0=vau[kb_][:], scalar1=flag[:, 0:1], scalar2=None,
                                                          op0=ALU.mult), reads=[d_vau[kb_], d_const], writes=[d_vau[kb_]])
                if main:
                    q_prep(ti % 2)
                    yield
                if ti == ntile - 1:
                    S.dma("sp", lambda e: e.dma_start(out=k_p, in_=kn[:]), reads=[d_kn], is_out=True)
                    S.dma("sp", lambda e: e.dma_start(out=v_p, in_=vf[:]), reads=[d_vf], is_out=True)

            pending = None
            run_gens([head_gen(0)])
            for ti in range(ntile):
                par = ti % 2
                main = ti >= NPRE
                if ti == NPRE:
                    S.op("dve", lambda e: e.tensor_scalar(out=h0[:, :, 0], in0=hh[1 - par][:, :, T - 1], scalar1=flag[:, 0:1],
                                                          scalar2=None, op0=ALU.mult),
                         reads=[d_hh[1 - par], d_const], writes=[d_h0])
                gens = [lru_tile(par, sample=False, first=(ti == 0 or ti == NPRE), need_out=main)]
                if pending is not None:
                    gens.append(attn_gen(pending))
                if ti + 1 < ntile:
                    gens.append(head_gen(ti + 1))
                run_gens(gens)
                pending = ti if main else None
                if ti == ntile - 1:
                    fm_to_dram(lambda c: xr[par][:, c, T:T + 3], 3, conv_p, [d_xr[par]])
                    fm_to_dram(lambda c: hh[par][:, c, T - 1:T], 1, lru_p, [d_hh[par]])
            run_gens([attn_gen(pending)])

            par = ntile % 2
            S.dma("sp", lambda e: e.dma_start(out=x_t[par][:], in_=xs), writes=[d_x[par]])
            S.dma("act", lambda e: e.dma_start(out=tmo[0:48, :], in_=cconv), reads=[], writes=[d_tmo])
            for c in range(NCH):
                S.op("pe", lambda e: e.transpose(out=PS[:, 6, c * 48:(c + 1) * 48], in_=tmo[0:48, c * 128:(c + 1) * 128],
                                                 identity=identf[0:48, 0:48]), reads=[d_tmo, d_const], writes=[dP[6]])
            S.op("dve", lambda e: e.tensor_copy(out=xr_s[:, :, :, 0:3], in_=PS[:, 6, 0:384].rearrange("p (c s j) -> p c s j", c=NCH, j=3)),
                 reads=[dP[6]], writes=[d_xr_s])
            S.dma("act", lambda e: e.dma_start(out=tmo[0:16, :], in_=slru), reads=[], writes=[d_tmo])
            for c in range(NCH):
                S.op("pe", lambda e: e.transpose(out=PS[:, 6, c * 16:(c + 1) * 16], in_=tmo[0:16, c * 128:(c + 1) * 128],
                                                 identity=identf[0:16, 0:16]), reads=[d_tmo, d_const], writes=[dP[6]])
            S.op("dve", lambda e: e.tensor_copy(out=h0[:], in_=PS[:, 6, 0:128].rearrange("p (c s) -> p c s", c=NCH)),
                 reads=[dP[6]], writes=[d_h0])
            norm_to_xnT(x_t[par][:], d_x[par])
            proj_xr(par, sample=True)
            proj_gr(par)
            for _ in lru_tile(par, sample=True, first=True, need_out=True):
                pass
            kv_prep(0)
            q_prep(0)
            qT, d_qT = qT2[0], d_qT2[0]
            for _ in attn_own_prev(0, 0, lambda g4: E_so[:, g4 * 4:(g4 + 1) * 4, :], None, None):
                pass
            pv = PS[:, 5:8, 0:480].rearrange("p b (s e) -> p b s e", e=80)
            S.op("dve", lambda e: e.tensor_copy(out=oacc[:, 0:12, :].rearrange("p (b s) e -> p b s e", b=2), in_=pv[:, 0:2, :, 0:65]),
                 reads=[dP[5], dP[6]], writes=[d_oacc])
            S.op("dve", lambda e: e.tensor_copy(out=oacc[:, 12:16, :], in_=pv[:, 2, 0:4, 0:65]),
                 reads=[dP[7]], writes=[d_oacc])
            for sq_ in range(NSEQ):
                cb = sq_ % 2
                S.dma("sp", lambda e: e.dma_start(out=ckf[cb][:], in_=ck[sq_]), writes=[d_ckf[cb]])
                S.dma("act", lambda e: e.dma_start(out=cvf[cb][:], in_=cv[sq_]), writes=[d_cvf[cb]])
                S.op("pool", lambda e: e.tensor_tensor(out=cks[:].rearrange("p (h d) -> p h d", d=HD),
                                                       in0=ckf[cb][:].rearrange("p (h d) -> p h d", d=HD),
                                                       in1=gq8[:].unsqueeze(1).to_broadcast([128, NKV, HD]), op=ALU.mult),
                     reads=[d_ckf[cb], d_ac], writes=[d_cks])
                S.op("pool", lambda e: e.tensor_copy(out=cva[:, :, 0:64], in_=cvf[cb][:].rearrange("p (h d) -> p h d", d=HD)),
                     reads=[d_cvf[cb]], writes=[d_cva])
                for h in range(NKV):
                    S.op("pe", lambda e: e.transpose(out=pbank16(0)[0:64, h * 128:(h + 1) * 128], in_=cks[:, h * 64:(h + 1) * 64],
                                                     identity=identb[:]), reads=[d_cks, d_const], writes=[dP[0]])
                S.op("act", lambda e: e.activation(out=ckT[:].rearrange("p h t -> p (h t)"), in_=pbank16(0)[0:64, 0:512], func=AF.Copy),
                     reads=[dP[0]], writes=[d_ckT])
                for g4 in range(NKV):
                    S.op("pe", lambda e: e.matmul(PS[:, 1, g4 * 32:(g4 + 1) * 32], lhsT=ckT[:, g4, :],
                                                  rhs=qT[:, g4 * 4:(g4 + 1) * 4, sq_ * DEC:(sq_ + 1) * DEC],
                                                  start=True, stop=True), reads=[d_ckT, d_qT], writes=[dP[1]])
                S.op("act", lambda e: e.activation(out=exc[:].rearrange("p h t -> p (h t)"), in_=PS[:, 1, 0:128], func=AF.Exp),
                     reads=[dP[1]], writes=[d_exc])
                S.op("dve", lambda e: e.tensor_tensor(out=Zp[cb][:, :, 120:128], in0=exc[:], in1=E_sc[:], op=ALU.mult),
                     reads=[d_exc, d_ac], writes=[d_Zp[cb]])
                for h in range(NH):
                    b = 5 + h // 6
                    S.op("pe", lambda e: e.matmul(oslot(h), lhsT=Zp[cb][:, h, 120 - sq_ * DEC:248 - sq_ * DEC], rhs=cva[:, h // 4, :],
                                                  start=True, stop=True), reads=[d_Zp[cb], d_cva], writes=[dP[b]])
                S.op("dve", lambda e: e.tensor_tensor(out=oacc[:, 0:12, :].rearrange("p (b s) e -> p b s e", b=2),
                                                      in0=oacc[:, 0:12, :].rearrange("p (b s) e -> p b s e", b=2), in1=pv[:, 0:2, :, 0:65], op=ALU.add),
                     reads=[dP[5], dP[6], d_oacc], writes=[d_oacc])
                S.op("dve", lambda e: e.tensor_tensor(out=oacc[:, 12:16, :], in0=oacc[:, 12:16, :], in1=pv[:, 2, 0:4, 0:65], op=ALU.add),
                     reads=[dP[7], d_oacc], writes=[d_oacc])
            attn_finish(par, True)
            S.dma("pool", lambda e: e.dma_start(out=scr[NMAIN], in_=fmo[par][:].rearrange("p a c t -> p (a c t)")),
                  reads=[d_fmo[par]])
            S.op("dve", lambda e: e.tensor_copy(out=cmp_[:].rearrange("p c (s j) -> p c s j", j=3), in_=xr_s[:, :, :, DEC:DEC + 3]),
                 reads=[d_xr_s], writes=[d_cmp])
            fm_to_dram(lambda c: cmp_[:, c, :], 48, conv_s, [d_cmp])
            S.op("dve", lambda e: e.tensor_copy(out=cmp_[:, :, 0:16], in_=hh[par][:].rearrange("p c (s t) -> p c s t", t=DEC)[:, :, :, DEC - 1]),
                 reads=[d_hh[par]], writes=[d_cmp])
            fm_to_dram(lambda c: cmp_[:, c, 0:16], 16, lru_s, [d_cmp])
            S.dma("sp", lambda e: e.dma_start(out=k_s[:, 0:120, :], in_=ck[:, 8:128, :]), is_out=True)
            S.dma("act", lambda e: e.dma_start(out=v_s[:, 0:120, :], in_=cv[:, 8:128, :]), is_out=True)
            for sq_ in range(NSEQ):
                S.dma("sp", lambda e: e.dma_start(out=k_s[sq_, 120:128, :], in_=kn[sq_ * DEC:(sq_ + 1) * DEC, :]), reads=[d_kn], is_out=True)
                S.dma("act", lambda e: e.dma_start(out=v_s[sq_, 120:128, :], in_=vf[sq_ * DEC:(sq_ + 1) * DEC, :]), reads=[d_vf], is_out=True)
            S.barrier()

        sbc = ExitStack()
        st.enter_context(sbc)
        d_hres = [Dep() for _ in range(NT)]
        spc = ExitStack()
        st.enter_context(spc)
        xn2T_all = SB(spc, "xn2T_all", [128, NCH, NTOK], BF16)
        d_xn2T_all = [Dep() for _ in range(NT)]
        slotT = SB(spc, "slotT", [128, 3, NTOK], BF16)
        d_slotT = [Dep() for _ in range(NT)]
        swq = ExitStack()
        Wq = SB(swq, "Wq", [128, NCH, 2048], BF16)
        SKT = SB(swq, "SKT", [128, 16, 128], BF16)
        d_WC = Dep("WC")

        if STAGE >= 2:
          with ExitStack() as sb_:
            WgB = SB(sb_, "WgB", [128, NCH, 2048], BF16)
            Wl = SB(sb_, "Wl", [128, NCH, D], BF16)
            Wa = SB(sb_, "Wa", [128, NCH, D], BF16)
            Wo = SB(sb_, "Wo", [128, NCH, D], BF16)
            d_WB = Dep("WB")
            with ExitStack() as s1:
                stg = [SB(s1, "stgB%d" % i, [128, 2048], F32) for i in range(4)]
                d_stg = [Dep() for _ in range(4)]
                jobs = []
                for kc in range(NCH):
                    jobs.append((w_in_d[kc * 128:(kc + 1) * 128, 3584:5632], WgB[:, kc, :], cp[:, kc, 8:9], d_WB))
                for (wd, wsb) in ((w_bl_d, Wl), (w_ba_d, Wa), (w_o_d, Wo)):
                    wv = wd.rearrange("(kc p) n -> p kc n", p=128)
                    for k2 in range(NCH // 2):
                        jobs.append((wv[:, 2 * k2:2 * k2 + 2, :], wsb[:, 2 * k2:2 * k2 + 2, :].rearrange("p a n -> p (a n)"), None, d_WB))
                if STAGE >= 3:
                    for kc in range(NCH):
                        jobs.append((w_q_d[kc * 128:(kc + 1) * 128, :], Wq[:, kc, :], cp[:, kc, 9:10], d_WC))
                for n, (src, dst, sc_ap, ddst) in enumerate(jobs):
                    b = n % 4
                    o_ap = stg[b][:] if len(src.shape) == 2 else stg[b][:].rearrange("p (a n) -> p a n", a=2)
                    S.dma("sp" if n % 2 == 0 else "pool", lambda e: e.dma_start(out=o_ap, in_=src), writes=[d_stg[b]])
                    if n % 2 == 0:
                        if sc_ap is None:
                            S.op("dve", lambda e: e.tensor_copy(out=dst, in_=stg[b][:]), reads=[d_stg[b]], writes=[ddst])
                        else:
                            S.op("dve", lambda e: e.tensor_scalar(out=dst, in0=stg[b][:], scalar1=sc_ap, scalar2=None, op0=ALU.mult),
                                 reads=[d_stg[b], d_const], writes=[ddst])
                    else:
                        if sc_ap is None:
                            S.op("act", lambda e: e.activation(out=dst, in_=stg[b][:], func=AF.Copy), reads=[d_stg[b]], writes=[ddst])
                        else:
                            S.op("act", lambda e: e.activation(out=dst, in_=stg[b][:], func=AF.Copy, scale=sc_ap),
                                 reads=[d_stg[b], d_const], writes=[ddst])
                if STAGE >= 3:
                    S.dma("sp", lambda e: e.dma_start(out=stg[0][:].rearrange("p (h d) -> p h d", d=128),
                                                      in_=subk_d.rearrange("h k d -> k h d")),
                          reads=[d_stg[0]], writes=[d_stg[0]])
                    for hp in range(16):
                        b = 1 + hp // 4
                        S.op("pe", lambda e: e.transpose(out=PS[:, b, (hp % 4) * 128:(hp % 4 + 1) * 128],
                                                         in_=stg[0][:, hp * 128:(hp + 1) * 128], identity=identf[:]),
                             reads=[d_stg[0], d_const], writes=[dP[b]])
                    S.op("dve", lambda e: e.tensor_copy(out=SKT[:].rearrange("p (b h) k -> p b (h k)", b=4), in_=PS[:, 1:5, :]),
                         reads=[dP[1], dP[2], dP[3], dP[4]], writes=[d_WC])
                S.barrier()

            x_t = [SB(sb_, "xB%d" % i, [128, D], F32) for i in range(2)]
            d_x = [Dep(), Dep()]
            st1 = SB(sb_, "st1B", [128, 8], F32); d_st1 = Dep()
            xn2 = [SB(sb_, "xnB%d" % i, [128, D], BF16) for i in range(2)]; d_xn2b = [Dep(), Dep()]
            xnT2 = [SB(sb_, "xnTB%d" % i, [128, NCH, T], BF16) for i in range(2)]; d_xnT2b = [Dep(), Dep()]
            fmoB = [SB(sb_, "fmoB%d" % i, [128, 2, NCH, T], BF16) for i in range(2)]
            d_fmoB = [Dep(), Dep()]
            sga = SB(sb_, "sga", [128, D], F32); d_sga = Dep()
            sgb = SB(sb_, "sgb", [128, D], F32); d_sgb = Dep()
            m1 = SB(sb_, "m1", [128, D], F32); d_m1 = Dep()
            m2 = sga; d_m2 = d_sga
            mtm = SB(sb_, "mtm", [128, D], BF16); d_mtm = Dep()
            mT = SB(sb_, "mT", [128, NCH, T], BF16); d_mT = Dep()

            def b_head(ti):
                par = ti % 2
                xn, d_xn, xnT, d_xnT = xn2[par], d_xn2b[par], xnT2[par], d_xnT2b[par]
                src = xp[(NPRE + ti) * T:(NPRE + ti + 1) * T, :] if ti < NMAIN else xs
                S.dma("sp", lambda e: e.dma_start(out=x_t[par][:], in_=src), writes=[d_x[par]])
                S.dma("act", lambda e: e.dma_start(out=fmoB[par][:].rearrange("p a c t -> p (a c t)"), in_=scr[ti]),
                      writes=[d_fmoB[par]])
                xtile, d_xtile = x_t[par][:], d_x[par]
                S.op("act", lambda e: e.activation(out=xn[:], in_=xtile, func=AF.Square, accum_out=st1[:, 0:1]),
                     reads=[d_xtile], writes=[d_xn, d_st1])
                S.op("act", lambda e: e.activation(out=st1[:, 1:2], in_=st1[:, 0:1], func=AF.Sqrt, scale=1.0 / D, bias=EPS),
                     reads=[d_st1], writes=[d_st1])
                S.op("dve", lambda e: e.reciprocal(out=st1[:, 2:3], in_=st1[:, 1:2]), reads=[d_st1], writes=[d_st1])
                S.op("act", lambda e: e.activation(out=xn[:], in_=xtile, func=AF.Copy, scale=st1[:, 2:3]),
                     reads=[d_xtile, d_st1], writes=[d_xn])
                for c in range(NCH):
                    S.op("pe", lambda e: e.transpose(out=pbank16(0)[:, c * 128:(c + 1) * 128], in_=xn[:, c * 128:(c + 1) * 128],
                                                     identity=identb[:]), reads=[d_xn, d_const], writes=[dP[0]])
                S.op("dve", lambda e: e.tensor_copy(out=xnT[:].rearrange("p c t -> p (c t)"), in_=pbank16(0)),
                     reads=[dP[0]], writes=[d_xnT])

            def b_part1(ti):
                par = ti % 2
                xnT, d_xnT = xnT2[par], d_xnT2b[par]
                for gi, (sg, dsg) in enumerate(((sga, d_sga), (sgb, d_sgb))):
                    for j in range(2):
                        b = 1 + gi * 2 + j
                        for k in range(NCH):
                            S.op("pe", lambda e: e.matmul(PS[:, b, :], lhsT=xnT[:, k, :],
                                                          rhs=WgB[:, k, gi * D + j * 512: gi * D + (j + 1) * 512],
                                                          start=(k == 0), stop=(k == NCH - 1)),
                                 reads=[d_WB, d_xnT], writes=[dP[b]])
                    b0 = 1 + gi * 2
                    S.op("act", lambda e: e.activation(out=sg[:].rearrange("p (b x) -> p b x", b=2), in_=PS[:, b0:b0 + 2, :], func=AF.Sigmoid),
                         reads=[dP[b0], dP[b0 + 1]], writes=[dsg])
                for bi, (W_, banks, sg, dsg, mm, dmm) in enumerate(((Wl, (5, 6), sga, d_sga, m1, d_m1), (Wa, (7, 0), sgb, d_sgb, m2, d_m2))):
                    for j in range(2):
                        b = banks[j]
                        for k in range(NCH):
                            S.op("pe", lambda e: e.matmul(PS[:, b, :], lhsT=fmoB[par][:, bi, k, :], rhs=W_[:, k, j * 512:(j + 1) * 512],
                                                          start=(k == 0), stop=(k == NCH - 1)),
                                 reads=[d_WB, d_fmoB[par]], writes=[dP[b]])
                        S.op("dve", lambda e: e.tensor_tensor(out=mm[:, j * 512:(j + 1) * 512], in0=PS[:, b, :], in1=sg[:, j * 512:(j + 1) * 512],
                                                              op=ALU.mult), reads=[dP[b], dsg], writes=[dmm])
                S.op("dve", lambda e: e.tensor_tensor(out=mtm[:], in0=m1[:], in1=m2[:], op=ALU.add),
                     reads=[d_m1, d_m2], writes=[d_mtm])

            def b_part2(ti):
                par = ti % 2
                for c in range(NCH):
                    S.op("pe", lambda e: e.transpose(out=pbank16(1)[:, c * 128:(c + 1) * 128], in_=mtm[:, c * 128:(c + 1) * 128],
                                                     identity=identb[:]), reads=[d_mtm, d_const], writes=[dP[1]])
                S.op("act", lambda e: e.activation(out=mT[:].rearrange("p c t -> p (c t)"), in_=pbank16(1), func=AF.Copy),
                     reads=[dP[1]], writes=[d_mT])
                for j in range(2):
                    for k in range(NCH):
                        S.op("pe", lambda e: e.matmul(PS[:, 2 + j, :], lhsT=mT[:, k, :], rhs=Wo[:, k, j * 512:(j + 1) * 512],
                                                      start=(k == 0), stop=(k == NCH - 1)),
                             reads=[d_WB, d_mT], writes=[dP[2 + j]])
                S.op("dve", lambda e: e.tensor_tensor(out=x_t[par][:].rearrange("p (b x) -> p b x", b=2),
                                                      in0=PS[:, 2:4, :], in1=x_t[par][:].rearrange("p (b x) -> p b x", b=2),
                                                      op=ALU.add),
                     reads=[dP[2], dP[3], d_x[par]], writes=[d_x[par]])
                S.dma("pool", lambda e: e.dma_start(out=hres_scr[ti], in_=x_t[par][:]), reads=[d_x[par]], writes=[d_hres[ti]])
                if STAGE == 2:
                    dst = y_p[ti * T:(ti + 1) * T, :] if ti < NMAIN else y_s
                    S.dma("sp", lambda e: e.dma_start(out=dst, in_=x_t[par][:]), reads=[d_x[par]], is_out=True)

            b_head(0)
            for ti in range(NT):
                b_part1(ti)
                if ti + 1 < NT:
                    b_head(ti + 1)
                b_part2(ti)
            S.barrier()

        PG = 4
        if STAGE >= 3:
          if True:
            with ExitStack() as sc_:
                hr_c = [SB(sc_, "hr_c%d" % i, [128, D], F32) for i in range(2)]
                d_hrc = [Dep(), Dep()]
                junk = SB(sc_, "junkC", [128, D], F32); d_junk = Dep()
                st1 = SB(sc_, "st1C", [128, 8], F32); d_st1 = Dep()
                xn2 = SB(sc_, "xn2", [128, D], BF16); d_xn2 = Dep()
                qrT = SB(sc_, "qrT", [128, 16, T], BF16); d_qrT = Dep()
                scs2 = [SB(sc_, "scs%d" % i, [128, 16, 128], F32) for i in range(2)]; d_scs2 = [Dep(), Dep()]
                d_tvr = [Dep() for _ in range(16)]; d_tiur = [Dep() for _ in range(16)]; d_wrkr = [Dep() for _ in range(16)]
                d_bestr = [Dep() for _ in range(8)]; d_posur = [Dep() for _ in range(8)]; d_cwkr = [Dep() for _ in range(8)]
                wrk = SB(sc_, "wrk", [128, 16, 128], F32); d_wrk = Dep()
                tv = SB(sc_, "tv", [128, 16, 16], F32); d_tv = Dep()
                tiu = SB(sc_, "tiu", [128, 16, 16], U32); d_tiu = Dep()
                tif = SB(sc_, "tif", [128, 16, 16], F32); d_tif = Dep()
                cand = SB(sc_, "cand", [128, 8, 256], F32); d_cand = Dep()
                cwk = SB(sc_, "cwk", [128, 8, 256], F32); d_cwk = Dep()
                best = SB(sc_, "best", [128, 8, 16], F32); d_best = Dep()
                posu = SB(sc_, "posu", [128, 8, 16], U32); d_posu = Dep()
                abu = SB(sc_, "abu", [128, 2, 128], U32); d_abu = Dep()
                abf = SB(sc_, "abf", [128, 2, 128], F32); d_abf = Dep()
                oh = [SB(sc_, "oh%d" % i, [128, 128, 16], F32) for i in range(2)]
                d_oh = [Dep(), Dep()]
                sel = SB(sc_, "sel", [128, 3, 128], F32); d_sel = Dep()
                gsm = SB(sc_, "gsm", [128, 8], F32); d_gsm = Dep()
                iota16 = SB(sc_, "iota16", [128, 16], F32)
                S.op("pool", lambda e: e.iota(iota16[:], pattern=[[1, 16]], base=0, channel_multiplier=0,
                                              allow_small_or_imprecise_dtypes=True), writes=[d_WC])
                gat = sel[:, 2, :].rearrange("p (h k) -> p h k", k=16)

                def c1a_head(ti):
                    par = ti % 2
                    tok0 = ti * T
                    S.dma("sp", lambda e: e.dma_start(out=hr_c[par][:], in_=hres_scr[ti]), reads=[d_hres[ti]], writes=[d_hrc[par]])
                    hres = hr_c[par][:]
                    S.op("act", lambda e: e.activation(out=junk[:], in_=hres, func=AF.Square, accum_out=st1[:, 0:1]),
                         reads=[d_hrc[par]], writes=[d_junk, d_st1])
                    S.op("act", lambda e: e.activation(out=st1[:, 1:2], in_=st1[:, 0:1], func=AF.Sqrt, scale=1.0 / D, bias=EPS),
                         reads=[d_st1], writes=[d_st1])
                    S.op("dve", lambda e: e.reciprocal(out=st1[:, 2:3], in_=st1[:, 1:2]), reads=[d_st1], writes=[d_st1])
                    S.op("act", lambda e: e.activation(out=xn2[:], in_=hres, func=AF.Copy, scale=st1[:, 2:3]),
                         reads=[d_hrc[par], d_st1], writes=[d_xn2])
                    for c in range(NCH):
                        S.op("pe", lambda e: e.transpose(out=pbank16(0)[:, c * 128:(c + 1) * 128], in_=xn2[:, c * 128:(c + 1) * 128],
                                                         identity=identb[:]), reads=[d_xn2, d_const], writes=[dP[0]])
                    S.op("act", lambda e: e.activation(out=xn2T_all[:, :, tok0:tok0 + T],
                                                       in_=pbank16(0).rearrange("p (c t) -> p c t", c=NCH), func=AF.Copy),
                         reads=[dP[0]], writes=[d_xn2T_all[ti]])
                    for hp in range(16):
                        b = 1 + hp // 4
                        for k in range(NCH):
                            S.op("pe", lambda e: e.matmul(PS[:, b, (hp % 4) * 128:(hp % 4 + 1) * 128],
                                                          lhsT=Wq[:, k, hp * 128:(hp + 1) * 128], rhs=xn2T_all[:, k, tok0:tok0 + T],
                                                          start=(k == 0), stop=(k == NCH - 1)),
                                 reads=[d_WC, d_xn2T_all[ti]], writes=[dP[b]])
                    S.op("act", lambda e: e.activation(out=qrT[:, 0:8, :].rearrange("p (b h) t -> p b (h t)", b=2), in_=PS[:, 1:3, :],
                                                       func=AF.Copy), reads=[dP[1], dP[2]], writes=[d_qrT])
                    S.op("act", lambda e: e.activation(out=qrT[:, 8:16, :].rearrange("p (b h) t -> p b (h t)", b=2), in_=PS[:, 3:5, :],
                                                       func=AF.Copy), reads=[dP[3], dP[4]], writes=[d_qrT])
                    sbanks = (5, 6, 7, 0)
                    for hp in range(16):
                        b = sbanks[hp // 4]
                        S.op("pe", lambda e: e.matmul(PS[:, b, (hp % 4) * 128:(hp % 4 + 1) * 128], lhsT=qrT[:, hp, :], rhs=SKT[:, hp, :],
                                                      start=True, stop=True), reads=[d_qrT, d_WC], writes=[dP[b]])
                    scs = scs2[par]
                    d_scs = d_scs2[par]
                    for q4 in range(4):
                        b = sbanks[q4]
                        S.op("act", lambda e: e.activation(out=scs[:, q4 * 4:(q4 + 1) * 4, :].rearrange("p h k -> p (h k)"), in_=PS[:, b, :], func=AF.Copy),
                             reads=[dP[b]], writes=[d_scs])

                def c1a_body(ti):
                    par = ti % 2
                    tok0 = ti * T
                    scs = scs2[par]
                    d_scs = d_scs2[par]
                    for hp in range(16):
                        S.op("dve", lambda e: e.max(out=tv[:, hp, 0:8], in_=scs[:, hp, :]), reads=[d_scs], writes=[d_tvr[hp]])
                    for hp in range(16):
                        S.op("dve", lambda e: e.max_index(out=tiu[:, hp, 0:8], in_max=tv[:, hp, 0:8], in_values=scs[:, hp, :]),
                             reads=[d_scs, d_tvr[hp]], writes=[d_tiur[hp]])
                    for hp in range(16):
                        S.op("dve", lambda e: e.match_replace(out=wrk[:, hp, :], in_to_replace=tv[:, hp, 0:8], in_values=scs[:, hp, :],
                                                              imm_value=-1e30), reads=[d_scs, d_tvr[hp]], writes=[d_wrkr[hp]])
                    for hp in range(16):
                        S.op("dve", lambda e: e.max(out=tv[:, hp, 8:16], in_=wrk[:, hp, :]), reads=[d_wrkr[hp]], writes=[d_tvr[hp]])
                    for hp in range(16):
                        S.op("dve", lambda e: e.max_index(out=tiu[:, hp, 8:16], in_max=tv[:, hp, 8:16], in_values=wrk[:, hp, :]),
                             reads=[d_wrkr[hp], d_tvr[hp]], writes=[d_tiur[hp]])
                    S.op("dve", lambda e: e.tensor_copy(out=tif[:], in_=tiu[:]), reads=d_tiur, writes=[d_tif])
                    tvv = tv[:].rearrange("p (h two) k -> p h two k", two=2)
                    S.op("dve", lambda e: e.tensor_tensor(out=cand[:].rearrange("p h (a b) -> p h a b", b=16),
                                                          in0=tvv[:, :, 0, :].unsqueeze(3).to_broadcast([128, 8, 16, 16]),
                                                          in1=tvv[:, :, 1, :].unsqueeze(2).to_broadcast([128, 8, 16, 16]), op=ALU.add),
                         reads=d_tvr, writes=[d_cand])
                    for h in range(8):
                        S.op("dve", lambda e: e.max(out=best[:, h, 0:8], in_=cand[:, h, :]), reads=[d_cand], writes=[d_bestr[h]])
                    for h in range(8):
                        S.op("dve", lambda e: e.max_index(out=posu[:, h, 0:8], in_max=best[:, h, 0:8], in_values=cand[:, h, :]),
                             reads=[d_cand, d_bestr[h]], writes=[d_posur[h]])
                    for h in range(8):
                        S.op("dve", lambda e: e.match_replace(out=cwk[:, h, :], in_to_replace=best[:, h, 0:8], in_values=cand[:, h, :],
                                                              imm_value=-1e30), reads=[d_cand, d_bestr[h]], writes=[d_cwkr[h]])
                    for h in range(8):
                        S.op("dve", lambda e: e.max(out=best[:, h, 8:16], in_=cwk[:, h, :]), reads=[d_cwkr[h]], writes=[d_bestr[h]])
                    for h in range(8):
                        S.op("dve", lambda e: e.max_index(out=posu[:, h, 8:16], in_max=best[:, h, 8:16], in_values=cwk[:, h, :]),
                             reads=[d_cwkr[h], d_bestr[h]], writes=[d_posur[h]])
                    d_posu_l = d_posur
                    d_best_l = d_bestr
                    pflat = posu[:].rearrange("p h k -> p (h k)")
                    S.op("dve", lambda e: e.tensor_single_scalar(out=abu[:, 0, :], in_=pflat, scalar=4, op=ALU.logical_shift_right),
                         reads=d_posu_l, writes=[d_abu])
                    S.op("dve", lambda e: e.tensor_single_scalar(out=abu[:, 1, :], in_=pflat, scalar=15, op=ALU.bitwise_and),
                         reads=d_posu_l, writes=[d_abu])
                    S.op("dve", lambda e: e.tensor_copy(out=abf[:], in_=abu[:]), reads=[d_abu], writes=[d_abf])
                    tfv = tif[:].rearrange("p (h two) k -> p h two k", two=2)
                    for w in range(2):
                        S.op("dve",
                             lambda e: e.tensor_tensor(out=oh[w][:], in0=iota16[:].unsqueeze(1).to_broadcast([128, 128, 16]),
                                                       in1=abf[:, w, :].unsqueeze(2).to_broadcast([128, 128, 16]), op=ALU.is_equal),
                             reads=[d_abf, d_WC], writes=[d_oh[w]])
                    for w in range(2):
                        ohv = oh[w][:].rearrange("p (h j) a -> p h j a", j=16)
                        S.op("dve",
                             lambda e: e.tensor_tensor(out=ohv, in0=ohv,
                                                       in1=tfv[:, :, w, :].unsqueeze(2).to_broadcast([128, 8, 16, 16]), op=ALU.mult),
                             reads=[d_oh[w], d_tif], writes=[d_oh[w]])
                    S.op("dve", lambda e: e.tensor_tensor(out=gat, in0=best[:], in1=best[:, :, 0:1].to_broadcast([128, 8, 16]),
                                                          op=ALU.subtract), reads=d_best_l, writes=[d_sel])
                    S.op("act", lambda e: e.activation(out=gat, in_=gat, func=AF.Exp), reads=[d_sel], writes=[d_sel])
                    S.op("dve", lambda e: e.tensor_reduce(out=gsm[:], in_=gat, axis=AX.X, op=ALU.add), reads=[d_sel], writes=[d_gsm])
                    S.op("dve", lambda e: e.reciprocal(out=gsm[:], in_=gsm[:]), reads=[d_gsm], writes=[d_gsm])
                    S.op("dve", lambda e: e.tensor_tensor(out=gat, in0=gat, in1=gsm[:].unsqueeze(2).to_broadcast([128, 8, 16]),
                                                          op=ALU.mult), reads=[d_sel, d_gsm], writes=[d_sel])
                    for w in range(2):
                        S.op("dve", lambda e: e.tensor_reduce(out=sel[:, w, :], in_=oh[w][:], axis=AX.X, op=ALU.add),
                             reads=[d_oh[w]], writes=[d_sel])
                    for k in range(3):
                        S.op("pe", lambda e: e.transpose(out=PS[:, 7, k * 128:(k + 1) * 128], in_=sel[:, k, :], identity=identf[:]),
                             reads=[d_sel, d_const], writes=[dP[7]])
                    S.op("act", lambda e: e.activation(out=slotT[:, :, tok0:tok0 + T], in_=PS[:, 7, 0:384].rearrange("p (k t) -> p k t", k=3),
                                                       func=AF.Copy), reads=[dP[7]], writes=[d_slotT[ti]])

                c1a_head(0)
                for ti in range(NT):
                    if ti + 1 < NT:
                        c1a_head(ti + 1)
                    c1a_body(ti)
                S.barrier()

            swq.close()
            with ExitStack() as sd_:
                iotaC = SB(sd_, "iotaC", [128, 128], BF16)
                d_io = Dep()
                S.op("pool", lambda e: e.iota(iotaC[:], pattern=[[1, 128]], base=0, channel_multiplier=0,
                                              allow_small_or_imprecise_dtypes=True), writes=[d_io])
                HT = T // 2
                OH1 = [SB(sd_, "OH1_%d" % i, [128, 128, HT], BF16) for i in range(2)]; d_OH1 = [Dep(), Dep()]
                OH2 = [SB(sd_, "OH2_%d" % i, [128, 128, HT], BF16) for i in range(2)]; d_OH2 = [Dep(), Dep()]
                iotaR = SB(sd_, "iotaR", [128, 128, HT], BF16)
                S.op("dve", lambda e: e.tensor_copy(out=iotaR[:], in_=iotaC[:].unsqueeze(2).to_broadcast([128, 128, HT])),
                     reads=[d_io], writes=[d_io])
                Gt = [SB(sd_, "Gt%d" % i, [128, 128, T], BF16) for i in range(2)]
                d_Gt = [Dep(), Dep()]
                ecnt = [0]

                def c1b_build(hi):
                    ti, hf = hi // 2, hi % 2
                    hb_ = hi % 2
                    th0 = ti * T + hf * HT
                    S.op("dve", lambda e: e.tensor_tensor(out=OH1[hb_][:], in0=iotaR[:],
                                                          in1=slotT[:, 0, th0:th0 + HT].unsqueeze(1).to_broadcast([128, 128, HT]),
                                                          op=ALU.is_equal), reads=[d_io, d_slotT[ti]], writes=[d_OH1[hb_]])
                    S.op("dve", lambda e: e.tensor_tensor(out=OH2[hb_][:], in0=iotaR[:],
                                                          in1=slotT[:, 1, th0:th0 + HT].unsqueeze(1).to_broadcast([128, 128, HT]),
                                                          op=ALU.is_equal), reads=[d_io, d_slotT[ti]], writes=[d_OH2[hb_]])
                    S.op("dve", lambda e: e.tensor_tensor(out=OH2[hb_][:], in0=OH2[hb_][:],
                                                          in1=slotT[:, 2, th0:th0 + HT].unsqueeze(1).to_broadcast([128, 128, HT]),
                                                          op=ALU.mult), reads=[d_OH2[hb_], d_slotT[ti]], writes=[d_OH2[hb_]])

                def c1b_mm(hi):
                    ti, hf = hi // 2, hi % 2
                    par = ti % 2
                    hb_ = hi % 2
                    tok0 = ti * T
                    for t4 in range(HT // 4):
                        b = ecnt[0] % 4
                        ecnt[0] += 1
                        for tt in range(4):
                            t = t4 * 4 + tt
                            if PE_STRIDED:
                                o_ap = PS[:, b, :].rearrange("c (p t) -> c t p", t=4)[:, tt, :]
                            else:
                                o_ap = PS[:, b, tt * 128:(tt + 1) * 128]
                            S.op("pe", lambda e: e.matmul(o_ap, lhsT=OH1[hb_][:, :, t], rhs=OH2[hb_][:, :, t],
                                                          start=True, stop=True), reads=[d_OH1[hb_], d_OH2[hb_]], writes=[dP[b]])
                        tg = hf * HT + t4 * 4
                        if PE_STRIDED:
                            dst = Gt[par][:, :, tg:tg + 4]
                            src = PS[:, b, :].rearrange("c (p t) -> c p t", t=4)
                        else:
                            dst = Gt[par][:, :, tg:tg + 4].rearrange("c p t -> c t p")
                            src = PS[:, b, :].rearrange("c (t p) -> c t p", t=4)
                        S.op("act", lambda e: e.activation(out=dst, in_=src, func=AF.Copy), reads=[dP[b]], writes=[d_Gt[par]])
                    if hf == 1:
                        for q in range(4):
                            S.dma("sp" if q % 2 == 0 else "pool",
                                  lambda e: e.dma_start(out=G_scr[q * 32:(q + 1) * 32, :, tok0:tok0 + T].rearrange("p c t -> c p t"),
                                                        in_=Gt[par][:, q * 32:(q + 1) * 32, :]),
                                  reads=[d_Gt[par]])

                c1b_build(0)
                for hi in range(2 * NT):
                    if hi + 1 < 2 * NT:
                        c1b_build(hi + 1)
                    c1b_mm(hi)
                S.barrier()

            with ExitStack() as se_:
                NGRP = 128 // PG
                out_acc = SB(se_, "out_acc", [128, NT, D], F32); d_acc = [Dep() for _ in range(NT)]
                su = [SB(se_, "su%d" % i, [128, D], F32) for i in range(2)] * 2; d_su = [Dep(), Dep()] * 2
                sv = [SB(se_, "sv%d" % i, [128, D], F32) for i in range(2)]; d_sv = [Dep(), Dep()]
                ub = [SB(se_, "ub%d" % i, [128, D], BF16) for i in range(2)] * 2; d_ub = [Dep(), Dep()] * 2
                UT = [SB(se_, "UT%d" % i, [128, PG, NCH, 128], BF16) for i in range(2)]; d_UT = [Dep(), Dep()]
                Vb = [SB(se_, "Vb%d" % i, [128, PG, D], BF16) for i in range(2)]; d_Vb = [Dep(), Dep()]
                Gp = [slotT[:, i, :] for i in range(3)]; d_Gp = [Dep() for _ in range(3)]
                Ab = [SB(se_, "Ab%d" % i, [128, 512], BF16) for i in range(2)]; d_Ab = [Dep(), Dep()]
                A2 = [SB(se_, "A_%d" % i, [128, PG, NTOK], BF16) for i in range(2)]
                d_A2 = [[Dep() for _ in range(PG)] for _ in range(2)]
                eu_v = eu_d.rearrange("(c p) d -> p c d", p=128)
                ev_v = ev_d.rearrange("(c p) d -> p c d", p=128)
                chunks = [(t0, min(512, NTOK - t0)) for t0 in range(0, NTOK, 512)]
                pcount = [0]

                def run_gens2(gens):
                    gens = list(gens)
                    while gens:
                        for g_ in list(gens):
                            try:
                                next(g_)
                            except StopIteration:
                                gens.remove(g_)

                def prepU(g):
                    gb = g % 2
                    for half_ in range(PG // 2):
                        for k in range(2):
                            pi = half_ * 2 + k
                            p = g * PG + pi
                            S.dma("sp", lambda e: e.dma_start(out=su[k][:], in_=eu_v[p]), writes=[d_su[k]])
                            S.op("act", lambda e: e.activation(out=ub[k][:], in_=su[k][:], func=AF.Copy), reads=[d_su[k]], writes=[d_ub[k]])
                        yield
                        yield
                        yield
                        for k in range(2):
                            pi = half_ * 2 + k
                            for dc in range(NCH):
                                S.op("pe", lambda e: e.transpose(out=pbank16(0)[:, dc * 128:(dc + 1) * 128], in_=ub[k][:, dc * 128:(dc + 1) * 128],
                                                                 identity=identb[:]), reads=[d_ub[k], d_const], writes=[dP[0]])
                            S.op("dve", lambda e: e.tensor_copy(out=UT[gb][:, pi, :, :].rearrange("p c x -> p (c x)"), in_=pbank16(0)),
                                 reads=[dP[0]], writes=[d_UT[gb]])
                            yield

                def prepV(g):
                    gb = g % 2
                    for pi in range(PG):
                        p = g * PG + pi
                        k = vcount[0] % 2
                        vcount[0] += 1
                        S.dma("act", lambda e: e.dma_start(out=sv[k][:], in_=ev_v[p]), writes=[d_sv[k]])
                        S.op("act", lambda e: e.activation(out=Vb[gb][:, pi, :], in_=sv[k][:], func=AF.Copy), reads=[d_sv[k]], writes=[d_Vb[gb]])

                gcnt = [0]
                hcnt = [0]

                def H_gen(g):
                    gb = g % 2
                    A_ = A2[gb]
                    for pi in range(PG):
                        p = g * PG + pi
                        gk = gcnt[0] % 3
                        gcnt[0] += 1
                        S.dma("pool", lambda e: e.dma_start(out=Gp[gk], in_=G_scr[p]), writes=[d_Gp[gk]])
                        for (t0, n) in chunks:
                            hb = 1 + hcnt[0] % 2
                            ak = hcnt[0] % 2
                            hcnt[0] += 1
                            for dc in range(NCH):
                                S.op("pe", lambda e: e.matmul(PS[:, hb, 0:n], lhsT=UT[gb][:, pi, dc, :], rhs=xn2T_all[:, dc, t0:t0 + n],
                                                              start=(dc == 0), stop=(dc == NCH - 1)),
                                     reads=[d_UT[gb]], writes=[dP[hb]])
                            S.op("act", lambda e: e.activation(out=Ab[ak][:, 0:n], in_=PS[:, hb, 0:n], func=AF.Gelu_apprx_tanh),
                                 reads=[dP[hb]], writes=[d_Ab[ak]])
                            S.op("dve", lambda e: e.tensor_tensor(out=A_[:, pi, t0:t0 + n], in0=Ab[ak][:, 0:n], in1=Gp[gk][:, t0:t0 + n],
                                                                  op=ALU.mult), reads=[d_Ab[ak], d_Gp[gk]], writes=[d_A2[gb][pi]])
                            yield

                def V_gen(g):
                    gb = g % 2
                    A_ = A2[gb]
                    for s_ in range(NT):
                        ab_ = 3 + 2 * (s_ % 2)
                        for j in range(2):
                            for pi in range(PG):
                                S.op("pe", lambda e: e.matmul(PS[:, ab_ + j, :], lhsT=A_[:, pi, s_ * T:(s_ + 1) * T],
                                                              rhs=Vb[gb][:, pi, j * 512:(j + 1) * 512],
                                                              start=(pi == 0), stop=(pi == PG - 1)),
                                     reads=[d_A2[gb][pi], d_Vb[gb]], writes=[dP[ab_ + j]])
                        S.op("dve", lambda e: e.tensor_tensor(out=out_acc[:, s_, :].rearrange("p (b x) -> p b x", b=2),
                                                              in0=PS[:, ab_:ab_ + 2, :],
                                                              in1=out_acc[:, s_, :].rearrange("p (b x) -> p b x", b=2), op=ALU.add),
                             reads=[dP[ab_], dP[ab_ + 1], d_acc[s_]], writes=[d_acc[s_]])
                        yield

                vcount = [0]
                run_gens2([prepU(0)])
                prepV(0)
                run_gens2([prepU(1)])
                prepV(1)
                for ti in range(NT):
                    S.dma("sp" if ti % 2 == 0 else "act", lambda e: e.dma_start(out=out_acc[:, ti, :], in_=hres_scr[ti]),
                          reads=[d_hres[ti]], writes=[d_acc[ti]])
                run_gens2([H_gen(0)])
                for g in range(NGRP):
                    gens = [V_gen(g)]
                    if g + 1 < NGRP:
                        gens.append(H_gen(g + 1))
                    if g + 2 < NGRP:
                        gens.append(prepU(g + 2))
                    run_gens2(gens)
                    if g + 2 < NGRP:
                        prepV(g + 2)
                for ti in range(NT):
                    dst = y_p[ti * T:(ti + 1) * T, :] if ti < NMAIN else y_s
                    S.dma("sp" if ti % 2 == 0 else "act", lambda e: e.dma_start(out=dst, in_=out_acc[:, ti, :]),
                          reads=[d_acc[ti]], is_out=True)
                S.barrier()

        if STAGE < 2:
            with ExitStack() as sz:
                z = SB(sz, "z", [128, D], F32)
                dz = Dep()
                S.op("pool", lambda e: e.memset(z[:], 0.0), writes=[dz])
                for mi in range(NMAIN):
                    S.dma("sp", lambda e: e.dma_start(out=y_p[mi * T:(mi + 1) * T, :], in_=z[:]), reads=[dz], is_out=True)
                S.dma("sp", lambda e: e.dma_start(out=y_s, in_=z[:]), reads=[dz], is_out=True)
                S.barrier()

        S.finish()
    print("instructions:", S.ninst, {k: S.cnt[k] for k in S.cnt}, S.dcnt)
    return nc


_CACHE = {}


def _tables():
    if "t" in _CACHE:
        return _CACHE["t"]
    slopes = 2.0 ** (-8.0 * np.arange(1, NH + 1, dtype=np.float64) / NH)
    ki = np.arange(128)[:, None]
    qi = np.arange(128)[None, :]
    E_pr = np.zeros((128, NH, 2, 128), np.float64)
    for h in range(NH):
        d_prev = qi - ki + 128
        E_pr[:, h, 0, :] = np.where(ki >= qi, np.exp(-slopes[h] * d_prev), 0.0)
        d_own = qi - ki
        E_pr[:, h, 1, :] = np.where(ki <= qi, np.exp(-slopes[h] * d_own), 0.0)
    ks_, kt_ = ki // DEC, ki % DEC
    qs_, qt_ = qi // DEC, qi % DEC
    E_so = np.zeros((128, NH, 128), np.float64)
    for h in range(NH):
        E_so[:, h, :] = np.where((ks_ == qs_) & (kt_ <= qt_), np.exp(-slopes[h] * (qt_ - kt_)), 0.0)
    E_sc = np.zeros((128, NH, DEC), np.float64)
    j = np.arange(128)[:, None]
    t = np.arange(DEC)[None, :]
    for h in range(NH):
        E_sc[:, h, :] = np.where(j >= t, np.exp(-slopes[h] * (t + 128 - j)), 0.0)
    tb = dict(E_pr=E_pr.reshape(128, -1).astype(np.float32), E_so=E_so.reshape(128, -1).astype(np.float32),
              E_sc=E_sc.reshape(128, -1).astype(np.float32), identf=np.eye(128, dtype=np.float32))
    _CACHE["t"] = tb
    return tb


def kernel(x_prompt, x_sample, cache_conv, state_lru, cache_k, cache_v, norm1_g, w_in, conv_w, conv_b,
           rg_w_a, rg_b_a, rg_w_x, rg_b_x, rg_lambda, q_norm_g, k_norm_g, attn_sinks, w_branch_lru,
           w_branch_attn, w_out, norm2_g, peer_w_query, peer_sub_keys, expert_u, expert_v):
    f = lambda a: np.ascontiguousarray(np.asarray(a, dtype=np.float32))
    x_prompt, x_sample = f(x_prompt), f(x_sample)
    tb = _tables()
    prow = np.zeros((16, D), np.float32)
    prow[0:4] = f(conv_w)[0]
    prow[4] = f(conv_b)[0]
    prow[5] = f(rg_b_a)[0]
    prow[6] = f(rg_b_x)[0]
    prow[7] = f(rg_lambda)[0]
    prow[8] = f(norm1_g)[0]
    prow[9] = f(norm2_g)[0]
    shared = dict(
        identf=tb["identf"], E_pr=tb["E_pr"], E_so=tb["E_so"], E_sc=tb["E_sc"], prow=prow,
        w_in=f(w_in)[0], rgw=np.stack([f(rg_w_a)[0], f(rg_w_x)[0]]),
        qkg=np.stack([f(q_norm_g)[0], f(k_norm_g)[0]]), sinks=f(attn_sinks)[0].reshape(1, NH),
        w_bl=f(w_branch_lru)[0], w_ba=f(w_branch_attn)[0], w_o=f(w_out)[0], w_q=f(peer_w_query)[0],
        subk=f(peer_sub_keys)[0].reshape(16, 128, 128),
    )
    if STAGE >= 3 and not DEBUG_C:
        shared.update(eu=f(expert_u)[0], ev=f(expert_v)[0])
    cc, sl, ckk, cvv = f(cache_conv)[0], f(state_lru)[0], f(cache_k)[0], f(cache_v)[0]
    in_maps = []
    for c in range(NCORE):
        s, half = c // 2, c % 2
        xpc = np.zeros(((NPRE + NMAIN) * T, D), np.float32)
        if half == 1:
            xpc[:NPRE * T] = x_prompt[s, :2048]
        xpc[NPRE * T:] = x_prompt[s, half * 2048:(half + 1) * 2048]
        m = dict(shared)
        m.update(
            xp=xpc, xs=x_sample[c * NSEQ:(c + 1) * NSEQ].reshape(T, D),
            cconv=cc[c * NSEQ:(c + 1) * NSEQ].reshape(NSEQ * 3, D), slru=sl[c * NSEQ:(c + 1) * NSEQ],
            ck=ckk[c * NSEQ:(c + 1) * NSEQ].reshape(NSEQ, 128, 256), cv=cvv[c * NSEQ:(c + 1) * NSEQ].reshape(NSEQ, 128, 256),
            flag=np.full((128, 1), float(half), np.float32),
        )
        in_maps.append(m)
    if "nc" not in _CACHE:
        _CACHE["nc"] = build_program()
    res = run_bass_kernel_spmd(_CACHE["nc"], in_maps, core_ids=list(range(NCORE)))
    R = res.results
    y_prompt = np.zeros((4, 4096, D), np.float32)
    y_sample = np.zeros((128, DEC, D), np.float32)
    conv_pr = np.zeros((1, 4, 3, D), np.float32)
    lru_pr = np.zeros((1, 4, D), np.float32)
    k_pr = np.zeros((1, 4, 128, NKV, HD), np.float32)
    v_pr = np.zeros((1, 4, 128, NKV, HD), np.float32)
    conv_sa = np.zeros((1, 128, 3, D), np.float32)
    lru_sa = np.zeros((1, 128, D), np.float32)
    k_sa = np.zeros((1, 128, 128, NKV, HD), np.float32)
    v_sa = np.zeros((1, 128, 128, NKV, HD), np.float32)
    for c in range(NCORE):
        s, half = c // 2, c % 2
        r = R[c]
        y_prompt[s, half * 2048:(half + 1) * 2048] = r["y_p"]
        y_sample[c * NSEQ:(c + 1) * NSEQ] = r["y_s"].reshape(NSEQ, DEC, D)
        if half == 1:
            conv_pr[0, s] = r["conv_p"]
            lru_pr[0, s] = r["lru_p"][0]
            k_pr[0, s] = r["k_p"].reshape(128, NKV, HD)
            v_pr[0, s] = r["v_p"].reshape(128, NKV, HD)
        conv_sa[0, c * NSEQ:(c + 1) * NSEQ] = r["conv_s"].reshape(NSEQ, 3, D)
        lru_sa[0, c * NSEQ:(c + 1) * NSEQ] = r["lru_s"]
        k_sa[0, c * NSEQ:(c + 1) * NSEQ] = r["k_s"].reshape(NSEQ, 128, NKV, HD)
        v_sa[0, c * NSEQ:(c + 1) * NSEQ] = r["v_s"].reshape(NSEQ, 128, NKV, HD)
    return (y_prompt, y_sample, conv_pr, lru_pr, k_pr, v_pr, conv_sa, lru_sa, k_sa, v_sa)
```
